# Optimizing a Trainium2 kernel written in Bass

```python
import jax, jax.numpy as jnp
from jax import lax
import numpy as np

D_MODEL = 2048
BATCH = 4
SEQ = 4096
DEPTH = 2

GRID_W = 64
CTX_LEN = 256
MLA_HEADS = 8
MLA_Q_RANK = 512
MLA_KV_RANK = 512
MLA_NOPE = 128
MLA_ROPE = 64
MLA_V = 128
SWA_HEADS = 8
SWA_KV_HEADS = 2
SWA_HEAD_DIM = 64
SWA_WINDOW = 128
SWA_BLOCK = 128
CONV_CH = 512
CONV_K = 31
D_FF = 4 * D_MODEL

Q_BLOCK = 128
ROPE_THETA = 10000.0
EPS = 1e-6
NEG_INF = -1e30
MLA_SCALE = (MLA_NOPE + MLA_ROPE) ** -0.5
SWA_SCALE = SWA_HEAD_DIM ** -0.5
SWA_GROUP = SWA_HEADS // SWA_KV_HEADS

MLA_OUT = MLA_HEADS * MLA_V
SWA_OUT = SWA_HEADS * SWA_HEAD_DIM
MIX_WIDTH = MLA_OUT + SWA_OUT + CONV_CH
IN_SIZES = (MLA_Q_RANK, MLA_KV_RANK, MLA_ROPE,
            SWA_HEADS * SWA_HEAD_DIM, SWA_KV_HEADS * SWA_HEAD_DIM, SWA_KV_HEADS * SWA_HEAD_DIM,
            2 * CONV_CH)
IN_WIDTH = sum(IN_SIZES)
IN_OFFSETS = tuple(sum(IN_SIZES[:i + 1]) for i in range(len(IN_SIZES) - 1))

kernel_name = "hybrid_parallel_groups_dit_block"


def rms_norm(x, g):
    xf = x.astype(jnp.float32)
    y = xf * lax.rsqrt(jnp.mean(xf * xf, axis=-1, keepdims=True) + EPS)
    return (y * g.astype(jnp.float32)).astype(x.dtype)


def layer_norm(x, g, b):
    xf = x.astype(jnp.float32)
    mu = jnp.mean(xf, axis=-1, keepdims=True)
    xc = xf - mu
    y = xc * lax.rsqrt(jnp.mean(xc * xc, axis=-1, keepdims=True) + EPS)
    return (y * g.astype(jnp.float32) + b.astype(jnp.float32)).astype(x.dtype)


def modulate(x, g, shift, scale):
    return rms_norm(x, g) * (1 + scale) + shift


def axial_rope(rows, rot_dim):
    row = jnp.repeat(jnp.arange(rows, dtype=jnp.float32), GRID_W)
    col = jnp.tile(jnp.arange(GRID_W, dtype=jnp.float32), rows)
    n_freq = rot_dim // 4
    inv_freq = ROPE_THETA ** (-jnp.arange(n_freq, dtype=jnp.float32) / n_freq)
    ang = jnp.stack([row[:, None] * inv_freq, col[:, None] * inv_freq], axis=1)
    return jnp.cos(ang), jnp.sin(ang)


def rope_2d(x, cos, sin):
    b_, s_, h_, r_ = x.shape
    xr = x.reshape(b_, s_, h_, 2, 2, r_ // 4)
    x1, x2 = xr[..., 0, :], xr[..., 1, :]
    c = cos[None, :, None].astype(x.dtype)
    s = sin[None, :, None].astype(x.dtype)
    y = jnp.stack([x1 * c - x2 * s, x2 * c + x1 * s], axis=-2)
    return y.reshape(x.shape)


def blocked_attention(q, k, v, scale):
    b_, s_, h_, d_ = q.shape
    nb = s_ // Q_BLOCK
    qb = q.reshape(b_, nb, Q_BLOCK, h_, d_).transpose(1, 0, 2, 3, 4)

    def one_block(q_blk):
        s = jnp.einsum('bqhd,bkhd->bhqk', q_blk, k).astype(jnp.float32) * scale
        p = jax.nn.softmax(s, axis=-1).astype(v.dtype)
        return jnp.einsum('bhqk,bkhd->bqhd', p, v)

    out = lax.map(one_block, qb)
    return out.transpose(1, 0, 2, 3, 4).reshape(b_, s_, h_, v.shape[-1])


def mla_q(a_q, q_norm, w_uq, rope):
    q = jnp.einsum('bsr,rhd->bshd', rms_norm(a_q, q_norm), w_uq)
    q_nope, q_pe = q[..., :MLA_NOPE], q[..., MLA_NOPE:]
    if rope is not None:
        q_pe = rope_2d(q_pe, *rope)
    return jnp.concatenate([q_nope, q_pe], axis=-1)


def mla_kv(a_kv, a_kr, kv_norm, w_ukv, rope):
    kv = jnp.einsum('bsr,rhd->bshd', rms_norm(a_kv, kv_norm), w_ukv)
    k_nope, v = kv[..., :MLA_NOPE], kv[..., MLA_NOPE:]
    k_pe = a_kr[:, :, None, :]
    if rope is not None:
        k_pe = rope_2d(k_pe, *rope)
    k_pe = jnp.broadcast_to(k_pe, k_nope.shape[:-1] + (MLA_ROPE,))
    return jnp.concatenate([k_nope, k_pe], axis=-1), v


def swa_latent(q, k, v, kc, vc, sink):
    b_, s_, h_, dh = q.shape
    nb = s_ // SWA_BLOCK
    win = 3 * SWA_BLOCK
    pad = ((0, 0), (SWA_BLOCK, SWA_BLOCK), (0, 0), (0, 0))
    kp = jnp.pad(k, pad).reshape(b_, nb + 2, SWA_BLOCK, SWA_KV_HEADS, dh)
    vp = jnp.pad(v, pad).reshape(b_, nb + 2, SWA_BLOCK, SWA_KV_HEADS, dh)
    kw = jnp.concatenate([kp[:, :-2], kp[:, 1:-1], kp[:, 2:]], axis=2)
    vw = jnp.concatenate([vp[:, :-2], vp[:, 1:-1], vp[:, 2:]], axis=2)
    qb = q.reshape(b_, nb, SWA_BLOCK, SWA_KV_HEADS, SWA_GROUP, dh)
    s_loc = jnp.einsum('bnqkgd,bnwkd->bnkgqw', qb, kw).astype(jnp.float32) * SWA_SCALE
    s_ctx = jnp.einsum('bnqkgd,bckd->bnkgqc', qb, kc).astype(jnp.float32) * SWA_SCALE
    qi = jnp.arange(SWA_BLOCK)[:, None]
    wi = jnp.arange(win)[None, :]
    kpos = jnp.arange(nb)[:, None, None] * SWA_BLOCK - SWA_BLOCK + wi
    valid = (jnp.abs(wi - SWA_BLOCK - qi) <= SWA_WINDOW)[None] & (kpos >= 0) & (kpos < s_)
    s_loc = jnp.where(valid[None, :, None, None], s_loc, NEG_INF)
    s_sink = jnp.broadcast_to(
        sink.reshape(SWA_KV_HEADS, SWA_GROUP)[None, None, :, :, None, None].astype(jnp.float32),
        s_loc.shape[:-1] + (1,))
    p = jax.nn.softmax(jnp.concatenate([s_loc, s_ctx, s_sink], axis=-1), axis=-1).astype(v.dtype)
    n_ctx = kc.shape[1]
    out = (jnp.einsum('bnkgqw,bnwkd->bnqkgd', p[..., :win], vw)
           + jnp.einsum('bnkgqc,bckd->bnqkgd', p[..., win:win + n_ctx], vc))
    return out.reshape(b_, s_, h_, dh)


def swa_context(qc, kc, vc, sink):
    b_, l_, h_, dh = qc.shape
    qg = qc.reshape(b_, l_, SWA_KV_HEADS, SWA_GROUP, dh)
    s = jnp.einsum('bqkgd,bckd->bkgqc', qg, kc).astype(jnp.float32) * SWA_SCALE
    s_sink = jnp.broadcast_to(
        sink.reshape(SWA_KV_HEADS, SWA_GROUP)[None, :, :, None, None].astype(jnp.float32),
        s.shape[:-1] + (1,))
    p = jax.nn.softmax(jnp.concatenate([s, s_sink], axis=-1), axis=-1).astype(vc.dtype)
    out = jnp.einsum('bkgqc,bckd->bqkgd', p[..., :l_], vc)
    return out.reshape(b_, l_, h_, dh)


def conformer_conv(u, conv_w, conv_b, ln_g, ln_b):
    a, g = jnp.split(u, 2, axis=-1)
    h = a * jax.nn.sigmoid(g)
    h = lax.conv_general_dilated(
        h, conv_w, window_strides=(1,), padding=((CONV_K // 2, CONV_K // 2),),
        dimension_numbers=('NWC', 'WIO', 'NWC'), feature_group_count=CONV_CH) + conv_b
    return jax.nn.silu(layer_norm(h, ln_g, ln_b))


def merge_groups(out_a, out_b, out_c, out_norm, w_out):
    b_, s_ = out_c.shape[:2]
    y = jnp.concatenate([
        rms_norm(out_a.reshape(b_, s_, MLA_OUT), out_norm[:MLA_OUT]),
        rms_norm(out_b.reshape(b_, s_, SWA_OUT), out_norm[MLA_OUT:MLA_OUT + SWA_OUT]),
        rms_norm(out_c, out_norm[MLA_OUT + SWA_OUT:]),
    ], axis=-1)
    return y @ w_out


def token_mixer(h, hc, w_in, q_norm, w_uq, kv_norm, w_ukv, sink, conv_w, conv_b,
                ln_g, ln_b, out_norm, w_out, rope_a, rope_b, with_ctx_out):
    b_, s_, _ = h.shape
    l_ = hc.shape[1]
    a_q, a_kv, a_kr, b_q, b_k, b_v, c_in = jnp.split(h @ w_in, IN_OFFSETS, axis=-1)
    ac_q, ac_kv, ac_kr, bc_q, bc_k, bc_v, cc_in = jnp.split(hc @ w_in, IN_OFFSETS, axis=-1)

    k_a, v_a = mla_kv(a_kv, a_kr, kv_norm, w_ukv, rope_a)
    kc_a, vc_a = mla_kv(ac_kv, ac_kr, kv_norm, w_ukv, None)
    q_a = mla_q(a_q, q_norm, w_uq, rope_a)
    out_a = blocked_attention(q_a, jnp.concatenate([k_a, kc_a], axis=1),
                              jnp.concatenate([v_a, vc_a], axis=1), MLA_SCALE)

    q_b = rope_2d(b_q.reshape(b_, s_, SWA_HEADS, SWA_HEAD_DIM), *rope_b)
    k_b = rope_2d(b_k.reshape(b_, s_, SWA_KV_HEADS, SWA_HEAD_DIM), *rope_b)
    v_b = b_v.reshape(b_, s_, SWA_KV_HEADS, SWA_HEAD_DIM)
    kc_b = bc_k.reshape(b_, l_, SWA_KV_HEADS, SWA_HEAD_DIM)
    vc_b = bc_v.reshape(b_, l_, SWA_KV_HEADS, SWA_HEAD_DIM)
    out_b = swa_latent(q_b, k_b, v_b, kc_b, vc_b, sink)

    out_c = conformer_conv(c_in, conv_w, conv_b, ln_g, ln_b)

    y = merge_groups(out_a, out_b, out_c, out_norm, w_out)
    if not with_ctx_out:
        return y, None
    outc_a = blocked_attention(mla_q(ac_q, q_norm, w_uq, None), kc_a, vc_a, MLA_SCALE)
    outc_b = swa_context(bc_q.reshape(b_, l_, SWA_HEADS, SWA_HEAD_DIM), kc_b, vc_b, sink)
    outc_c = conformer_conv(cc_in, conv_w, conv_b, ln_g, ln_b)
    yc = merge_groups(outc_a, outc_b, outc_c, out_norm, w_out)
    return y, yc


def sq_relu_mlp(h, w1, w2):
    return jnp.square(jax.nn.relu(h @ w1)) @ w2


def setup_inputs(seed: int = 0) -> dict:
    key = jax.random.key(seed)
    ks = jax.random.split(key, 24)
    f32 = jnp.float32
    L, D = DEPTH, D_MODEL

    def dense(k, shape, fan_in, mult=1.0):
        return jax.random.normal(k, shape, f32) * (mult * fan_in ** -0.5)

    def gain(k, shape):
        return 1.0 + 0.05 * jax.random.normal(k, shape, f32)

    def small(k, shape, s=0.02):
        return s * jax.random.normal(k, shape, f32)

    return {
        "x": jax.random.normal(ks[0], (BATCH, SEQ, D), f32),
        "c": jax.random.normal(ks[1], (BATCH, D), f32),
        "ctx": jax.random.normal(ks[2], (BATCH, CTX_LEN, D), f32),
        "c_ctx": jax.random.normal(ks[3], (D,), f32),
        "ada_w": dense(ks[4], (L, D, 6 * D), D, 0.5),
        "ada_b": small(ks[5], (L, 6 * D)),
        "norm_mix": gain(ks[6], (L, D)),
        "norm_mlp": gain(ks[7], (L, D)),
        "w_in": dense(ks[8], (L, D, IN_WIDTH), D),
        "mla_q_norm": gain(ks[9], (L, MLA_Q_RANK)),
        "mla_w_uq": dense(ks[10], (L, MLA_Q_RANK, MLA_HEADS, MLA_NOPE + MLA_ROPE), MLA_Q_RANK),
        "mla_kv_norm": gain(ks[11], (L, MLA_KV_RANK)),
        "mla_w_ukv": dense(ks[12], (L, MLA_KV_RANK, MLA_HEADS, MLA_NOPE + MLA_V), MLA_KV_RANK),
        "swa_sink": small(ks[13], (L, SWA_HEADS), 0.5),
        "conv_w": dense(ks[14], (L, CONV_K, 1, CONV_CH), CONV_K),
        "conv_b": small(ks[15], (L, CONV_CH)),
        "conv_ln_g": gain(ks[16], (L, CONV_CH)),
        "conv_ln_b": small(ks[17], (L, CONV_CH)),
        "out_norm": gain(ks[18], (L, MIX_WIDTH)),
        "w_out": dense(ks[19], (L, MIX_WIDTH, D), MIX_WIDTH),
        "mlp_w1": dense(ks[20], (L, D, D_FF), D),
        "mlp_w2": dense(ks[21], (L, D_FF, D), D_FF),
        "final_norm": gain(ks[22], (D,)),
    }


def reference(x, c, ctx, c_ctx, ada_w, ada_b, norm_mix, norm_mlp, w_in, mla_q_norm,
              mla_w_uq, mla_kv_norm, mla_w_ukv, swa_sink, conv_w, conv_b, conv_ln_g,
              conv_ln_b, out_norm, w_out, mlp_w1, mlp_w2, final_norm):
    n_tok = x.shape[1]
    rows = n_tok // GRID_W
    rope_a = axial_rope(rows, MLA_ROPE)
    rope_b = axial_rope(rows, SWA_HEAD_DIM)
    silu_c = jax.nn.silu(c)
    silu_cc = jax.nn.silu(c_ctx)
    xc = ctx
    for l in range(DEPTH):
        update_ctx = l < DEPTH - 1
        mod = silu_c @ ada_w[l] + ada_b[l]
        mod_c = silu_cc @ ada_w[l] + ada_b[l]
        sh1, sc1, g1, sh2, sc2, g2 = jnp.split(mod[:, None, :], 6, axis=-1)
        sh1c, sc1c, g1c, sh2c, sc2c, g2c = jnp.split(mod_c, 6, axis=-1)
        h = modulate(x, norm_mix[l], sh1, sc1)
        hc = modulate(xc, norm_mix[l], sh1c, sc1c)
        y, yc = token_mixer(h, hc, w_in[l], mla_q_norm[l], mla_w_uq[l], mla_kv_norm[l],
                            mla_w_ukv[l], swa_sink[l], conv_w[l], conv_b[l], conv_ln_g[l],
                            conv_ln_b[l], out_norm[l], w_out[l], rope_a, rope_b, update_ctx)
        x = x + g1 * y
        x = x + g2 * sq_relu_mlp(modulate(x, norm_mlp[l], sh2, sc2), mlp_w1[l], mlp_w2[l])
        if update_ctx:
            xc = xc + g1c * yc
            xc = xc + g2c * sq_relu_mlp(modulate(xc, norm_mlp[l], sh2c, sc2c), mlp_w1[l], mlp_w2[l])
    return rms_norm(x, final_norm)
```

```python
import numpy as np
import ml_dtypes
from contextlib import ExitStack
import concourse.bass as bass
import concourse.mybir as mybir
from concourse.bass_utils import run_bass_kernel_spmd

F32 = mybir.dt.float32
BF16 = mybir.dt.bfloat16
AF = mybir.ActivationFunctionType
ALU = mybir.AluOpType

D = 2048
TL = 4096
CL = 256
T = TL + CL
DEPTH = 2
EPS = 1e-6
MLA_SCALE = 192 ** -0.5
SWA_SCALE = 64 ** -0.5
NEG = -30000.0
NCORES = 8
TH = 2048
GROUPS = [(i * 512, 512, 0) for i in range(8)] + [(TL, 256, 1)]

O_NMIX, O_NMLP, O_QN, O_KVN, O_CB, O_LNG, O_LNB, O_ON, O_ADAB, O_CW, O_SINK = (
    0, 16, 32, 36, 40, 44, 48, 52, 68, 164, 288)
NS = 296

COMPUTE = ("pe", "act", "dve")
DMAQ = {"sp": 8, "pool": 6}


class Op:
    __slots__ = ("q", "fn", "deps", "signal", "sem", "val", "slot", "key")

    def __init__(self, q, fn):
        self.q = q
        self.fn = fn
        self.deps = {}
        self.signal = False
        self.sem = None
        self.val = 0
        self.slot = None
        self.key = q


class Res:
    __slots__ = ("w", "r", "name")

    def __init__(self, sched, name=""):
        self.w = dict(sched.snapshot)
        self.r = {}
        self.name = name


class Sched:
    def __init__(self, nc):
        self.nc = nc
        self.streams = {q: [] for q in ("pe", "act", "dve", "pool", "sp")}
        self.latest = {}
        self.snapshot = {}
        self.slot_rr = {q: 0 for q in DMAQ}
        self.slot_last = {}

    def barrier(self):
        self.snapshot = dict(self.latest)

    def res(self, name=""):
        return Res(self, name)

    def _add(self, op, reads, writes):
        allk = (op.slot is not None) or (op.q != "pe")
        for r in reads:
            for k, d in r.w.items():
                if allk or k != op.key:
                    op.deps[id(d)] = d
        for r in writes:
            for k, d in r.w.items():
                if allk or k != op.key:
                    op.deps[id(d)] = d
            for k, d in r.r.items():
                if (allk and k != op.key) or (k != op.key):
                    op.deps[id(d)] = d
        for r in reads:
            r.r[op.key] = op
        for r in writes:
            r.w[op.key] = op
        self.latest[op.key] = op
        self.streams[op.q].append(op)
        return op

    def op(self, q, fn, reads=(), writes=()):
        return self._add(Op(q, fn), reads, writes)

    def dma(self, q, out, in_, reads=(), writes=()):
        op = Op(q, lambda e: e.dma_start(out=out, in_=in_))
        n = DMAQ[q]
        s = self.slot_rr[q]
        self.slot_rr[q] = (s + 1) % n
        op.slot = s
        op.key = (q, s)
        prev = self.slot_last.get((q, s))
        if prev is not None:
            op.deps[id(prev)] = prev
        self.slot_last[(q, s)] = op
        op.signal = True
        return self._add(op, reads, writes)

    def finalize(self, sems, block):
        for q, ops in self.streams.items():
            for op in ops:
                for d in op.deps.values():
                    d.signal = True
        cnt = {}
        for q, ops in self.streams.items():
            for op in ops:
                if not op.signal:
                    continue
                if op.slot is not None:
                    key = (q, op.slot)
                    cnt[key] = cnt.get(key, 0) + 16
                    op.sem = sems[key]
                    op.val = cnt[key]
                else:
                    cnt[q] = cnt.get(q, 0) + 1
                    op.sem = sems[q]
                    op.val = cnt[q]
        self.counts = cnt
        engs = {"pe": block.tensor, "act": block.scalar, "dve": block.vector,
                "pool": block.gpsimd, "sp": block.sync}
        for q, deco in engs.items():
            ops = self.streams[q]

            def body(e, ops=ops, q=q):
                waited = {}
                for op in ops:
                    need = {}
                    for d in op.deps.values():
                        sid = id(d.sem)
                        if waited.get(sid, 0) >= d.val:
                            continue
                        if sid not in need or need[sid][1] < d.val:
                            need[sid] = (d.sem, d.val)
                    for sid, (sem, val) in need.items():
                        e.wait_ge(sem, val)
                        waited[sid] = val
                    ins = op.fn(e)
                    if op.signal:
                        ins.then_inc(op.sem, 16 if op.slot is not None else 1)
                if q in DMAQ:
                    for s in range(DMAQ[q]):
                        last = self.slot_last.get((q, s))
                        if last is not None and waited.get(id(last.sem), 0) < last.val:
                            e.wait_ge(last.sem, last.val)

            deco(body)


class Arena:
    def __init__(self, ap2d, nwords):
        self.ap = ap2d
        self.n = nwords
        self.off = 0

    def reset(self):
        self.off = 0

    def f32(self, n):
        a = self.ap[:, self.off:self.off + n]
        self.off += n
        assert self.off <= self.n, ("arena overflow", self.off)
        return a

    def bf16(self, n):
        w = (n + 1) // 2
        a = self.ap[:, self.off:self.off + w].bitcast(BF16)
        self.off += w
        assert self.off <= self.n, ("arena overflow", self.off)
        return a


class Rot:
    def __init__(self, S, aps, name=""):
        self.items = [(a, S.res(name)) for a in aps]
        self.i = 0

    def next(self):
        it = self.items[self.i % len(self.items)]
        self.i += 1
        return it


def build_program(layers=(0, 1), first_in="xT", final=True, debug=()):
    nc = bass.Bass("TRN2", target_bir_lowering=False)
    es = ExitStack()
    L = DEPTH

    def din(name, shape, dt=F32):
        return nc.dram_tensor(name, list(shape), dt, kind="ExternalInput").ap()

    def dscr(name, shape, dt):
        kind = "ExternalOutput" if name in debug else None
        if kind:
            return nc.dram_tensor(name, list(shape), dt, kind=kind).ap()
        return nc.dram_tensor(name, list(shape), dt).ap()

    xT_in = din("xT", [16, 128, T])
    cT_d = din("cT", [128, 34])
    small_d = din("small", [L, 128, NS])
    fnorm_d = din("fnorm", [128, 16])
    rope_d = din("rope", [2, 128, T])
    cbf_d = din("cbf", [128, 3200], BF16)
    ada_d = din("ada_t", [L, 24, 128, 16, 512])
    win_d = din("win_t", [L, 29, 128, 16, 128])
    wk_d = din("wk_t", [L, 8, 128, 4, 128])
    wv_d = din("wv_t", [L, 2, 128, 4, 512])
    wqn_d = din("wqn_t", [L, 8, 128, 4, 128])
    wqp_d = din("wqp_t", [L, 8, 128, 4, 64])
    wqr_d = din("wqr_t", [L, 8, 128, 4, 64])
    wout_d = din("wout_t", [L, 16, 128, 16, 128])
    w1_d = din("w1_t", [L, 64, 128, 16, 128])
    w2_d = din("w2_t", [L, 2, 16, 128, 32, 128])
    outT = nc.dram_tensor("outT", [16, 128, TH], F32, kind="ExternalOutput").ap()

    cq_s = dscr("cq_s", [4, 128, T], BF16)
    ckv_s = dscr("ckv_s", [4, 128, T], BF16)
    kpe_s = dscr("kpe_s", [64, T], BF16)
    qb_s = dscr("qb_s", [4, 128, T], BF16)
    kb_s = dscr("kb_s", [128, T], BF16)
    vb_s = dscr("vb_s", [34, 128, 128], BF16)
    glu_s = dscr("glu_s", [4, 128, T], BF16)
    kn_s = dscr("kn_s", [8, 128, T], BF16)
    va_s = dscr("va_s", [34, 128, 1024], BF16)
    qn_s = dscr("qn_s", [8, 128, T], BF16)
    qpe_s = dscr("qpe_s", [8, 64, T], BF16)
    mix_s = dscr("mix_s", [16, 128, T], F32)
    xB = dscr("xB", [16, 128, T], F32)

    arena_t = es.enter_context(nc.sbuf_tensor("arena", [128, 44544], F32))
    ar = Arena(arena_t[:, :], 44544)
    ones_f = es.enter_context(nc.sbuf_tensor("ones_f", [128, 128], F32))
    ones_b = es.enter_context(nc.sbuf_tensor("ones_b", [128, 128], BF16))
    cbf = es.enter_context(nc.sbuf_tensor("cbf_sb", [128, 3200], BF16))
    small = es.enter_context(nc.sbuf_tensor("small_sb", [128, L, NS], F32))
    fnorm = es.enter_context(nc.sbuf_tensor("fnorm_sb", [128, 16], F32))
    cTs = es.enter_context(nc.sbuf_tensor("cT_sb", [128, 34], F32))
    scT = es.enter_context(nc.sbuf_tensor("scT_sb", [128, 32], F32))
    modT_all = es.enter_context(nc.sbuf_tensor("modT", [128, L * 96, 2], F32))
    dtab_all = es.enter_context(nc.sbuf_tensor("dtab", [128, L * 6, 16, 2], F32))
    sinkx_all = es.enter_context(nc.sbuf_tensor("sinkx", [128, L * 8], F32))
    sinkbc_all = es.enter_context(nc.sbuf_tensor("sinkbc", [64, L * 8, 128], F32))
    PS = [es.enter_context(nc.psum_tensor(f"ps{i}", [128, 512], F32)) for i in range(8)]

    S = Sched(nc)
    sems = {}
    for q in COMPUTE:
        sems[q] = es.enter_context(nc.semaphore(f"sem_{q}"))
    for q, n in DMAQ.items():
        for s in range(n):
            sems[(q, s)] = es.enter_context(nc.semaphore(f"sem_{q}{s}"))

    RPS = [S.res(f"ps{i}") for i in range(8)]
    r_const = S.res("const")
    r_small = S.res("small")
    r_mod_l = [S.res("mod") for _ in range(L)]
    r_dtab_l = [S.res("dtab") for _ in range(L)]
    r_sink_l = [S.res("sink") for _ in range(L)]
    DR = {n: S.res(n) for n in ("xT", "xB", "cq", "ckv", "kpe", "qb", "kb", "vb", "glu", "kn", "va",
                                "qn", "qpe", "mix", "out")}

    ident = cbf[:, 0:128]
    maskP = cbf[:, 128:640]
    maskN = cbf[:, 640:1152]
    maskM = [cbf[:, 1152 + i * 512:1152 + (i + 1) * 512] for i in range(4)]

    def MM(out, lhsT, rhs, start, stop, reads, writes):
        S.op("pe", lambda e: e.matmul(out, lhsT=lhsT, rhs=rhs, start=start, stop=stop), reads, writes)

    def ACT(out, in_, func, reads, writes, bias=None, scale=None, q="act"):
        kw = {}
        if bias is not None:
            kw["bias"] = bias
        if scale is not None:
            kw["scale"] = scale
        S.op(q, lambda e: e.activation(out=out, in_=in_, func=func, **kw), reads, writes)

    def TT(out, in0, in1, op, reads, writes):
        S.op("dve", lambda e: e.tensor_tensor(out=out, in0=in0, in1=in1, op=op), reads, writes)

    def STT(out, in0, scalar, in1, op0, op1, reads, writes):
        S.op("dve", lambda e: e.scalar_tensor_tensor(out=out, in0=in0, scalar=scalar, in1=in1,
                                                      op0=op0, op1=op1), reads, writes)

    def TS(out, in0, s1, s2, op0, op1, reads, writes):
        if op1 is None:
            S.op("dve", lambda e: e.tensor_scalar(out=out, in0=in0, scalar1=s1, scalar2=None, op0=op0),
                 reads, writes)
        else:
            S.op("dve", lambda e: e.tensor_scalar(out=out, in0=in0, scalar1=s1, scalar2=s2, op0=op0,
                                                  op1=op1), reads, writes)

    def RECIP(out, in_, reads, writes):
        S.op("dve", lambda e: e.reciprocal(out=out, in_=in_), reads, writes)

    def COPY(q, out, in_, reads, writes):
        if q == "act":
            S.op("act", lambda e: e.activation(out=out, in_=in_, func=AF.Copy), reads, writes)
        else:
            S.op("dve", lambda e: e.tensor_copy(out=out, in_=in_), reads, writes)

    evac_rr = [0]

    def EVAC(out, in_, reads, writes):
        q = "act" if evac_rr[0] % 2 == 0 else "dve"
        evac_rr[0] += 1
        COPY(q, out, in_, reads, writes)

    def wstream(bufs, dram_list, lookahead):
        st = {"n": 0}

        def get(i):
            while st["n"] < min(len(dram_list), i + lookahead + 1):
                b, rb = bufs[st["n"] % len(bufs)]
                S.dma("pool", b, dram_list[st["n"]], writes=[rb])
                st["n"] += 1
            return bufs[i % len(bufs)]

        return get

    def rstd_from_ps(ps_ap, rps, n, sd_ap, rsd, out_ap, rout, eps_ap):
        ACT(sd_ap, ps_ap, AF.Sqrt, [rps, r_const], [rsd], bias=eps_ap, scale=1.0 / n)
        RECIP(out_ap, sd_ap, [rsd], [rout])

    eps_t = es.enter_context(nc.sbuf_tensor("eps_t", [128, 1], F32))
    S.op("dve", lambda e: e.memset(ones_f[:, :], 1.0), (), [r_const])
    S.op("dve", lambda e: e.memset(ones_b[:, :], 1.0), (), [r_const])
    S.op("dve", lambda e: e.memset(eps_t[:, :], EPS), (), [r_const])
    S.dma("sp", cbf[:, :], cbf_d, writes=[r_const])
    S.dma("sp", small[:, :, :], small_d.rearrange("l p n -> p l n"), writes=[r_small])
    S.dma("sp", fnorm[:, :], fnorm_d, writes=[r_small])
    S.dma("sp", cTs[:, :], cT_d, writes=[r_small])
    ACT(scT[:, :], cTs[:, 0:32], AF.Silu, [r_small], [r_small])
    flagA = cTs[:, 32:33]
    flagB = cTs[:, 33:34]
    eps_ap = eps_t[:, 0:1]

    def mod_steps(l, bufs, CW, pbank, rpbank):
        modT = modT_all[:, l * 96:(l + 1) * 96, :]
        dt = dtab_all[:, l * 6:(l + 1) * 6]
        sx = sinkx_all[:, l * 8:(l + 1) * 8]
        sbc = sinkbc_all[:, l * 8:(l + 1) * 8, :]
        r_mod, r_dt, r_sk = r_mod_l[l], r_dtab_l[l], r_sink_l[l]
        sc3 = scT[:, :].rearrange("p (k r) -> p k r", r=2)
        nst = 12288 // CW
        per = 512 // CW
        nj = CW // 128
        steps = []

        def issue(i):
            wt, rw = bufs[i % 2]
            S.dma("sp", wt, ada_d[l, i // per][:, :, (i % per) * CW:(i % per + 1) * CW], writes=[rw])

        def mk(i):
            def step():
                if i == 0:
                    issue(0)
                if i + 1 < nst:
                    issue(i + 1)
                wt, rw = bufs[i % 2]
                for j in range(nj):
                    for k in range(16):
                        MM(pbank[:, 2 * j:2 * j + 2], wt[:, k, j * 128:(j + 1) * 128], sc3[:, k, :],
                           k == 0, k == 15, [rw, r_small], [rpbank])
                c0 = i * nj
                for r in range(2):
                    src_ = pbank[:, 0:2 * nj].rearrange("p (j r) -> p j r", r=2)[:, :, r]
                    TT(modT[:, c0:c0 + nj, r], src_, small[:, l, O_ADAB + c0:O_ADAB + c0 + nj], ALU.add,
                       [rpbank, r_small], [r_mod])
            return step

        for i in range(nst):
            steps.append(mk(i))

        def derived():
            for r in range(2):
                STT(dt[:, 0, :, r], modT[:, 16:32, r], 1.0, small[:, l, O_NMIX:O_NMIX + 16],
                    ALU.add, ALU.mult, [r_mod, r_small], [r_dt])
                S.op("dve", lambda e, r=r: e.tensor_copy(out=dt[:, 1, :, r], in_=modT[:, 0:16, r]),
                     [r_mod], [r_dt])
                S.op("dve", lambda e, r=r: e.tensor_copy(out=dt[:, 2, :, r], in_=modT[:, 32:48, r]),
                     [r_mod], [r_dt])
                STT(dt[:, 3, :, r], modT[:, 64:80, r], 1.0, small[:, l, O_NMLP:O_NMLP + 16],
                    ALU.add, ALU.mult, [r_mod, r_small], [r_dt])
                S.op("dve", lambda e, r=r: e.tensor_copy(out=dt[:, 4, :, r], in_=modT[:, 48:64, r]),
                     [r_mod], [r_dt])
                S.op("dve", lambda e, r=r: e.tensor_copy(out=dt[:, 5, :, r], in_=modT[:, 80:96, r]),
                     [r_mod], [r_dt])
            ACT(sx[:, :], small[:, l, O_SINK:O_SINK + 8], AF.Exp, [r_small], [r_sk])
            for h in range(8):
                TS(sbc[:, h, :], ones_f[0:64, :], sx[0:64, h:h + 1], None, ALU.mult, None,
                   [r_const, r_sk], [r_sk])

        steps.append(derived)
        return steps

    def stage_mod(l):
        S.barrier()
        ar.reset()
        bufs = [(ar.f32(8192).rearrange("p (k j) -> p k j", j=512), S.res("adaw")) for _ in range(2)]
        for st in mod_steps(l, bufs, 512, PS[0], RPS[0]):
            st()

    def modulate_group(xg, rx, G, r, which, hT, rh, sqrot, tmprot, pss, rpss, rstd, rrstd, sd, rsd):
        for k in range(16):
            sq, rsq = sqrot.next()
            ACT(sq[:, :G], xg[:, k, :], AF.Square, [rx], [rsq])
            MM(pss[:, :G], ones_f[:, :], sq[:, :G], k == 0, k == 15, [rsq, r_const], [rpss])
        rstd_from_ps(pss[:, :G], rpss, D, sd[:, :G], rsd, rstd[:, :G], rrstd, eps_ap)
        for k in range(16):
            tmp, rt = tmprot.next()
            STT(tmp[:, :G], xg[:, k, :], dtab[:, 3 * which, k, r:r + 1], rstd[:, :G], ALU.mult, ALU.mult,
                [rx, rrstd, r_dtab], [rt])
            ACT(hT[:, k, :], tmp[:, :G], AF.Identity, [rt, r_dtab], [rh],
                bias=dtab[:, 3 * which + 1, k, r:r + 1], scale=1.0)

    def stage_in(l, xsrc, rxsrc, groups, modes):
        S.barrier()
        ar.reset()
        bigf = [(ar.f32(8192), S.res("xg")) for _ in range(2)]
        bigh = [(ar.bf16(8192), S.res("hT")) for _ in range(2)]
        wb = [(ar.bf16(2048).rearrange("p (k j) -> p k j", j=128), S.res("w")) for _ in range(3)]
        aqf = ar.f32(2048)
        r_aqf = S.res()
        cqst = ar.bf16(2048)
        r_cqst = S.res()
        ropeo = Rot(S, [ar.bf16(512) for _ in range(2)])
        tmpf = Rot(S, [ar.f32(512) for _ in range(4)])
        sqrot = Rot(S, [ar.f32(512) for _ in range(2)])
        vbst = ar.bf16(512)
        r_vbst = S.res()
        glust = ar.bf16(2048)
        r_glust = S.res()
        rstd = ar.f32(512)
        r_rstd = S.res()
        sd = ar.f32(512)
        r_sd = S.res()
        rstd2 = ar.f32(512)
        r_rstd2 = S.res()
        tabs = [(ar.f32(512), ar.f32(512), S.res("tab")) for _ in range(2)]
        psrot = Rot(S, [None] * 4)
        psi = [0]

        def nextps():
            i = psi[0] % 6
            psi[0] += 1
            return PS[i], RPS[i]

        pss, rpss = PS[7], RPS[7]
        xv = xsrc.rearrange("k p t -> p k t")
        wl = []
        for gi, (t0, G, r) in enumerate(groups):
            cgs = {"full": list(range(29)), "kv": [4, 5, 6, 7, 8, 9, 18, 19, 20],
                   "kvglu": [4, 5, 6, 7, 8, 9, 18, 19, 20] + list(range(21, 29))}[modes[gi]]
            for c in cgs:
                wl.append((gi, c))
        getw = wstream(wb, [win_d[l, c] for (_, c) in wl], 2)
        wi = 0

        def load_x(gi):
            t0, G, r = groups[gi]
            xg2, rx = bigf[gi % 2]
            xg = xg2[:, :16 * G].rearrange("p (k t) -> p k t", t=G)
            S.dma("sp", xg, xv[:, :, t0:t0 + G], reads=[rxsrc], writes=[rx])
            ct, st, rt = tabs[gi % 2]
            S.dma("sp", ct[:, :G], rope_d[0, :, t0:t0 + G], writes=[rt])
            S.dma("sp", st[:, :G], rope_d[1, :, t0:t0 + G], writes=[rt])

        load_x(0)
        for gi, (t0, G, r) in enumerate(groups):
            kvonly = modes[gi] != "full"
            doglu = modes[gi] != "kv"
            if gi + 1 < len(groups):
                load_x(gi + 1)
            xg2, rx = bigf[gi % 2]
            xg = xg2[:, :16 * G].rearrange("p (k t) -> p k t", t=G)
            h2, rh = bigh[gi % 2]
            hT = h2[:, :16 * G].rearrange("p (k t) -> p k t", t=G)
            ct, st, rtab = tabs[gi % 2]
            if gi == 0:
                modulate_group(xg, rx, G, r, 0, hT, rh, sqrot, tmpf, pss, rpss, rstd, r_rstd, sd, r_sd)

            def proj(M):
                nonlocal wi
                wt, rw = getw(wi)
                wi += 1
                pb, rp = nextps()
                for k in range(16):
                    MM(pb[:M, :G], wt[:, k, :M], hT[:, k, :], k == 0, k == 15, [rw, rh], [rp])
                return pb, rp

            def rope(pa, rpa, pbb, rpb, M, out_ap, rout):
                t1, r1 = tmpf.next()
                t2, r2 = tmpf.next()
                TT(t1[:M, :G], pbb[:M, :G], st[:M, :G], ALU.mult, [rpb, rtab], [r1])
                TT(t2[:M, :G], pa[:M, :G], ct[:M, :G], ALU.mult, [rpa, rtab], [r2])
                TT(out_ap, t1[:M, :G], t2[:M, :G], ALU.add, [r1, r2], [rout])

            def latent_norm(norm_off, dst, rdst):
                aq3 = aqf[:, :4 * G].rearrange("p (j t) -> p j t", t=G)
                for j in range(4):
                    pb, rp = proj(128)
                    COPY("act", aq3[:, j, :], pb[:, :G], [rp], [r_aqf])
                for j in range(4):
                    sq, rsq = sqrot.next()
                    ACT(sq[:, :G], aq3[:, j, :], AF.Square, [r_aqf], [rsq])
                    MM(pss[:, :G], ones_f[:, :], sq[:, :G], j == 0, j == 3, [rsq, r_const], [rpss])
                rstd_from_ps(pss[:, :G], rpss, 512, sd[:, :G], r_sd, rstd2[:, :G], r_rstd2, eps_ap)
                cq3 = cqst[:, :4 * G].rearrange("p (j t) -> p j t", t=G)
                for j in range(4):
                    STT(cq3[:, j, :], aq3[:, j, :], small[:, l, norm_off + j:norm_off + j + 1],
                        rstd2[:, :G], ALU.mult, ALU.mult, [r_aqf, r_rstd2, r_small], [r_cqst])
                S.dma("sp", dst.rearrange("j p t -> p j t")[:, :, t0:t0 + G], cq3, reads=[r_cqst],
                      writes=[rdst])

            if not kvonly:
                latent_norm(O_QN, cq_s, DR["cq"])
            latent_norm(O_KVN, ckv_s, DR["ckv"])
            pa, rpa = proj(64)
            pbb, rpb = proj(64)
            o, ro = ropeo.next()
            rope(pa, rpa, pbb, rpb, 64, o[:64, :G], ro)
            S.dma("sp", kpe_s[:, t0:t0 + G], o[:64, :G], reads=[ro], writes=[DR["kpe"]])
            if gi + 1 < len(groups):
                t0n, Gn, rn = groups[gi + 1]
                xn2, rxn = bigf[(gi + 1) % 2]
                hn2, rhn = bigh[(gi + 1) % 2]
                modulate_group(xn2[:, :16 * Gn].rearrange("p (k t) -> p k t", t=Gn), rxn, Gn, rn, 0,
                               hn2[:, :16 * Gn].rearrange("p (k t) -> p k t", t=Gn), rhn, sqrot, tmpf,
                               pss, rpss, rstd, r_rstd, sd, r_sd)
            if not kvonly:
                for j in range(4):
                    pa, rpa = proj(128)
                    pbb, rpb = proj(128)
                    o, ro = ropeo.next()
                    rope(pa, rpa, pbb, rpb, 128, o[:, :G], ro)
                    S.dma("sp", qb_s[j, :, t0:t0 + G], o[:, :G], reads=[ro], writes=[DR["qb"]])
            pa, rpa = proj(128)
            pbb, rpb = proj(128)
            o, ro = ropeo.next()
            rope(pa, rpa, pbb, rpb, 128, o[:, :G], ro)
            S.dma("sp", kb_s[:, t0:t0 + G], o[:, :G], reads=[ro], writes=[DR["kb"]])
            wt, rw = getw(wi)
            wi += 1
            pb, rp = nextps()
            ntt = G // 128
            for tt in range(ntt):
                for k in range(16):
                    MM(pb[:, tt * 128:(tt + 1) * 128], hT[:, k, tt * 128:(tt + 1) * 128], wt[:, k, :],
                       k == 0, k == 15, [rw, rh], [rp])
            COPY("act", vbst[:, :G], pb[:, :G], [rp], [r_vbst])
            S.dma("sp", vb_s.rearrange("c p d -> p c d")[:, t0 // 128:t0 // 128 + ntt, :],
                  vbst[:, :G].rearrange("p (c d) -> p c d", d=128), reads=[r_vbst], writes=[DR["vb"]])
            if doglu:
                gl3 = glust[:, :4 * G].rearrange("p (j t) -> p j t", t=G)
                for j in range(4):
                    pg, rpg = proj(128)
                    sg, rsg = tmpf.next()
                    ACT(sg[:, :G], pg[:, :G], AF.Sigmoid, [rpg], [rsg])
                    pa, rpa = proj(128)
                    TT(gl3[:, j, :], pa[:, :G], sg[:, :G], ALU.mult, [rpa, rsg], [r_glust])
                S.dma("sp", glu_s.rearrange("j p t -> p j t")[:, :, t0:t0 + G], gl3, reads=[r_glust],
                      writes=[DR["glu"]])

    def stage_kvup(l):
        S.barrier()
        ar.reset()
        wk = ar.bf16(8 * 512).rearrange("p (h r d) -> p h r d", h=8, r=4)
        wv = ar.bf16(2 * 2048).rearrange("p (a r d) -> p a r d", a=2, r=4)
        r_w = S.res()
        S.dma("pool", wk, wk_d[l].rearrange("h p r d -> p h r d"), writes=[r_w])
        S.dma("pool", wv, wv_d[l].rearrange("a p r d -> p a r d"), writes=[r_w])
        ck = [(ar.bf16(2048), S.res()) for _ in range(2)]
        kst = [(ar.bf16(8 * 512), S.res()) for _ in range(2)]
        vst = [(ar.bf16(4 * 1024), S.res()) for _ in range(2)]
        psi = [0]

        def nextps():
            i = psi[0] % 8
            psi[0] += 1
            return PS[i], RPS[i]

        def load(gi):
            t0, G, r = GROUPS[gi]
            c2, rc = ck[gi % 2]
            S.dma("sp", c2[:, :4 * G].rearrange("p (j t) -> p j t", t=G),
                  ckv_s.rearrange("j p t -> p j t")[:, :, t0:t0 + G], reads=[DR["ckv"]], writes=[rc])

        load(0)
        for gi, (t0, G, r) in enumerate(GROUPS):
            if gi + 1 < len(GROUPS):
                load(gi + 1)
            c2, rc = ck[gi % 2]
            c3 = c2[:, :4 * G].rearrange("p (j t) -> p j t", t=G)
            k2, rk = kst[gi % 2]
            k3 = k2[:, :8 * G].rearrange("p (h t) -> p h t", t=G)
            for h in range(8):
                pb, rp = nextps()
                for j in range(4):
                    MM(pb[:, :G], wk[:, h, j, :], c3[:, j, :], j == 0, j == 3, [r_w, rc], [rp])
                EVAC(k3[:, h, :], pb[:, :G], [rp], [rk])
            S.dma("sp", kn_s.rearrange("h p t -> p h t")[:, :, t0:t0 + G], k3, reads=[rk], writes=[DR["kn"]])
            ntt = G // 128
            v2, rv = vst[gi % 2]
            v3 = v2[:, :ntt * 1024].rearrange("p (c d) -> p c d", d=1024)
            for tt in range(ntt):
                for a in range(2):
                    pb, rp = nextps()
                    for j in range(4):
                        MM(pb[:, :], c3[:, j, tt * 128:(tt + 1) * 128], wv[:, a, j, :], j == 0, j == 3,
                           [r_w, rc], [rp])
                    EVAC(v3[:, tt, a * 512:(a + 1) * 512], pb[:, :], [rp], [rv])
            S.dma("sp", va_s.rearrange("c p d -> p c d")[:, t0 // 128:t0 // 128 + ntt, :], v3, reads=[rv],
                  writes=[DR["va"]])

    def stage_qup(l, groups):
        S.barrier()
        ar.reset()
        wqn = ar.bf16(8 * 512).rearrange("p (h r d) -> p h r d", h=8, r=4)
        wqp = ar.bf16(8 * 256).rearrange("p (h r d) -> p h r d", h=8, r=4)
        wqr = ar.bf16(8 * 256).rearrange("p (h r d) -> p h r d", h=8, r=4)
        r_w = S.res()
        S.dma("pool", wqn, wqn_d[l].rearrange("h p r d -> p h r d"), writes=[r_w])
        S.dma("pool", wqp, wqp_d[l].rearrange("h p r d -> p h r d"), writes=[r_w])
        S.dma("pool", wqr, wqr_d[l].rearrange("h p r d -> p h r d"), writes=[r_w])
        ck = [(ar.bf16(2048), S.res()) for _ in range(2)]
        qst = [(ar.bf16(8 * 512), S.res()) for _ in range(2)]
        pst = [(ar.bf16(8 * 512), S.res()) for _ in range(2)]
        tabs = [(ar.f32(512), ar.f32(512), S.res("tab")) for _ in range(2)]
        tmpf = Rot(S, [ar.f32(512) for _ in range(4)])
        psi = [0]

        def nextps():
            i = psi[0] % 8
            psi[0] += 1
            return PS[i], RPS[i]

        def load(gi):
            t0, G, r = groups[gi]
            c2, rc = ck[gi % 2]
            S.dma("sp", c2[:, :4 * G].rearrange("p (j t) -> p j t", t=G),
                  cq_s.rearrange("j p t -> p j t")[:, :, t0:t0 + G], reads=[DR["cq"]], writes=[rc])
            ct, st, rt = tabs[gi % 2]
            S.dma("sp", ct[:64, :G], rope_d[0, 0:64, t0:t0 + G], writes=[rt])
            S.dma("sp", st[:64, :G], rope_d[1, 0:64, t0:t0 + G], writes=[rt])

        load(0)
        for gi, (t0, G, r) in enumerate(groups):
            if gi + 1 < len(groups):
                load(gi + 1)
            c2, rc = ck[gi % 2]
            c3 = c2[:, :4 * G].rearrange("p (j t) -> p j t", t=G)
            ct, st, rtab = tabs[gi % 2]
            q2, rq = qst[gi % 2]
            q3 = q2[:, :8 * G].rearrange("p (h t) -> p h t", t=G)
            p2, rpp = pst[gi % 2]
            p3 = p2[:, :8 * G].rearrange("p (h t) -> p h t", t=G)
            for h in range(8):
                pb, rp = nextps()
                for j in range(4):
                    MM(pb[:, :G], wqn[:, h, j, :], c3[:, j, :], j == 0, j == 3, [r_w, rc], [rp])
                EVAC(q3[:, h, :], pb[:, :G], [rp], [rq])
                pa, rpa = nextps()
                for j in range(4):
                    MM(pa[:64, :G], wqp[:, h, j, :], c3[:, j, :], j == 0, j == 3, [r_w, rc], [rpa])
                pbb, rpb = nextps()
                for j in range(4):
                    MM(pbb[:64, :G], wqr[:, h, j, :], c3[:, j, :], j == 0, j == 3, [r_w, rc], [rpb])
                t1, r1 = tmpf.next()
                t2, r2 = tmpf.next()
                TT(t1[:64, :G], pbb[:64, :G], st[:64, :G], ALU.mult, [rpb, rtab], [r1])
                TT(t2[:64, :G], pa[:64, :G], ct[:64, :G], ALU.mult, [rpa, rtab], [r2])
                TT(p3[:64, h, :], t1[:64, :G], t2[:64, :G], ALU.add, [r1, r2], [rpp])
            S.dma("sp", qn_s.rearrange("h p t -> p h t")[:, :, t0:t0 + G], q3, reads=[rq], writes=[DR["qn"]])
            S.dma("sp", qpe_s.rearrange("h p t -> p h t")[:, :, t0:t0 + G], p3[:64], reads=[rpp],
                  writes=[DR["qpe"]])

    def stage_atta(l, groups, extra_mod=None):
        S.barrier()
        ar.reset()
        extra = []
        if extra_mod is not None:
            ebufs = [(ar.f32(4096).rearrange("p (k j) -> p k j", j=256), S.res("adaw")) for _ in range(2)]
            extra = mod_steps(extra_mod, ebufs, 256, PS[3], RPS[3])
        kpe = ar.bf16(T)
        r_kpe = S.res()
        S.dma("sp", kpe[:64, :], kpe_s, reads=[DR["kpe"]], writes=[r_kpe])
        kn = [(ar.bf16(T), S.res()) for _ in range(2)]
        vh = [(ar.bf16(34 * 128).rearrange("p (c d) -> p c d", d=128), S.res()) for _ in range(2)]
        qn = [(ar.bf16(512), ar.bf16(512), S.res()) for _ in range(3)]
        pt = Rot(S, [ar.bf16(512) for _ in range(4)])
        ost = Rot(S, [ar.f32(512) for _ in range(2)])
        rden = Rot(S, [ar.f32(512) for _ in range(2)])
        work = [(h, gi) for h in range(8) for gi in range(len(groups))]

        def load_head(h):
            k2, rk = kn[h % 2]
            S.dma("sp", k2[:, :], kn_s[h], reads=[DR["kn"]], writes=[rk])
            v3, rv = vh[h % 2]
            S.dma("sp", v3, va_s.rearrange("c p d -> p c d")[:, :, h * 128:(h + 1) * 128],
                  reads=[DR["va"]], writes=[rv])

        def load_q(wi_):
            h, gi = work[wi_]
            t0, G, r = groups[gi]
            qa, qp, rq = qn[wi_ % 3]
            S.dma("sp", qa[:, :G], qn_s[h, :, t0:t0 + G], reads=[DR["qn"]], writes=[rq])
            S.dma("sp", qp[:64, :G], qpe_s[h, :, t0:t0 + G], reads=[DR["qpe"]], writes=[rq])

        load_head(0)
        load_q(0)
        load_q(1)
        sps = [0]
        pend = []

        def flush(keep):
            while len(pend) > keep:
                pend.pop(0)()

        for wi_, (h, gi) in enumerate(work):
            t0, G, r = groups[gi]
            if gi == 1 and h + 1 < 8:
                load_head(h + 1)
            if wi_ + 2 < len(work):
                load_q(wi_ + 2)
            k2, rk = kn[h % 2]
            v3, rv = vh[h % 2]
            qa, qp, rq = qn[wi_ % 3]
            chunks = list(range(34)) if r == 0 else [32, 33]
            po, rpo = PS[4 + (wi_ % 2) * 2], RPS[4 + (wi_ % 2) * 2]
            pd, rpd = PS[5 + (wi_ % 2) * 2], RPS[5 + (wi_ % 2) * 2]
            for ci, c in enumerate(chunks):
                psb, rps_ = PS[sps[0] % 3], RPS[sps[0] % 3]
                sps[0] += 1
                MM(psb[:, :G], k2[:, c * 128:(c + 1) * 128], qa[:, :G], True, False, [rk, rq], [rps_])
                MM(psb[:, :G], kpe[:64, c * 128:(c + 1) * 128], qp[:64, :G], False, True, [r_kpe, rq], [rps_])
                p, rp = pt.next()
                ACT(p[:, :G], psb[:, :G], AF.Exp, [rps_], [rp], scale=MLA_SCALE)

                def back(ci=ci, c=c, p=p, rp=rp, po=po, rpo=rpo, pd=pd, rpd=rpd, v3=v3, rv=rv, G=G,
                         n=len(chunks), h=h, t0=t0):
                    MM(po[:, :G], v3[:, c, :], p[:, :G], ci == 0, ci == n - 1, [rv, rp], [rpo])
                    MM(pd[:, :G], ones_b[:, :], p[:, :G], ci == 0, ci == n - 1, [r_const, rp], [rpd])
                    if ci == n - 1:
                        rd, rrd = rden.next()
                        RECIP(rd[:, :G], pd[:, :G], [rpd], [rrd])
                        o, ro = ost.next()
                        TT(o[:, :G], po[:, :G], rd[:, :G], ALU.mult, [rpo, rrd], [ro])
                        S.dma("sp", mix_s[h, :, t0:t0 + G], o[:, :G], reads=[ro], writes=[DR["mix"]])

                pend.append(back)
                flush(2)
            if extra and wi_ >= 1:
                extra.pop(0)()
        flush(0)
        while extra:
            extra.pop(0)()

    def stage_attb(l, groups):
        S.barrier()
        ar.reset()
        kb = ar.bf16(2 * T).rearrange("p (h t) -> p h t", h=2)
        vb = ar.bf16(34 * 128).rearrange("p (c d) -> p c d", d=128)
        r_kv = S.res()
        S.dma("sp", kb[:64], kb_s.rearrange("(h p) t -> p h t", p=64), reads=[DR["kb"]], writes=[r_kv])
        S.dma("sp", vb, vb_s.rearrange("c p d -> p c d"), reads=[DR["vb"]], writes=[r_kv])
        qg = [(ar.bf16(8 * 512).rearrange("p (h t) -> p h t", h=8), S.res()) for _ in range(2)]
        pt = Rot(S, [ar.bf16(512) for _ in range(4)])
        ost = Rot(S, [ar.f32(512) for _ in range(2)])
        den = Rot(S, [ar.f32(512) for _ in range(2)])
        qbv = qb_s.rearrange("j (a p) t -> p (j a) t", p=64)
        mixb = mix_s.rearrange("k (a p) t -> p (k a) t", p=64)

        def load(gi):
            t0, G, r = groups[gi]
            q3, rq = qg[gi % 2]
            S.dma("sp", q3[:64, :, :G], qbv[:, :, t0:t0 + G], reads=[DR["qb"]], writes=[rq])

        load(0)
        it = 0
        sps = [0]
        pend = []
        for gi, (t0, G, r) in enumerate(groups):
            if gi + 1 < len(groups):
                load(gi + 1)
            q3, rq = qg[gi % 2]
            for nn in range(G // 128):
                n = t0 // 128 + nn
                if r == 0:
                    prv = (31, maskM[0]) if n == 0 else ((15, maskM[2]) if n == 16 else (n - 1, maskP))
                    nxt = (16, maskM[1]) if n == 15 else ((0, maskM[3]) if n == 31 else (n + 1, maskN))
                    chunks = [prv, (n, None), nxt, (32, None), (33, None)]
                else:
                    chunks = [(32, None), (33, None)]
                for kh in range(2):
                    po, rpo = PS[4 + (it % 2) * 2], RPS[4 + (it % 2) * 2]
                    pd, rpd = PS[5 + (it % 2) * 2], RPS[5 + (it % 2) * 2]
                    it += 1
                    rhs = q3[:64, 4 * kh:4 * kh + 4, nn * 128:(nn + 1) * 128]
                    for ci, (c, mask) in enumerate(chunks):
                        psb, rps_ = PS[sps[0] % 4], RPS[sps[0] % 4]
                        sps[0] += 1
                        ps3 = psb[:, :].rearrange("p (g q) -> p g q", g=4)
                        MM(ps3, kb[:64, kh, c * 128:(c + 1) * 128], rhs, True, mask is None, [r_kv, rq], [rps_])
                        if mask is not None:
                            MM(ps3, ident, mask.rearrange("p (g q) -> p g q", g=4), False, True,
                               [r_const], [rps_])
                        p, rp = pt.next()
                        ACT(p[:, :], psb[:, :], AF.Exp, [rps_], [rp], scale=SWA_SCALE)

                        def back(ci=ci, c=c, p=p, rp=rp, po=po, rpo=rpo, pd=pd, rpd=rpd, kh=kh, n=n,
                                 nch=len(chunks)):
                            last = ci == nch - 1
                            MM(po[:64, :], vb[:, c, kh * 64:(kh + 1) * 64], p[:, :], ci == 0, last,
                               [r_kv, rp], [rpo])
                            MM(pd[:64, :], ones_b[:, 0:64], p[:, :], ci == 0, last, [r_const, rp], [rpd])
                            if last:
                                dn, rdn = den.next()
                                TT(dn[:64, :], pd[:64, :],
                                   sinkbc[:, 4 * kh:4 * kh + 4, :].rearrange("p g q -> p (g q)"), ALU.add,
                                   [rpd, r_sink], [rdn])
                                RECIP(dn[:64, :], dn[:64, :], [rdn], [rdn])
                                o, ro = ost.next()
                                TT(o[:64, :], po[:64, :], dn[:64, :], ALU.mult, [rpo, rdn], [ro])
                                S.dma("sp", mixb[:, 16 + 4 * kh:16 + 4 * kh + 4, n * 128:(n + 1) * 128],
                                      o[:64, :].rearrange("p (g q) -> p g q", g=4), reads=[ro],
                                      writes=[DR["mix"]])

                        pend.append(back)
                        while len(pend) > 2:
                            pend.pop(0)()
        while pend:
            pend.pop(0)()

    def stage_conv(l, groups):
        S.barrier()
        ar.reset()
        gin = [(ar.bf16(4 * 544).rearrange("p (c t) -> p c t", c=4), S.res()) for _ in range(2)]
        dg = ar.bf16(124 * 128).rearrange("p (i j) -> p i j", j=128)
        r_dg = S.res()
        for i in range(124):
            if i % 2 == 0:
                TS(dg[:, i, :], ident, small[:, l, O_CW + i:O_CW + i + 1], None, ALU.mult, None,
                   [r_const, r_small], [r_dg])
            else:
                ACT(dg[:, i, :], ident, AF.Copy, [r_const, r_small], [r_dg],
                    scale=small[:, l, O_CW + i:O_CW + i + 1])
        hb = [(ar.f32(512), S.res()) for _ in range(4)]
        sqrot = Rot(S, [ar.f32(512) for _ in range(2)])
        mean = ar.f32(512)
        r_mean = S.res()
        msq = ar.f32(512)
        r_msq = S.res()
        var = ar.f32(512)
        r_var = S.res()
        rstd = ar.f32(512)
        r_rstd = S.res()
        tmpf = Rot(S, [ar.f32(512) for _ in range(2)])
        ost = [(ar.f32(2048).rearrange("p (c t) -> p c t", c=4), S.res()) for _ in range(2)]
        gv = glu_s.rearrange("j p t -> p j t")
        cps = [0]

        def load(gi):
            t0, G, r = groups[gi]
            g3, rg = gin[gi % 2]
            if r == 1:
                S.op("dve", lambda e, g3=g3: e.memset(g3[:, :, 0:15], 0.0), (), [rg])
                S.op("dve", lambda e, g3=g3, G=G: e.memset(g3[:, :, G + 15:G + 30], 0.0), (), [rg])
                S.dma("sp", g3[:, :, 15:15 + G], gv[:, :, t0:t0 + G], reads=[DR["glu"]], writes=[rg])
                return
            gidx = t0 // 512
            a, b = t0 - 15, t0 + G + 15
            if gidx == 0:
                S.dma("sp", g3[:, :, 0:15], gv[:, :, TL - 15:TL], reads=[DR["glu"]], writes=[rg])
                a = 0
            if gidx == 7:
                S.dma("sp", g3[:, :, G + 15:G + 30], gv[:, :, 0:15], reads=[DR["glu"]], writes=[rg])
                b = TL
            S.dma("sp", g3[:, :, a - (t0 - 15):b - (t0 - 15)], gv[:, :, a:b], reads=[DR["glu"]], writes=[rg])
            if gidx in (0, 4):
                fl = flagA if gidx == 0 else flagB
                TS(g3[:, :, 0:15], g3[:, :, 0:15], fl, None, ALU.mult, None, [rg, r_small], [rg])
            if gidx in (3, 7):
                fl = flagA if gidx == 7 else flagB
                TS(g3[:, :, G + 15:G + 30], g3[:, :, G + 15:G + 30], fl, None, ALU.mult, None,
                   [rg, r_small], [rg])

        load(0)
        cw = O_CW
        for gi, (t0, G, r) in enumerate(groups):
            if gi + 1 < len(groups):
                load(gi + 1)
            g3, rg = gin[gi % 2]
            pm, rpm = PS[(gi % 2) * 2], RPS[(gi % 2) * 2]
            pv, rpv = PS[(gi % 2) * 2 + 1], RPS[(gi % 2) * 2 + 1]
            for c in range(4):
                pc, rpc = PS[4 + cps[0] % 4], RPS[4 + cps[0] % 4]
                cps[0] += 1
                for k in range(31):
                    MM(pc[:, :G], dg[:, c * 31 + k, :], g3[:, c, k:k + G], k == 0, k == 30, [r_dg, rg], [rpc])
                h_, rh_ = hb[c]
                ACT(h_[:, :G], pc[:, :G], AF.Identity, [rpc, r_small], [rh_],
                    bias=small[:, l, O_CB + c:O_CB + c + 1], scale=1.0)
                MM(pm[:, :G], ones_f[:, :], h_[:, :G], c == 0, c == 3, [rh_, r_const], [rpm])
                sq, rsq = sqrot.next()
                ACT(sq[:, :G], h_[:, :G], AF.Square, [rh_], [rsq])
                MM(pv[:, :G], ones_f[:, :], sq[:, :G], c == 0, c == 3, [rsq, r_const], [rpv])
            S.op("act", lambda e, pm=pm, G=G: e.activation(out=mean[:, :G], in_=pm[:, :G], func=AF.Copy,
                                                          scale=1.0 / 512), [rpm], [r_mean])
            TT(msq[:, :G], mean[:, :G], mean[:, :G], ALU.mult, [r_mean], [r_msq])
            STT(var[:, :G], pv[:, :G], 1.0 / 512, msq[:, :G], ALU.mult, ALU.subtract, [rpv, r_msq], [r_var])
            ACT(var[:, :G], var[:, :G], AF.Sqrt, [r_var, r_const], [r_var], bias=eps_ap, scale=1.0)
            RECIP(rstd[:, :G], var[:, :G], [r_var], [r_rstd])
            o3, ro = ost[gi % 2]
            for c in range(4):
                h_, rh_ = hb[c]
                t1, r1 = tmpf.next()
                TT(t1[:, :G], h_[:, :G], mean[:, :G], ALU.subtract, [rh_, r_mean], [r1])
                STT(t1[:, :G], t1[:, :G], small[:, l, O_LNG + c:O_LNG + c + 1], rstd[:, :G], ALU.mult, ALU.mult,
                    [r1, r_rstd, r_small], [r1])
                ACT(o3[:, c, :G], t1[:, :G], AF.Silu, [r1, r_small], [ro],
                    bias=small[:, l, O_LNB + c:O_LNB + c + 1], scale=1.0)
            S.dma("sp", mix_s.rearrange("k p t -> p k t")[:, 12:16, t0:t0 + G], o3[:, :, :G], reads=[ro],
                  writes=[DR["mix"]])

    def stage_out_mlp(l, groups, xsrc, rxsrc, xdst, rxdst, final_norm):
        S.barrier()
        ar.reset()
        bigf = [(ar.f32(8192), S.res("bigf")) for _ in range(2)]
        bigh = (ar.bf16(8192), S.res("bigh"))
        ut = (ar.bf16(32 * 512), S.res("ut"))
        wb = [(ar.bf16(2048).rearrange("p (k j) -> p k j", j=128), S.res("w")) for _ in range(3)]
        w2b = [(ar.bf16(4096).rearrange("p (k j) -> p k j", j=128), S.res("w2")) for _ in range(3)]
        sqrot = Rot(S, [ar.f32(512) for _ in range(2)])
        tmpf = Rot(S, [ar.f32(512) for _ in range(2)])
        rs = [(ar.f32(512), S.res()) for _ in range(3)]
        sd = ar.f32(512)
        r_sd = S.res()
        rstd = ar.f32(512)
        r_rstd = S.res()
        psi = [0]

        def nextps():
            i = psi[0] % 4
            psi[0] += 1
            return PS[i], RPS[i]

        xv = xsrc.rearrange("k p t -> p k t")
        mv = mix_s.rearrange("k p t -> p k t")
        wl = []
        w2l = []
        for gi in range(len(groups)):
            wl += [wout_d[l, d] for d in range(16)]
            for hf in range(2):
                wl += [w1_d[l, hf * 32 + f] for f in range(32)]
                w2l += [w2_d[l, hf, d] for d in range(16)]
        getw = wstream(wb, wl, 2)
        getw2 = wstream(w2b, w2l, 2)
        wi = 0
        w2i = 0

        def load_mix(gi):
            t0, G, r = groups[gi]
            m2, rm = bigf[0]
            S.dma("sp", m2[:, :16 * G].rearrange("p (k t) -> p k t", t=G), mv[:, :, t0:t0 + G],
                  reads=[DR["mix"]], writes=[rm])

        def load_x(gi):
            t0, G, r = groups[gi]
            x2, rx = bigf[1]
            S.dma("sp", x2[:, :16 * G].rearrange("p (k t) -> p k t", t=G), xv[:, :, t0:t0 + G],
                  reads=[rxsrc], writes=[rx])

        load_mix(0)
        for gi, (t0, G, r) in enumerate(groups):
            load_x(gi)
            m2, rm = bigf[0]
            mix = m2[:, :16 * G].rearrange("p (k t) -> p k t", t=G)
            x2, rx = bigf[1]
            xg = x2[:, :16 * G].rearrange("p (k t) -> p k t", t=G)
            h2, rh = bigh
            hT = h2[:, :16 * G].rearrange("p (k t) -> p k t", t=G)
            segs = [(0, 8, 1024), (8, 12, 512), (12, 16, 512)]
            for si, (k0, k1, n) in enumerate(segs):
                pb, rp = PS[4 + si], RPS[4 + si]
                for k in range(k0, k1):
                    sq, rsq = sqrot.next()
                    ACT(sq[:, :G], mix[:, k, :], AF.Square, [rm], [rsq])
                    MM(pb[:, :G], ones_f[:, :], sq[:, :G], k == k0, k == k1 - 1, [rsq, r_const], [rp])
                rstd_from_ps(pb[:, :G], rp, n, sd[:, :G], r_sd, rs[si][0][:, :G], rs[si][1], eps_ap)
            for si, (k0, k1, n) in enumerate(segs):
                for k in range(k0, k1):
                    STT(hT[:, k, :], mix[:, k, :], small[:, l, O_ON + k:O_ON + k + 1], rs[si][0][:, :G],
                        ALU.mult, ALU.mult, [rm, rs[si][1], r_small], [rh])
            if gi + 1 < len(groups):
                load_mix(gi + 1)
            for d in range(16):
                wt, rw = getw(wi)
                wi += 1
                pb, rp = nextps()
                for k in range(16):
                    MM(pb[:, :G], wt[:, k, :], hT[:, k, :], k == 0, k == 15, [rw, rh], [rp])
                STT(xg[:, d, :], pb[:, :G], dtab[:, 2, d, r:r + 1], xg[:, d, :], ALU.mult, ALU.add,
                    [rp, rx, r_dtab], [rx])
            modulate_group(xg, rx, G, r, 1, hT, rh, sqrot, tmpf, PS[7], RPS[7], rstd, r_rstd, sd, r_sd)
            u2, ru = ut
            u3 = u2[:, :32 * G].rearrange("p (f t) -> p f t", t=G)
            for hf in range(2):
                for f in range(32):
                    wt, rw = getw(wi)
                    wi += 1
                    pb, rp = nextps()
                    for k in range(16):
                        MM(pb[:, :G], wt[:, k, :], hT[:, k, :], k == 0, k == 15, [rw, rh], [rp])
                    sq, rsq = sqrot.next()
                    ACT(sq[:, :G], pb[:, :G], AF.Square, [rp], [rsq])
                    STT(u3[:, f, :], pb[:, :G], 0.0, sq[:, :G], ALU.is_gt, ALU.mult, [rp, rsq], [ru])
                for d in range(16):
                    w2t, rw2 = getw2(w2i)
                    w2i += 1
                    pb, rp = nextps()
                    for f in range(32):
                        MM(pb[:, :G], w2t[:, f, :], u3[:, f, :], f == 0, f == 31, [rw2, ru], [rp])
                    STT(xg[:, d, :], pb[:, :G], dtab[:, 5, d, r:r + 1], xg[:, d, :], ALU.mult, ALU.add,
                        [rp, rx, r_dtab], [rx])
            if final_norm:
                for k in range(16):
                    sq, rsq = sqrot.next()
                    ACT(sq[:, :G], xg[:, k, :], AF.Square, [rx], [rsq])
                    MM(PS[7][:, :G], ones_f[:, :], sq[:, :G], k == 0, k == 15, [rsq, r_const], [RPS[7]])
                rstd_from_ps(PS[7][:, :G], RPS[7], D, sd[:, :G], r_sd, rstd[:, :G], r_rstd, eps_ap)
                for k in range(16):
                    STT(xg[:, k, :], xg[:, k, :], fnorm[:, k:k + 1], rstd[:, :G], ALU.mult, ALU.mult,
                        [rx, r_rstd, r_small], [rx])
            S.dma("sp", xdst.rearrange("k p t -> p k t")[:, :, t0:t0 + G], xg, reads=[rx], writes=[rxdst])

    for l in layers:
        lastl = (l == DEPTH - 1)
        xsrc, rxsrc = (xT_in, DR["xT"]) if l == layers[0] else (xB, DR["xB"])
        xdst, rxdst = (outT, DR["out"]) if (lastl and final) else (xB, DR["xB"])
        qgroups = GROUPS[:4] if lastl else GROUPS
        modes = (["full"] * 4 + ["kvglu", "kv", "kv", "kvglu", "kv"]) if lastl else ["full"] * 9
        dtab = dtab_all[:, l * 6:(l + 1) * 6]
        sinkbc = sinkbc_all[:, l * 8:(l + 1) * 8, :]
        r_dtab = r_dtab_l[l]
        r_sink = r_sink_l[l]
        if l == layers[0]:
            stage_mod(l)
        stage_in(l, xsrc, rxsrc, GROUPS, modes)
        stage_kvup(l)
        stage_qup(l, qgroups)
        stage_atta(l, qgroups, extra_mod=(l + 1) if (l + 1) in layers else None)
        stage_attb(l, qgroups)
        stage_conv(l, qgroups)
        stage_out_mlp(l, qgroups, xsrc, rxsrc, xdst, rxdst, lastl and final)

    blk = es.enter_context(nc.Block())
    S.finalize(sems, blk)
    es.close()
    return nc, S


def _fm(v):
    return np.ascontiguousarray(np.asarray(v, np.float32).reshape(-1, 128).T)


def _tile(W, nc_):
    K, N = W.shape
    return np.ascontiguousarray(W.reshape(K // 128, 128, N // nc_, nc_).transpose(2, 1, 0, 3))


PERM64 = np.concatenate([np.arange(16, 32), np.arange(0, 16), np.arange(48, 64), np.arange(32, 48)])


def _rope_tables():
    rows = TL // 64
    row = np.repeat(np.arange(rows, dtype=np.float32), 64)
    col = np.tile(np.arange(64, dtype=np.float32), rows)
    inv = (np.float32(10000.0) ** (-(np.arange(16, dtype=np.float32) / np.float32(16)))).astype(np.float32)
    C = np.ones((64, T), np.float32)
    Sg = np.zeros((64, T), np.float32)
    for d in range(64):
        blk, i = d // 16, d % 16
        pos = row if blk < 2 else col
        ang = (pos * inv[i]).astype(np.float32).astype(np.float64)
        C[d, :TL] = np.cos(ang)
        Sg[d, :TL] = np.sin(ang) * (-1.0 if blk % 2 == 0 else 1.0)
    tab = np.zeros((2, 128, T), np.float32)
    tab[0, :64] = C
    tab[0, 64:] = C
    tab[1, :64] = Sg
    tab[1, 64:] = Sg
    return tab


def _prep_shared(inp):
    f = lambda k: np.asarray(inp[k], np.float32)
    L = DEPTH
    sh = {}
    small = np.zeros((L, 128, NS), np.float32)
    for l in range(L):
        small[l, :, O_NMIX:O_NMIX + 16] = _fm(f("norm_mix")[l])
        small[l, :, O_NMLP:O_NMLP + 16] = _fm(f("norm_mlp")[l])
        small[l, :, O_QN:O_QN + 4] = _fm(f("mla_q_norm")[l])
        small[l, :, O_KVN:O_KVN + 4] = _fm(f("mla_kv_norm")[l])
        small[l, :, O_CB:O_CB + 4] = _fm(f("conv_b")[l])
        small[l, :, O_LNG:O_LNG + 4] = _fm(f("conv_ln_g")[l])
        small[l, :, O_LNB:O_LNB + 4] = _fm(f("conv_ln_b")[l])
        small[l, :, O_ON:O_ON + 16] = _fm(f("out_norm")[l])
        small[l, :, O_ADAB:O_ADAB + 96] = _fm(f("ada_b")[l])
        cw = f("conv_w")[l][:, 0, :]
        small[l, :, O_CW:O_CW + 124] = cw.T.reshape(4, 128, 31).transpose(1, 0, 2).reshape(128, 124)
        small[l, :, O_SINK:O_SINK + 8] = np.broadcast_to(f("swa_sink")[l][None, :], (128, 8))
    sh["small"] = small
    sh["fnorm"] = _fm(f("final_norm"))
    sh["_rope"] = _rope_tables()
    cbf = np.zeros((128, 1152), np.float32)
    cbf[:, 0:128] = np.eye(128, dtype=np.float32)
    j = np.arange(128)[:, None]
    qi = np.arange(128)[None, :]
    mp = np.where(j >= qi, 0.0, NEG).astype(np.float32)
    mn = np.where(j <= qi, 0.0, NEG).astype(np.float32)
    cbf[:, 128:640] = np.tile(mp, (1, 4))
    cbf[:, 640:1152] = np.tile(mn, (1, 4))
    sh["_cbf"] = cbf
    sh["ada_t"] = np.stack([_tile(f("ada_w")[l], 512) for l in range(L)])
    wins = []
    for l in range(L):
        W = f("w_in")[l]
        groups = []
        pad = lambda a: np.concatenate([a, np.zeros((2048, 128 - a.shape[1]), np.float32)], 1) if a.shape[1] < 128 else a
        for j_ in range(4):
            groups.append(W[:, j_ * 128:(j_ + 1) * 128])
        for j_ in range(4):
            groups.append(W[:, 512 + j_ * 128:512 + (j_ + 1) * 128])
        kr = W[:, 1024:1088]
        groups.append(pad(kr))
        groups.append(pad(kr[:, PERM64]))
        p128 = np.concatenate([PERM64, 64 + PERM64])
        for j_ in range(4):
            bq = W[:, 1088 + j_ * 128:1088 + (j_ + 1) * 128]
            groups.append(bq)
            groups.append(bq[:, p128])
        bk = W[:, 1600:1728]
        groups.append(bk)
        groups.append(bk[:, p128])
        groups.append(W[:, 1728:1856])
        for j_ in range(4):
            groups.append(W[:, 2368 + j_ * 128:2368 + (j_ + 1) * 128])
            groups.append(W[:, 1856 + j_ * 128:1856 + (j_ + 1) * 128])
        wins.append(np.stack([_tile(np.ascontiguousarray(g), 128)[0] for g in groups]))
    sh["win_t"] = np.stack(wins)
    t4 = lambda A: np.ascontiguousarray(A.reshape(4, 128, A.shape[1]).transpose(1, 0, 2))
    ukv = f("mla_w_ukv")
    uq = f("mla_w_uq")
    sh["wk_t"] = np.stack([np.stack([t4(ukv[l][:, h, 0:128]) for h in range(8)]) for l in range(L)])
    sh["wv_t"] = np.stack([np.stack([t4(ukv[l][:, 4 * a:4 * a + 4, 128:256].reshape(512, 512))
                                     for a in range(2)]) for l in range(L)])
    sh["wqn_t"] = np.stack([np.stack([t4(uq[l][:, h, 0:128]) for h in range(8)]) for l in range(L)])
    sh["wqp_t"] = np.stack([np.stack([t4(uq[l][:, h, 128:192]) for h in range(8)]) for l in range(L)])
    sh["wqr_t"] = np.stack([np.stack([t4(uq[l][:, h, 128:192][:, PERM64]) for h in range(8)]) for l in range(L)])
    sh["wout_t"] = np.stack([_tile(f("w_out")[l], 128) for l in range(L)])
    sh["w1_t"] = np.stack([_tile(f("mlp_w1")[l], 128) for l in range(L)])
    sh["w2_t"] = np.stack([np.ascontiguousarray(
        f("mlp_w2")[l].reshape(2, 32, 128, 16, 128).transpose(0, 3, 2, 1, 4)) for l in range(L)])
    return sh


def _prep_core(inp, sh, b, half):
    perm = np.concatenate([np.arange(half * TH, (half + 1) * TH), np.arange((1 - half) * TH, (2 - half) * TH)])
    x = np.asarray(inp["x"], np.float32)[b][perm]
    ctx = np.asarray(inp["ctx"], np.float32)[b]
    xt = np.concatenate([x, ctx], 0).T
    d = {"xT": np.ascontiguousarray(xt.reshape(16, 128, T))}
    cT = np.zeros((128, 17, 2), np.float32)
    cT[:, :16, 0] = _fm(np.asarray(inp["c"], np.float32)[b])
    cT[:, :16, 1] = _fm(np.asarray(inp["c_ctx"], np.float32))
    cT[:, 16, 0] = 1.0 if half == 1 else 0.0
    cT[:, 16, 1] = 1.0 if half == 0 else 0.0
    d["cT"] = cT.reshape(128, 34)
    rt = sh["_rope"]
    d["rope"] = np.ascontiguousarray(np.concatenate([rt[:, :, :TL][:, :, perm], rt[:, :, TL:]], 2))
    base = sh["_cbf"]
    triP = base[:, 128:640]
    triN = base[:, 640:1152]
    allm = np.full((128, 512), NEG, np.float32)
    ms = [allm, triN, triP, allm] if half == 0 else [triP, allm, allm, triN]
    d["cbf"] = np.concatenate([base] + ms, 1).astype(ml_dtypes.bfloat16)
    return d


_CACHE = {}


def kernel(**inputs):
    if "nc" not in _CACHE:
        _CACHE["nc"] = build_program()[0]
    nc = _CACHE["nc"]
    sh = _prep_shared(inputs)
    shared = {k: v for k, v in sh.items() if not k.startswith("_")}
    in_maps = []
    for cid in range(NCORES):
        m = dict(shared)
        m.update(_prep_core(inputs, sh, cid // 2, cid % 2))
        in_maps.append(m)
    res = run_bass_kernel_spmd(nc, in_maps, core_ids=list(range(NCORES)))
    out = np.zeros((4, TL, D), np.float32)
    for cid in range(NCORES):
        b, half = cid // 2, cid % 2
        o = np.asarray(res.results[cid]["outT"], np.float32).reshape(2048, TH)
        out[b, half * TH:(half + 1) * TH, :] = o.T
    return out
```

```python
import numpy as np
import ml_dtypes
from contextlib import ExitStack
import concourse.bass as bass
import concourse.mybir as mybir
from concourse.bass_utils import run_bass_kernel_spmd

F32 = mybir.dt.float32
BF16 = mybir.dt.bfloat16
AF = mybir.ActivationFunctionType
ALU = mybir.AluOpType

D = 2048
TL = 4096
CL = 256
T = TL + CL
DEPTH = 2
EPS = 1e-6
MLA_SCALE = 192 ** -0.5
SWA_SCALE = 64 ** -0.5
NEG = -30000.0
NCORES = 8
TH = 2048
GROUPS = [(i * 512, 512, 0) for i in range(8)] + [(TL, 256, 1)]

O_NMIX, O_NMLP, O_QN, O_KVN, O_CB, O_LNG, O_LNB, O_ON, O_ADAB, O_CW, O_SINK = (
    0, 16, 32, 36, 40, 44, 48, 52, 68, 164, 288)
NS = 296

COMPUTE = ("pe", "act", "dve")
DMAQ = {"sp": 8, "pool": 6}


class Op:
    __slots__ = ("q", "fn", "deps", "signal", "sem", "val", "slot", "key")

    def __init__(self, q, fn):
        self.q = q
        self.fn = fn
        self.deps = {}
        self.signal = False
        self.sem = None
        self.val = 0
        self.slot = None
        self.key = q


class Res:
    __slots__ = ("w", "r", "name")

    def __init__(self, sched, name=""):
        self.w = dict(sched.snapshot)
        self.r = {}
        self.name = name


class Sched:
    def __init__(self, nc):
        self.nc = nc
        self.streams = {q: [] for q in ("pe", "act", "dve", "pool", "sp")}
        self.latest = {}
        self.snapshot = {}
        self.slot_rr = {q: 0 for q in DMAQ}
        self.slot_last = {}

    def barrier(self):
        self.snapshot = dict(self.latest)

    def res(self, name=""):
        return Res(self, name)

    def _add(self, op, reads, writes):
        allk = (op.slot is not None) or (op.q != "pe")
        for r in reads:
            for k, d in r.w.items():
                if allk or k != op.key:
                    op.deps[id(d)] = d
        for r in writes:
            for k, d in r.w.items():
                if allk or k != op.key:
                    op.deps[id(d)] = d
            for k, d in r.r.items():
                if (allk and k != op.key) or (k != op.key):
                    op.deps[id(d)] = d
        for r in reads:
            r.r[op.key] = op
        for r in writes:
            r.w[op.key] = op
        self.latest[op.key] = op
        self.streams[op.q].append(op)
        return op

    def op(self, q, fn, reads=(), writes=()):
        return self._add(Op(q, fn), reads, writes)

    def dma(self, q, out, in_, reads=(), writes=()):
        op = Op(q, lambda e: e.dma_start(out=out, in_=in_))
        n = DMAQ[q]
        s = self.slot_rr[q]
        self.slot_rr[q] = (s + 1) % n
        op.slot = s
        op.key = (q, s)
        prev = self.slot_last.get((q, s))
        if prev is not None:
            op.deps[id(prev)] = prev
        self.slot_last[(q, s)] = op
        op.signal = True
        return self._add(op, reads, writes)

    def finalize(self, sems, block):
        for q, ops in self.streams.items():
            for op in ops:
                for d in op.deps.values():
                    d.signal = True
        cnt = {}
        for q, ops in self.streams.items():
            for op in ops:
                if not op.signal:
                    continue
                if op.slot is not None:
                    key = (q, op.slot)
                    cnt[key] = cnt.get(key, 0) + 16
                    op.sem = sems[key]
                    op.val = cnt[key]
                else:
                    cnt[q] = cnt.get(q, 0) + 1
                    op.sem = sems[q]
                    op.val = cnt[q]
        self.counts = cnt
        engs = {"pe": block.tensor, "act": block.scalar, "dve": block.vector,
                "pool": block.gpsimd, "sp": block.sync}
        for q, deco in engs.items():
            ops = self.streams[q]

            def body(e, ops=ops, q=q):
                waited = {}
                for op in ops:
                    need = {}
                    for d in op.deps.values():
                        sid = id(d.sem)
                        if waited.get(sid, 0) >= d.val:
                            continue
                        if sid not in need or need[sid][1] < d.val:
                            need[sid] = (d.sem, d.val)
                    for sid, (sem, val) in need.items():
                        e.wait_ge(sem, val)
                        waited[sid] = val
                    ins = op.fn(e)
                    if op.signal:
                        ins.then_inc(op.sem, 16 if op.slot is not None else 1)
                if q in DMAQ:
                    for s in range(DMAQ[q]):
                        last = self.slot_last.get((q, s))
                        if last is not None and waited.get(id(last.sem), 0) < last.val:
                            e.wait_ge(last.sem, last.val)

            deco(body)


class Arena:
    def __init__(self, ap2d, nwords):
        self.ap = ap2d
        self.n = nwords
        self.off = 0

    def reset(self):
        self.off = 0

    def f32(self, n):
        a = self.ap[:, self.off:self.off + n]
        self.off += n
        assert self.off <= self.n, ("arena overflow", self.off)
        return a

    def bf16(self, n):
        w = (n + 1) // 2
        a = self.ap[:, self.off:self.off + w].bitcast(BF16)
        self.off += w
        assert self.off <= self.n, ("arena overflow", self.off)
        return a


class Rot:
    def __init__(self, S, aps, name=""):
        self.items = [(a, S.res(name)) for a in aps]
        self.i = 0

    def next(self):
        it = self.items[self.i % len(self.items)]
        self.i += 1
        return it


def build_program(layers=(0, 1), first_in="xT", final=True, debug=()):
    nc = bass.Bass("TRN2", target_bir_lowering=False)
    es = ExitStack()
    L = DEPTH

    def din(name, shape, dt=F32):
        return nc.dram_tensor(name, list(shape), dt, kind="ExternalInput").ap()

    def dscr(name, shape, dt):
        kind = "ExternalOutput" if name in debug else None
        if kind:
            return nc.dram_tensor(name, list(shape), dt, kind=kind).ap()
        return nc.dram_tensor(name, list(shape), dt).ap()

    xT_in = din("xT", [16, 128, T])
    cT_d = din("cT", [128, 34])
    small_d = din("small", [L, 128, NS])
    fnorm_d = din("fnorm", [128, 16])
    rope_d = din("rope", [2, 128, T])
    cbf_d = din("cbf", [128, 3200], BF16)
    ada_d = din("ada_t", [L, 24, 128, 16, 512])
    win_d = din("win_t", [L, 29, 128, 16, 128])
    wk_d = din("wk_t", [L, 8, 128, 4, 128])
    wv_d = din("wv_t", [L, 2, 128, 4, 512])
    wqn_d = din("wqn_t", [L, 8, 128, 4, 128])
    wqp_d = din("wqp_t", [L, 8, 128, 4, 64])
    wqr_d = din("wqr_t", [L, 8, 128, 4, 64])
    wout_d = din("wout_t", [L, 16, 128, 16, 128])
    w1_d = din("w1_t", [L, 64, 128, 16, 128])
    w2_d = din("w2_t", [L, 2, 16, 128, 32, 128])
    outT = nc.dram_tensor("outT", [16, 128, TH], F32, kind="ExternalOutput").ap()

    cq_s = dscr("cq_s", [4, 128, T], BF16)
    ckv_s = dscr("ckv_s", [4, 128, T], BF16)
    kpe_s = dscr("kpe_s", [64, T], BF16)
    qb_s = dscr("qb_s", [4, 128, T], BF16)
    kb_s = dscr("kb_s", [128, T], BF16)
    vb_s = dscr("vb_s", [34, 128, 128], BF16)
    glu_s = dscr("glu_s", [4, 128, T], BF16)
    kn_s = dscr("kn_s", [8, 128, T], BF16)
    va_s = dscr("va_s", [34, 128, 1024], BF16)
    qn_s = dscr("qn_s", [8, 128, T], BF16)
    qpe_s = dscr("qpe_s", [8, 64, T], BF16)
    mix_s = dscr("mix_s", [16, 128, T], F32)
    xB = dscr("xB", [16, 128, T], F32)

    arena_t = es.enter_context(nc.sbuf_tensor("arena", [128, 44544], F32))
    ar = Arena(arena_t[:, :], 44544)
    ones_f = es.enter_context(nc.sbuf_tensor("ones_f", [128, 128], F32))
    ones_b = es.enter_context(nc.sbuf_tensor("ones_b", [128, 128], BF16))
    cbf = es.enter_context(nc.sbuf_tensor("cbf_sb", [128, 3200], BF16))
    small = es.enter_context(nc.sbuf_tensor("small_sb", [128, L, NS], F32))
    fnorm = es.enter_context(nc.sbuf_tensor("fnorm_sb", [128, 16], F32))
    cTs = es.enter_context(nc.sbuf_tensor("cT_sb", [128, 34], F32))
    scT = es.enter_context(nc.sbuf_tensor("scT_sb", [128, 32], F32))
    modT_all = es.enter_context(nc.sbuf_tensor("modT", [128, L * 96, 2], F32))
    dtab_all = es.enter_context(nc.sbuf_tensor("dtab", [128, L * 6, 16, 2], F32))
    sinkx_all = es.enter_context(nc.sbuf_tensor("sinkx", [128, L * 8], F32))
    sinkbc_all = es.enter_context(nc.sbuf_tensor("sinkbc", [64, L * 8, 128], F32))
    PS = [es.enter_context(nc.psum_tensor(f"ps{i}", [128, 512], F32)) for i in range(8)]

    S = Sched(nc)
    sems = {}
    for q in COMPUTE:
        sems[q] = es.enter_context(nc.semaphore(f"sem_{q}"))
    for q, n in DMAQ.items():
        for s in range(n):
            sems[(q, s)] = es.enter_context(nc.semaphore(f"sem_{q}{s}"))

    RPS = [S.res(f"ps{i}") for i in range(8)]
    r_const = S.res("const")
    r_small = S.res("small")
    r_mod_l = [S.res("mod") for _ in range(L)]
    r_dtab_l = [S.res("dtab") for _ in range(L)]
    r_sink_l = [S.res("sink") for _ in range(L)]
    DR = {n: S.res(n) for n in ("xT", "xB", "cq", "ckv", "kpe", "qb", "kb", "vb", "glu", "kn", "va",
                                "qn", "qpe", "mix", "out")}

    ident = cbf[:, 0:128]
    maskP = cbf[:, 128:640]
    maskN = cbf[:, 640:1152]
    maskM = [cbf[:, 1152 + i * 512:1152 + (i + 1) * 512] for i in range(4)]

    def MM(out, lhsT, rhs, start, stop, reads, writes):
        S.op("pe", lambda e: e.matmul(out, lhsT=lhsT, rhs=rhs, start=start, stop=stop), reads, writes)

    def ACT(out, in_, func, reads, writes, bias=None, scale=None, q="act"):
        kw = {}
        if bias is not None:
            kw["bias"] = bias
        if scale is not None:
            kw["scale"] = scale
        S.op(q, lambda e: e.activation(out=out, in_=in_, func=func, **kw), reads, writes)

    def TT(out, in0, in1, op, reads, writes):
        S.op("dve", lambda e: e.tensor_tensor(out=out, in0=in0, in1=in1, op=op), reads, writes)

    def STT(out, in0, scalar, in1, op0, op1, reads, writes):
        S.op("dve", lambda e: e.scalar_tensor_tensor(out=out, in0=in0, scalar=scalar, in1=in1,
                                                      op0=op0, op1=op1), reads, writes)

    def TS(out, in0, s1, s2, op0, op1, reads, writes):
        if op1 is None:
            S.op("dve", lambda e: e.tensor_scalar(out=out, in0=in0, scalar1=s1, scalar2=None, op0=op0),
                 reads, writes)
        else:
            S.op("dve", lambda e: e.tensor_scalar(out=out, in0=in0, scalar1=s1, scalar2=s2, op0=op0,
                                                  op1=op1), reads, writes)

    def RECIP(out, in_, reads, writes):
        S.op("dve", lambda e: e.reciprocal(out=out, in_=in_), reads, writes)

    def COPY(q, out, in_, reads, writes):
        if q == "act":
            S.op("act", lambda e: e.activation(out=out, in_=in_, func=AF.Copy), reads, writes)
        else:
            S.op("dve", lambda e: e.tensor_copy(out=out, in_=in_), reads, writes)

    evac_rr = [0]

    def EVAC(out, in_, reads, writes):
        q = "act" if evac_rr[0] % 2 == 0 else "dve"
        evac_rr[0] += 1
        COPY(q, out, in_, reads, writes)

    def wstream(bufs, dram_list, lookahead):
        st = {"n": 0}

        def get(i):
            while st["n"] < min(len(dram_list), i + lookahead + 1):
                b, rb = bufs[st["n"] % len(bufs)]
                S.dma("pool", b, dram_list[st["n"]], writes=[rb])
                st["n"] += 1
            return bufs[i % len(bufs)]

        return get

    def rstd_from_ps(ps_ap, rps, n, sd_ap, rsd, out_ap, rout, eps_ap):
        ACT(sd_ap, ps_ap, AF.Sqrt, [rps, r_const], [rsd], bias=eps_ap, scale=1.0 / n)
        RECIP(out_ap, sd_ap, [rsd], [rout])

    eps_t = es.enter_context(nc.sbuf_tensor("eps_t", [128, 1], F32))
    S.op("dve", lambda e: e.memset(ones_f[:, :], 1.0), (), [r_const])
    S.op("dve", lambda e: e.memset(ones_b[:, :], 1.0), (), [r_const])
    S.op("dve", lambda e: e.memset(eps_t[:, :], EPS), (), [r_const])
    S.dma("sp", cbf[:, :], cbf_d, writes=[r_const])
    S.dma("sp", small[:, :, :], small_d.rearrange("l p n -> p l n"), writes=[r_small])
    S.dma("sp", fnorm[:, :], fnorm_d, writes=[r_small])
    S.dma("sp", cTs[:, :], cT_d, writes=[r_small])
    ACT(scT[:, :], cTs[:, 0:32], AF.Silu, [r_small], [r_small])
    flagA = cTs[:, 32:33]
    flagB = cTs[:, 33:34]
    eps_ap = eps_t[:, 0:1]

    def mod_steps(l, bufs, CW, pbank, rpbank):
        modT = modT_all[:, l * 96:(l + 1) * 96, :]
        dt = dtab_all[:, l * 6:(l + 1) * 6]
        sx = sinkx_all[:, l * 8:(l + 1) * 8]
        sbc = sinkbc_all[:, l * 8:(l + 1) * 8, :]
        r_mod, r_dt, r_sk = r_mod_l[l], r_dtab_l[l], r_sink_l[l]
        sc3 = scT[:, :].rearrange("p (k r) -> p k r", r=2)
        nst = 12288 // CW
        per = 512 // CW
        nj = CW // 128
        steps = []

        def issue(i):
            wt, rw = bufs[i % 2]
            S.dma("sp", wt, ada_d[l, i // per][:, :, (i % per) * CW:(i % per + 1) * CW], writes=[rw])

        def mk(i):
            def step():
                if i == 0:
                    issue(0)
                if i + 1 < nst:
                    issue(i + 1)
                wt, rw = bufs[i % 2]
                for j in range(nj):
                    for k in range(16):
                        MM(pbank[:, 2 * j:2 * j + 2], wt[:, k, j * 128:(j + 1) * 128], sc3[:, k, :],
                           k == 0, k == 15, [rw, r_small], [rpbank])
                c0 = i * nj
                for r in range(2):
                    src_ = pbank[:, 0:2 * nj].rearrange("p (j r) -> p j r", r=2)[:, :, r]
                    TT(modT[:, c0:c0 + nj, r], src_, small[:, l, O_ADAB + c0:O_ADAB + c0 + nj], ALU.add,
                       [rpbank, r_small], [r_mod])
            return step

        for i in range(nst):
            steps.append(mk(i))

        def derived():
            for r in range(2):
                STT(dt[:, 0, :, r], modT[:, 16:32, r], 1.0, small[:, l, O_NMIX:O_NMIX + 16],
                    ALU.add, ALU.mult, [r_mod, r_small], [r_dt])
                S.op("dve", lambda e, r=r: e.tensor_copy(out=dt[:, 1, :, r], in_=modT[:, 0:16, r]),
                     [r_mod], [r_dt])
                S.op("dve", lambda e, r=r: e.tensor_copy(out=dt[:, 2, :, r], in_=modT[:, 32:48, r]),
                     [r_mod], [r_dt])
                STT(dt[:, 3, :, r], modT[:, 64:80, r], 1.0, small[:, l, O_NMLP:O_NMLP + 16],
                    ALU.add, ALU.mult, [r_mod, r_small], [r_dt])
                S.op("dve", lambda e, r=r: e.tensor_copy(out=dt[:, 4, :, r], in_=modT[:, 48:64, r]),
                     [r_mod], [r_dt])
                S.op("dve", lambda e, r=r: e.tensor_copy(out=dt[:, 5, :, r], in_=modT[:, 80:96, r]),
                     [r_mod], [r_dt])
            ACT(sx[:, :], small[:, l, O_SINK:O_SINK + 8], AF.Exp, [r_small], [r_sk])
            for h in range(8):
                TS(sbc[:, h, :], ones_f[0:64, :], sx[0:64, h:h + 1], None, ALU.mult, None,
                   [r_const, r_sk], [r_sk])

        steps.append(derived)
        return steps

    def stage_mod(l):
        S.barrier()
        ar.reset()
        bufs = [(ar.f32(8192).rearrange("p (k j) -> p k j", j=512), S.res("adaw")) for _ in range(2)]
        for st in mod_steps(l, bufs, 512, PS[0], RPS[0]):
            st()

    def modulate_group(xg, rx, G, r, which, hT, rh, sqrot, tmprot, pss, rpss, rstd, rrstd, sd, rsd):
        for k in range(16):
            sq, rsq = sqrot.next()
            ACT(sq[:, :G], xg[:, k, :], AF.Square, [rx], [rsq])
            MM(pss[:, :G], ones_f[:, :], sq[:, :G], k == 0, k == 15, [rsq, r_const], [rpss])
        rstd_from_ps(pss[:, :G], rpss, D, sd[:, :G], rsd, rstd[:, :G], rrstd, eps_ap)
        for k in range(16):
            tmp, rt = tmprot.next()
            STT(tmp[:, :G], xg[:, k, :], dtab[:, 3 * which, k, r:r + 1], rstd[:, :G], ALU.mult, ALU.mult,
                [rx, rrstd, r_dtab], [rt])
            ACT(hT[:, k, :], tmp[:, :G], AF.Identity, [rt, r_dtab], [rh],
                bias=dtab[:, 3 * which + 1, k, r:r + 1], scale=1.0)

    def stage_in(l, xsrc, rxsrc, groups, modes):
        S.barrier()
        ar.reset()
        bigf = [(ar.f32(8192), S.res("xg")) for _ in range(2)]
        bigh = [(ar.bf16(8192), S.res("hT")) for _ in range(2)]
        wb = [(ar.bf16(2048).rearrange("p (k j) -> p k j", j=128), S.res("w")) for _ in range(3)]
        aqf = ar.f32(2048)
        r_aqf = S.res()
        cqst = ar.bf16(2048)
        r_cqst = S.res()
        ropeo = Rot(S, [ar.bf16(512) for _ in range(2)])
        tmpf = Rot(S, [ar.f32(512) for _ in range(4)])
        sqrot = Rot(S, [ar.f32(512) for _ in range(2)])
        vbst = ar.bf16(512)
        r_vbst = S.res()
        glust = ar.bf16(2048)
        r_glust = S.res()
        rstd = ar.f32(512)
        r_rstd = S.res()
        sd = ar.f32(512)
        r_sd = S.res()
        rstd2 = ar.f32(512)
        r_rstd2 = S.res()
        tabs = [(ar.f32(512), ar.f32(512), S.res("tab")) for _ in range(2)]
        psrot = Rot(S, [None] * 4)
        psi = [0]

        def nextps():
            i = psi[0] % 6
            psi[0] += 1
            return PS[i], RPS[i]

        pss, rpss = PS[7], RPS[7]
        xv = xsrc.rearrange("k p t -> p k t")
        wl = []
        for gi, (t0, G, r) in enumerate(groups):
            cgs = {"full": list(range(29)), "kv": [4, 5, 6, 7, 8, 9, 18, 19, 20],
                   "kvglu": [4, 5, 6, 7, 8, 9, 18, 19, 20] + list(range(21, 29))}[modes[gi]]
            for c in cgs:
                wl.append((gi, c))
        getw = wstream(wb, [win_d[l, c] for (_, c) in wl], 2)
        wi = 0

        def load_x(gi):
            t0, G, r = groups[gi]
            xg2, rx = bigf[gi % 2]
            xg = xg2[:, :16 * G].rearrange("p (k t) -> p k t", t=G)
            S.dma("sp", xg, xv[:, :, t0:t0 + G], reads=[rxsrc], writes=[rx])
            ct, st, rt = tabs[gi % 2]
            S.dma("sp", ct[:, :G], rope_d[0, :, t0:t0 + G], writes=[rt])
            S.dma("sp", st[:, :G], rope_d[1, :, t0:t0 + G], writes=[rt])

        load_x(0)
        for gi, (t0, G, r) in enumerate(groups):
            kvonly = modes[gi] != "full"
            doglu = modes[gi] != "kv"
            if gi + 1 < len(groups):
                load_x(gi + 1)
            xg2, rx = bigf[gi % 2]
            xg = xg2[:, :16 * G].rearrange("p (k t) -> p k t", t=G)
            h2, rh = bigh[gi % 2]
            hT = h2[:, :16 * G].rearrange("p (k t) -> p k t", t=G)
            ct, st, rtab = tabs[gi % 2]
            if gi == 0:
                modulate_group(xg, rx, G, r, 0, hT, rh, sqrot, tmpf, pss, rpss, rstd, r_rstd, sd, r_sd)

            def proj(M):
                nonlocal wi
                wt, rw = getw(wi)
                wi += 1
                pb, rp = nextps()
                for k in range(16):
                    MM(pb[:M, :G], wt[:, k, :M], hT[:, k, :], k == 0, k == 15, [rw, rh], [rp])
                return pb, rp

            def rope(pa, rpa, pbb, rpb, M, out_ap, rout):
                t1, r1 = tmpf.next()
                t2, r2 = tmpf.next()
                TT(t1[:M, :G], pbb[:M, :G], st[:M, :G], ALU.mult, [rpb, rtab], [r1])
                TT(t2[:M, :G], pa[:M, :G], ct[:M, :G], ALU.mult, [rpa, rtab], [r2])
                TT(out_ap, t1[:M, :G], t2[:M, :G], ALU.add, [r1, r2], [rout])

            def latent_norm(norm_off, dst, rdst):
                aq3 = aqf[:, :4 * G].rearrange("p (j t) -> p j t", t=G)
                for j in range(4):
                    pb, rp = proj(128)
                    COPY("act", aq3[:, j, :], pb[:, :G], [rp], [r_aqf])
                for j in range(4):
                    sq, rsq = sqrot.next()
                    ACT(sq[:, :G], aq3[:, j, :], AF.Square, [r_aqf], [rsq])
                    MM(pss[:, :G], ones_f[:, :], sq[:, :G], j == 0, j == 3, [rsq, r_const], [rpss])
                rstd_from_ps(pss[:, :G], rpss, 512, sd[:, :G], r_sd, rstd2[:, :G], r_rstd2, eps_ap)
                cq3 = cqst[:, :4 * G].rearrange("p (j t) -> p j t", t=G)
                for j in range(4):
                    STT(cq3[:, j, :], aq3[:, j, :], small[:, l, norm_off + j:norm_off + j + 1],
                        rstd2[:, :G], ALU.mult, ALU.mult, [r_aqf, r_rstd2, r_small], [r_cqst])
                S.dma("sp", dst.rearrange("j p t -> p j t")[:, :, t0:t0 + G], cq3, reads=[r_cqst],
                      writes=[rdst])

            if not kvonly:
                latent_norm(O_QN, cq_s, DR["cq"])
            latent_norm(O_KVN, ckv_s, DR["ckv"])
            pa, rpa = proj(64)
            pbb, rpb = proj(64)
            o, ro = ropeo.next()
            rope(pa, rpa, pbb, rpb, 64, o[:64, :G], ro)
            S.dma("sp", kpe_s[:, t0:t0 + G], o[:64, :G], reads=[ro], writes=[DR["kpe"]])
            if gi + 1 < len(groups):
                t0n, Gn, rn = groups[gi + 1]
                xn2, rxn = bigf[(gi + 1) % 2]
                hn2, rhn = bigh[(gi + 1) % 2]
                modulate_group(xn2[:, :16 * Gn].rearrange("p (k t) -> p k t", t=Gn), rxn, Gn, rn, 0,
                               hn2[:, :16 * Gn].rearrange("p (k t) -> p k t", t=Gn), rhn, sqrot, tmpf,
                               pss, rpss, rstd, r_rstd, sd, r_sd)
            if not kvonly:
                for j in range(4):
                    pa, rpa = proj(128)
                    pbb, rpb = proj(128)
                    o, ro = ropeo.next()
                    rope(pa, rpa, pbb, rpb, 128, o[:, :G], ro)
                    S.dma("sp", qb_s[j, :, t0:t0 + G], o[:, :G], reads=[ro], writes=[DR["qb"]])
            pa, rpa = proj(128)
            pbb, rpb = proj(128)
            o, ro = ropeo.next()
            rope(pa, rpa, pbb, rpb, 128, o[:, :G], ro)
            S.dma("sp", kb_s[:, t0:t0 + G], o[:, :G], reads=[ro], writes=[DR["kb"]])
            wt, rw = getw(wi)
            wi += 1
            pb, rp = nextps()
            ntt = G // 128
            for tt in range(ntt):
                for k in range(16):
                    MM(pb[:, tt * 128:(tt + 1) * 128], hT[:, k, tt * 128:(tt + 1) * 128], wt[:, k, :],
                       k == 0, k == 15, [rw, rh], [rp])
            COPY("act", vbst[:, :G], pb[:, :G], [rp], [r_vbst])
            S.dma("sp", vb_s.rearrange("c p d -> p c d")[:, t0 // 128:t0 // 128 + ntt, :],
                  vbst[:, :G].rearrange("p (c d) -> p c d", d=128), reads=[r_vbst], writes=[DR["vb"]])
            if doglu:
                gl3 = glust[:, :4 * G].rearrange("p (j t) -> p j t", t=G)
                for j in range(4):
                    pg, rpg = proj(128)
                    sg, rsg = tmpf.next()
                    ACT(sg[:, :G], pg[:, :G], AF.Sigmoid, [rpg], [rsg])
                    pa, rpa = proj(128)
                    TT(gl3[:, j, :], pa[:, :G], sg[:, :G], ALU.mult, [rpa, rsg], [r_glust])
                S.dma("sp", glu_s.rearrange("j p t -> p j t")[:, :, t0:t0 + G], gl3, reads=[r_glust],
                      writes=[DR["glu"]])

    def stage_kvup(l):
        S.barrier()
        ar.reset()
        wk = ar.bf16(8 * 512).rearrange("p (h r d) -> p h r d", h=8, r=4)
        wv = ar.bf16(2 * 2048).rearrange("p (a r d) -> p a r d", a=2, r=4)
        r_w = S.res()
        S.dma("pool", wk, wk_d[l].rearrange("h p r d -> p h r d"), writes=[r_w])
        S.dma("pool", wv, wv_d[l].rearrange("a p r d -> p a r d"), writes=[r_w])
        ck = [(ar.bf16(2048), S.res()) for _ in range(2)]
        kst = [(ar.bf16(8 * 512), S.res()) for _ in range(2)]
        vst = [(ar.bf16(4 * 1024), S.res()) for _ in range(2)]
        psi = [0]

        def nextps():
            i = psi[0] % 8
            psi[0] += 1
            return PS[i], RPS[i]

        def load(gi):
            t0, G, r = GROUPS[gi]
            c2, rc = ck[gi % 2]
            S.dma("sp", c2[:, :4 * G].rearrange("p (j t) -> p j t", t=G),
                  ckv_s.rearrange("j p t -> p j t")[:, :, t0:t0 + G], reads=[DR["ckv"]], writes=[rc])

        load(0)
        for gi, (t0, G, r) in enumerate(GROUPS):
            if gi + 1 < len(GROUPS):
                load(gi + 1)
            c2, rc = ck[gi % 2]
            c3 = c2[:, :4 * G].rearrange("p (j t) -> p j t", t=G)
            k2, rk = kst[gi % 2]
            k3 = k2[:, :8 * G].rearrange("p (h t) -> p h t", t=G)
            for h in range(8):
                pb, rp = nextps()
                for j in range(4):
                    MM(pb[:, :G], wk[:, h, j, :], c3[:, j, :], j == 0, j == 3, [r_w, rc], [rp])
                EVAC(k3[:, h, :], pb[:, :G], [rp], [rk])
            S.dma("sp", kn_s.rearrange("h p t -> p h t")[:, :, t0:t0 + G], k3, reads=[rk], writes=[DR["kn"]])
            ntt = G // 128
            v2, rv = vst[gi % 2]
            v3 = v2[:, :ntt * 1024].rearrange("p (c d) -> p c d", d=1024)
            for tt in range(ntt):
                for a in range(2):
                    pb, rp = nextps()
                    for j in range(4):
                        MM(pb[:, :], c3[:, j, tt * 128:(tt + 1) * 128], wv[:, a, j, :], j == 0, j == 3,
                           [r_w, rc], [rp])
                    EVAC(v3[:, tt, a * 512:(a + 1) * 512], pb[:, :], [rp], [rv])
            S.dma("sp", va_s.rearrange("c p d -> p c d")[:, t0 // 128:t0 // 128 + ntt, :], v3, reads=[rv],
                  writes=[DR["va"]])

    def stage_qup(l, groups):
        S.barrier()
        ar.reset()
        wqn = ar.bf16(8 * 512).rearrange("p (h r d) -> p h r d", h=8, r=4)
        wqp = ar.bf16(8 * 256).rearrange("p (h r d) -> p h r d", h=8, r=4)
        wqr = ar.bf16(8 * 256).rearrange("p (h r d) -> p h r d", h=8, r=4)
        r_w = S.res()
        S.dma("pool", wqn, wqn_d[l].rearrange("h p r d -> p h r d"), writes=[r_w])
        S.dma("pool", wqp, wqp_d[l].rearrange("h p r d -> p h r d"), writes=[r_w])
        S.dma("pool", wqr, wqr_d[l].rearrange("h p r d -> p h r d"), writes=[r_w])
        ck = [(ar.bf16(2048), S.res()) for _ in range(2)]
        qst = [(ar.bf16(8 * 512), S.res()) for _ in range(2)]
        pst = [(ar.bf16(8 * 512), S.res()) for _ in range(2)]
        tabs = [(ar.f32(512), ar.f32(512), S.res("tab")) for _ in range(2)]
        tmpf = Rot(S, [ar.f32(512) for _ in range(4)])
        psi = [0]

        def nextps():
            i = psi[0] % 8
            psi[0] += 1
            return PS[i], RPS[i]

        def load(gi):
            t0, G, r = groups[gi]
            c2, rc = ck[gi % 2]
            S.dma("sp", c2[:, :4 * G].rearrange("p (j t) -> p j t", t=G),
                  cq_s.rearrange("j p t -> p j t")[:, :, t0:t0 + G], reads=[DR["cq"]], writes=[rc])
            ct, st, rt = tabs[gi % 2]
            S.dma("sp", ct[:64, :G], rope_d[0, 0:64, t0:t0 + G], writes=[rt])
            S.dma("sp", st[:64, :G], rope_d[1, 0:64, t0:t0 + G], writes=[rt])

        load(0)
        for gi, (t0, G, r) in enumerate(groups):
            if gi + 1 < len(groups):
                load(gi + 1)
            c2, rc = ck[gi % 2]
            c3 = c2[:, :4 * G].rearrange("p (j t) -> p j t", t=G)
            ct, st, rtab = tabs[gi % 2]
            q2, rq = qst[gi % 2]
            q3 = q2[:, :8 * G].rearrange("p (h t) -> p h t", t=G)
            p2, rpp = pst[gi % 2]
            p3 = p2[:, :8 * G].rearrange("p (h t) -> p h t", t=G)
            for h in range(8):
                pb, rp = nextps()
                for j in range(4):
                    MM(pb[:, :G], wqn[:, h, j, :], c3[:, j, :], j == 0, j == 3, [r_w, rc], [rp])
                EVAC(q3[:, h, :], pb[:, :G], [rp], [rq])
                pa, rpa = nextps()
                for j in range(4):
                    MM(pa[:64, :G], wqp[:, h, j, :], c3[:, j, :], j == 0, j == 3, [r_w, rc], [rpa])
                pbb, rpb = nextps()
                for j in range(4):
                    MM(pbb[:64, :G], wqr[:, h, j, :], c3[:, j, :], j == 0, j == 3, [r_w, rc], [rpb])
                t1, r1 = tmpf.next()
                t2, r2 = tmpf.next()
                TT(t1[:64, :G], pbb[:64, :G], st[:64, :G], ALU.mult, [rpb, rtab], [r1])
                TT(t2[:64, :G], pa[:64, :G], ct[:64, :G], ALU.mult, [rpa, rtab], [r2])
                TT(p3[:64, h, :], t1[:64, :G], t2[:64, :G], ALU.add, [r1, r2], [rpp])
            S.dma("sp", qn_s.rearrange("h p t -> p h t")[:, :, t0:t0 + G], q3, reads=[rq], writes=[DR["qn"]])
            S.dma("sp", qpe_s.rearrange("h p t -> p h t")[:, :, t0:t0 + G], p3[:64], reads=[rpp],
                  writes=[DR["qpe"]])

    def stage_atta(l, groups, extra_mod=None):
        S.barrier()
        ar.reset()
        extra = []
        if extra_mod is not None:
            ebufs = [(ar.f32(4096).rearrange("p (k j) -> p k j", j=256), S.res("adaw")) for _ in range(2)]
            extra = mod_steps(extra_mod, ebufs, 256, PS[3], RPS[3])
        kpe = ar.bf16(T)
        r_kpe = S.res()
        S.dma("sp", kpe[:64, :], kpe_s, reads=[DR["kpe"]], writes=[r_kpe])
        kn = [(ar.bf16(T), S.res()) for _ in range(2)]
        vh = [(ar.bf16(34 * 128).rearrange("p (c d) -> p c d", d=128), S.res()) for _ in range(2)]
        qn = [(ar.bf16(512), ar.bf16(512), S.res()) for _ in range(3)]
        pt = Rot(S, [ar.bf16(512) for _ in range(4)])
        ost = Rot(S, [ar.f32(512) for _ in range(2)])
        rden = Rot(S, [ar.f32(512) for _ in range(2)])
        work = [(h, gi) for h in range(8) for gi in range(len(groups))]

        def load_head(h):
            k2, rk = kn[h % 2]
            S.dma("sp", k2[:, :], kn_s[h], reads=[DR["kn"]], writes=[rk])
            v3, rv = vh[h % 2]
            S.dma("sp", v3, va_s.rearrange("c p d -> p c d")[:, :, h * 128:(h + 1) * 128],
                  reads=[DR["va"]], writes=[rv])

        def load_q(wi_):
            h, gi = work[wi_]
            t0, G, r = groups[gi]
            qa, qp, rq = qn[wi_ % 3]
            S.dma("sp", qa[:, :G], qn_s[h, :, t0:t0 + G], reads=[DR["qn"]], writes=[rq])
            S.dma("sp", qp[:64, :G], qpe_s[h, :, t0:t0 + G], reads=[DR["qpe"]], writes=[rq])

        load_head(0)
        load_q(0)
        load_q(1)
        sps = [0]
        pend = []

        def flush(keep):
            while len(pend) > keep:
                pend.pop(0)()

        for wi_, (h, gi) in enumerate(work):
            t0, G, r = groups[gi]
            if gi == 1 and h + 1 < 8:
                load_head(h + 1)
            if wi_ + 2 < len(work):
                load_q(wi_ + 2)
            k2, rk = kn[h % 2]
            v3, rv = vh[h % 2]
            qa, qp, rq = qn[wi_ % 3]
            chunks = list(range(34)) if r == 0 else [32, 33]
            po, rpo = PS[4 + (wi_ % 2) * 2], RPS[4 + (wi_ % 2) * 2]
            pd, rpd = PS[5 + (wi_ % 2) * 2], RPS[5 + (wi_ % 2) * 2]
            for ci, c in enumerate(chunks):
                nsb = 3 if extra_mod is not None else 4
                psb, rps_ = PS[sps[0] % nsb], RPS[sps[0] % nsb]
                sps[0] += 1
                MM(psb[:, :G], k2[:, c * 128:(c + 1) * 128], qa[:, :G], True, False, [rk, rq], [rps_])
                MM(psb[:, :G], kpe[:64, c * 128:(c + 1) * 128], qp[:64, :G], False, True, [r_kpe, rq], [rps_])
                p, rp = pt.next()
                ACT(p[:, :G], psb[:, :G], AF.Exp, [rps_], [rp], scale=MLA_SCALE)

                def back(ci=ci, c=c, p=p, rp=rp, po=po, rpo=rpo, pd=pd, rpd=rpd, v3=v3, rv=rv, G=G,
                         n=len(chunks), h=h, t0=t0):
                    MM(po[:, :G], v3[:, c, :], p[:, :G], ci == 0, ci == n - 1, [rv, rp], [rpo])
                    MM(pd[:, :G], ones_b[:, :], p[:, :G], ci == 0, ci == n - 1, [r_const, rp], [rpd])
                    if ci == n - 1:
                        rd, rrd = rden.next()
                        RECIP(rd[:, :G], pd[:, :G], [rpd], [rrd])
                        o, ro = ost.next()
                        TT(o[:, :G], po[:, :G], rd[:, :G], ALU.mult, [rpo, rrd], [ro])
                        S.dma("sp", mix_s[h, :, t0:t0 + G], o[:, :G], reads=[ro], writes=[DR["mix"]])

                pend.append(back)
                flush(2)
            if extra and wi_ >= 1:
                extra.pop(0)()
        flush(0)
        while extra:
            extra.pop(0)()

    def stage_attb(l, groups):
        S.barrier()
        ar.reset()
        kb = ar.bf16(2 * T).rearrange("p (h t) -> p h t", h=2)
        vb = ar.bf16(34 * 128).rearrange("p (c d) -> p c d", d=128)
        r_kv = S.res()
        S.dma("sp", kb[:64], kb_s.rearrange("(h p) t -> p h t", p=64), reads=[DR["kb"]], writes=[r_kv])
        S.dma("sp", vb, vb_s.rearrange("c p d -> p c d"), reads=[DR["vb"]], writes=[r_kv])
        qg = [(ar.bf16(8 * 512).rearrange("p (h t) -> p h t", h=8), S.res()) for _ in range(2)]
        pt = Rot(S, [ar.bf16(512) for _ in range(4)])
        ost = Rot(S, [ar.f32(512) for _ in range(2)])
        den = Rot(S, [ar.f32(512) for _ in range(2)])
        qbv = qb_s.rearrange("j (a p) t -> p (j a) t", p=64)
        mixb = mix_s.rearrange("k (a p) t -> p (k a) t", p=64)

        def load(gi):
            t0, G, r = groups[gi]
            q3, rq = qg[gi % 2]
            S.dma("sp", q3[:64, :, :G], qbv[:, :, t0:t0 + G], reads=[DR["qb"]], writes=[rq])

        load(0)
        it = 0
        sps = [0]
        pend = []
        for gi, (t0, G, r) in enumerate(groups):
            if gi + 1 < len(groups):
                load(gi + 1)
            q3, rq = qg[gi % 2]
            for nn in range(G // 128):
                n = t0 // 128 + nn
                if r == 0:
                    prv = (31, maskM[0]) if n == 0 else ((15, maskM[2]) if n == 16 else (n - 1, maskP))
                    nxt = (16, maskM[1]) if n == 15 else ((0, maskM[3]) if n == 31 else (n + 1, maskN))
                    chunks = [prv, (n, None), nxt, (32, None), (33, None)]
                else:
                    chunks = [(32, None), (33, None)]
                for kh in range(2):
                    po, rpo = PS[4 + (it % 2) * 2], RPS[4 + (it % 2) * 2]
                    pd, rpd = PS[5 + (it % 2) * 2], RPS[5 + (it % 2) * 2]
                    it += 1
                    rhs = q3[:64, 4 * kh:4 * kh + 4, nn * 128:(nn + 1) * 128]
                    for ci, (c, mask) in enumerate(chunks):
                        psb, rps_ = PS[sps[0] % 4], RPS[sps[0] % 4]
                        sps[0] += 1
                        ps3 = psb[:, :].rearrange("p (g q) -> p g q", g=4)
                        MM(ps3, kb[:64, kh, c * 128:(c + 1) * 128], rhs, True, mask is None, [r_kv, rq], [rps_])
                        if mask is not None:
                            MM(ps3, ident, mask.rearrange("p (g q) -> p g q", g=4), False, True,
                               [r_const], [rps_])
                        p, rp = pt.next()
                        ACT(p[:, :], psb[:, :], AF.Exp, [rps_], [rp], scale=SWA_SCALE)

                        def back(ci=ci, c=c, p=p, rp=rp, po=po, rpo=rpo, pd=pd, rpd=rpd, kh=kh, n=n,
                                 nch=len(chunks)):
                            last = ci == nch - 1
                            MM(po[:64, :], vb[:, c, kh * 64:(kh + 1) * 64], p[:, :], ci == 0, last,
                               [r_kv, rp], [rpo])
                            MM(pd[:64, :], ones_b[:, 0:64], p[:, :], ci == 0, last, [r_const, rp], [rpd])
                            if last:
                                dn, rdn = den.next()
                                TT(dn[:64, :], pd[:64, :],
                                   sinkbc[:, 4 * kh:4 * kh + 4, :].rearrange("p g q -> p (g q)"), ALU.add,
                                   [rpd, r_sink], [rdn])
                                RECIP(dn[:64, :], dn[:64, :], [rdn], [rdn])
                                o, ro = ost.next()
                                TT(o[:64, :], po[:64, :], dn[:64, :], ALU.mult, [rpo, rdn], [ro])
                                S.dma("sp", mixb[:, 16 + 4 * kh:16 + 4 * kh + 4, n * 128:(n + 1) * 128],
                                      o[:64, :].rearrange("p (g q) -> p g q", g=4), reads=[ro],
                                      writes=[DR["mix"]])

                        pend.append(back)
                        while len(pend) > 2:
                            pend.pop(0)()
        while pend:
            pend.pop(0)()

    def stage_conv(l, groups):
        S.barrier()
        ar.reset()
        gin = [(ar.bf16(4 * 544).rearrange("p (c t) -> p c t", c=4), S.res()) for _ in range(2)]
        dg = ar.bf16(124 * 128).rearrange("p (i j) -> p i j", j=128)
        r_dg = S.res()
        for i in range(124):
            if i % 2 == 0:
                TS(dg[:, i, :], ident, small[:, l, O_CW + i:O_CW + i + 1], None, ALU.mult, None,
                   [r_const, r_small], [r_dg])
            else:
                ACT(dg[:, i, :], ident, AF.Copy, [r_const, r_small], [r_dg],
                    scale=small[:, l, O_CW + i:O_CW + i + 1])
        hb = [(ar.f32(512), S.res()) for _ in range(4)]
        sqrot = Rot(S, [ar.f32(512) for _ in range(2)])
        mean = ar.f32(512)
        r_mean = S.res()
        msq = ar.f32(512)
        r_msq = S.res()
        var = ar.f32(512)
        r_var = S.res()
        rstd = ar.f32(512)
        r_rstd = S.res()
        tmpf = Rot(S, [ar.f32(512) for _ in range(2)])
        ost = [(ar.f32(2048).rearrange("p (c t) -> p c t", c=4), S.res()) for _ in range(2)]
        gv = glu_s.rearrange("j p t -> p j t")
        cps = [0]

        def load(gi):
            t0, G, r = groups[gi]
            g3, rg = gin[gi % 2]
            if r == 1:
                S.op("dve", lambda e, g3=g3: e.memset(g3[:, :, 0:15], 0.0), (), [rg])
                S.op("dve", lambda e, g3=g3, G=G: e.memset(g3[:, :, G + 15:G + 30], 0.0), (), [rg])
                S.dma("sp", g3[:, :, 15:15 + G], gv[:, :, t0:t0 + G], reads=[DR["glu"]], writes=[rg])
                return
            gidx = t0 // 512
            a, b = t0 - 15, t0 + G + 15
            if gidx == 0:
                S.dma("sp", g3[:, :, 0:15], gv[:, :, TL - 15:TL], reads=[DR["glu"]], writes=[rg])
                a = 0
            if gidx == 7:
                S.dma("sp", g3[:, :, G + 15:G + 30], gv[:, :, 0:15], reads=[DR["glu"]], writes=[rg])
                b = TL
            S.dma("sp", g3[:, :, a - (t0 - 15):b - (t0 - 15)], gv[:, :, a:b], reads=[DR["glu"]], writes=[rg])
            if gidx in (0, 4):
                fl = flagA if gidx == 0 else flagB
                TS(g3[:, :, 0:15], g3[:, :, 0:15], fl, None, ALU.mult, None, [rg, r_small], [rg])
            if gidx in (3, 7):
                fl = flagA if gidx == 7 else flagB
                TS(g3[:, :, G + 15:G + 30], g3[:, :, G + 15:G + 30], fl, None, ALU.mult, None,
                   [rg, r_small], [rg])

        load(0)
        cw = O_CW
        for gi, (t0, G, r) in enumerate(groups):
            if gi + 1 < len(groups):
                load(gi + 1)
            g3, rg = gin[gi % 2]
            pm, rpm = PS[(gi % 2) * 2], RPS[(gi % 2) * 2]
            pv, rpv = PS[(gi % 2) * 2 + 1], RPS[(gi % 2) * 2 + 1]
            for c in range(4):
                pc, rpc = PS[4 + cps[0] % 4], RPS[4 + cps[0] % 4]
                cps[0] += 1
                for k in range(31):
                    MM(pc[:, :G], dg[:, c * 31 + k, :], g3[:, c, k:k + G], k == 0, k == 30, [r_dg, rg], [rpc])
                h_, rh_ = hb[c]
                ACT(h_[:, :G], pc[:, :G], AF.Identity, [rpc, r_small], [rh_],
                    bias=small[:, l, O_CB + c:O_CB + c + 1], scale=1.0)
                MM(pm[:, :G], ones_f[:, :], h_[:, :G], c == 0, c == 3, [rh_, r_const], [rpm])
                sq, rsq = sqrot.next()
                ACT(sq[:, :G], h_[:, :G], AF.Square, [rh_], [rsq])
                MM(pv[:, :G], ones_f[:, :], sq[:, :G], c == 0, c == 3, [rsq, r_const], [rpv])
            S.op("act", lambda e, pm=pm, G=G: e.activation(out=mean[:, :G], in_=pm[:, :G], func=AF.Copy,
                                                          scale=1.0 / 512), [rpm], [r_mean])
            TT(msq[:, :G], mean[:, :G], mean[:, :G], ALU.mult, [r_mean], [r_msq])
            STT(var[:, :G], pv[:, :G], 1.0 / 512, msq[:, :G], ALU.mult, ALU.subtract, [rpv, r_msq], [r_var])
            ACT(var[:, :G], var[:, :G], AF.Sqrt, [r_var, r_const], [r_var], bias=eps_ap, scale=1.0)
            RECIP(rstd[:, :G], var[:, :G], [r_var], [r_rstd])
            o3, ro = ost[gi % 2]
            for c in range(4):
                h_, rh_ = hb[c]
                t1, r1 = tmpf.next()
                TT(t1[:, :G], h_[:, :G], mean[:, :G], ALU.subtract, [rh_, r_mean], [r1])
                STT(t1[:, :G], t1[:, :G], small[:, l, O_LNG + c:O_LNG + c + 1], rstd[:, :G], ALU.mult, ALU.mult,
                    [r1, r_rstd, r_small], [r1])
                ACT(o3[:, c, :G], t1[:, :G], AF.Silu, [r1, r_small], [ro],
                    bias=small[:, l, O_LNB + c:O_LNB + c + 1], scale=1.0)
            S.dma("sp", mix_s.rearrange("k p t -> p k t")[:, 12:16, t0:t0 + G], o3[:, :, :G], reads=[ro],
                  writes=[DR["mix"]])

    def stage_out_mlp(l, groups, xsrc, rxsrc, xdst, rxdst, final_norm):
        S.barrier()
        ar.reset()
        bigf = [(ar.f32(8192), S.res("bigf")) for _ in range(2)]
        bigh = (ar.bf16(8192), S.res("bigh"))
        ut = (ar.bf16(32 * 512), S.res("ut"))
        wb = [(ar.bf16(2048).rearrange("p (k j) -> p k j", j=128), S.res("w")) for _ in range(4)]
        w2b = [(ar.bf16(4096).rearrange("p (k j) -> p k j", j=128), S.res("w2")) for _ in range(3)]
        sqrot = Rot(S, [ar.f32(512) for _ in range(2)])
        tmpf = Rot(S, [ar.f32(512) for _ in range(2)])
        rs = [(ar.f32(512), S.res()) for _ in range(3)]
        sd = ar.f32(512)
        r_sd = S.res()
        rstd = ar.f32(512)
        r_rstd = S.res()
        psi = [0]

        def nextps():
            i = psi[0] % 4
            psi[0] += 1
            return PS[i], RPS[i]

        xv = xsrc.rearrange("k p t -> p k t")
        mv = mix_s.rearrange("k p t -> p k t")
        wl = []
        w2l = []
        for gi in range(len(groups)):
            wl += [wout_d[l, d] for d in range(16)]
            for hf in range(2):
                wl += [w1_d[l, hf * 32 + f] for f in range(32)]
                w2l += [w2_d[l, hf, d] for d in range(16)]
        getw = wstream(wb, wl, 3)
        getw2 = wstream(w2b, w2l, 2)
        wi = 0
        w2i = 0

        def load_mix(gi):
            t0, G, r = groups[gi]
            m2, rm = bigf[0]
            S.dma("sp", m2[:, :16 * G].rearrange("p (k t) -> p k t", t=G), mv[:, :, t0:t0 + G],
                  reads=[DR["mix"]], writes=[rm])

        def load_x(gi):
            t0, G, r = groups[gi]
            x2, rx = bigf[1]
            S.dma("sp", x2[:, :16 * G].rearrange("p (k t) -> p k t", t=G), xv[:, :, t0:t0 + G],
                  reads=[rxsrc], writes=[rx])

        load_mix(0)
        for gi, (t0, G, r) in enumerate(groups):
            load_x(gi)
            m2, rm = bigf[0]
            mix = m2[:, :16 * G].rearrange("p (k t) -> p k t", t=G)
            x2, rx = bigf[1]
            xg = x2[:, :16 * G].rearrange("p (k t) -> p k t", t=G)
            h2, rh = bigh
            hT = h2[:, :16 * G].rearrange("p (k t) -> p k t", t=G)
            segs = [(0, 8, 1024), (8, 12, 512), (12, 16, 512)]
            for si, (k0, k1, n) in enumerate(segs):
                pb, rp = PS[4 + si], RPS[4 + si]
                for k in range(k0, k1):
                    sq, rsq = sqrot.next()
                    ACT(sq[:, :G], mix[:, k, :], AF.Square, [rm], [rsq])
                    MM(pb[:, :G], ones_f[:, :], sq[:, :G], k == k0, k == k1 - 1, [rsq, r_const], [rp])
                rstd_from_ps(pb[:, :G], rp, n, sd[:, :G], r_sd, rs[si][0][:, :G], rs[si][1], eps_ap)
            for si, (k0, k1, n) in enumerate(segs):
                for k in range(k0, k1):
                    STT(hT[:, k, :], mix[:, k, :], small[:, l, O_ON + k:O_ON + k + 1], rs[si][0][:, :G],
                        ALU.mult, ALU.mult, [rm, rs[si][1], r_small], [rh])
            if gi + 1 < len(groups):
                load_mix(gi + 1)
            for d in range(16):
                wt, rw = getw(wi)
                wi += 1
                pb, rp = nextps()
                for k in range(16):
                    MM(pb[:, :G], wt[:, k, :], hT[:, k, :], k == 0, k == 15, [rw, rh], [rp])
                STT(xg[:, d, :], pb[:, :G], dtab[:, 2, d, r:r + 1], xg[:, d, :], ALU.mult, ALU.add,
                    [rp, rx, r_dtab], [rx])
            modulate_group(xg, rx, G, r, 1, hT, rh, sqrot, tmpf, PS[7], RPS[7], rstd, r_rstd, sd, r_sd)
            u2, ru = ut
            u3 = u2[:, :32 * G].rearrange("p (f t) -> p f t", t=G)
            for hf in range(2):
                for f in range(32):
                    wt, rw = getw(wi)
                    wi += 1
                    pb, rp = nextps()
                    for k in range(16):
                        MM(pb[:, :G], wt[:, k, :], hT[:, k, :], k == 0, k == 15, [rw, rh], [rp])
                    sq, rsq = sqrot.next()
                    ACT(sq[:, :G], pb[:, :G], AF.Square, [rp], [rsq])
                    STT(u3[:, f, :], pb[:, :G], 0.0, sq[:, :G], ALU.is_gt, ALU.mult, [rp, rsq], [ru])
                for d in range(16):
                    w2t, rw2 = getw2(w2i)
                    w2i += 1
                    pb, rp = nextps()
                    for f in range(32):
                        MM(pb[:, :G], w2t[:, f, :], u3[:, f, :], f == 0, f == 31, [rw2, ru], [rp])
                    STT(xg[:, d, :], pb[:, :G], dtab[:, 5, d, r:r + 1], xg[:, d, :], ALU.mult, ALU.add,
                        [rp, rx, r_dtab], [rx])
            if final_norm:
                for k in range(16):
                    sq, rsq = sqrot.next()
                    ACT(sq[:, :G], xg[:, k, :], AF.Square, [rx], [rsq])
                    MM(PS[7][:, :G], ones_f[:, :], sq[:, :G], k == 0, k == 15, [rsq, r_const], [RPS[7]])
                rstd_from_ps(PS[7][:, :G], RPS[7], D, sd[:, :G], r_sd, rstd[:, :G], r_rstd, eps_ap)
                for k in range(16):
                    STT(xg[:, k, :], xg[:, k, :], fnorm[:, k:k + 1], rstd[:, :G], ALU.mult, ALU.mult,
                        [rx, r_rstd, r_small], [rx])
            S.dma("sp", xdst.rearrange("k p t -> p k t")[:, :, t0:t0 + G], xg, reads=[rx], writes=[rxdst])

    for l in layers:
        lastl = (l == DEPTH - 1)
        xsrc, rxsrc = (xT_in, DR["xT"]) if l == layers[0] else (xB, DR["xB"])
        xdst, rxdst = (outT, DR["out"]) if (lastl and final) else (xB, DR["xB"])
        qgroups = GROUPS[:4] if lastl else GROUPS
        modes = (["full"] * 4 + ["kvglu", "kv", "kv", "kvglu", "kv"]) if lastl else ["full"] * 9
        dtab = dtab_all[:, l * 6:(l + 1) * 6]
        sinkbc = sinkbc_all[:, l * 8:(l + 1) * 8, :]
        r_dtab = r_dtab_l[l]
        r_sink = r_sink_l[l]
        if l == layers[0]:
            stage_mod(l)
        stage_in(l, xsrc, rxsrc, GROUPS, modes)
        stage_kvup(l)
        stage_qup(l, qgroups)
        stage_atta(l, qgroups, extra_mod=(l + 1) if (l + 1) in layers else None)
        stage_attb(l, qgroups)
        stage_conv(l, qgroups)
        stage_out_mlp(l, qgroups, xsrc, rxsrc, xdst, rxdst, lastl and final)

    blk = es.enter_context(nc.Block())
    S.finalize(sems, blk)
    es.close()
    return nc, S


def _fm(v):
    return np.ascontiguousarray(np.asarray(v, np.float32).reshape(-1, 128).T)


def _tile(W, nc_):
    K, N = W.shape
    return np.ascontiguousarray(W.reshape(K // 128, 128, N // nc_, nc_).transpose(2, 1, 0, 3))


PERM64 = np.concatenate([np.arange(16, 32), np.arange(0, 16), np.arange(48, 64), np.arange(32, 48)])


def _rope_tables():
    rows = TL // 64
    row = np.repeat(np.arange(rows, dtype=np.float32), 64)
    col = np.tile(np.arange(64, dtype=np.float32), rows)
    inv = (np.float32(10000.0) ** (-(np.arange(16, dtype=np.float32) / np.float32(16)))).astype(np.float32)
    C = np.ones((64, T), np.float32)
    Sg = np.zeros((64, T), np.float32)
    for d in range(64):
        blk, i = d // 16, d % 16
        pos = row if blk < 2 else col
        ang = (pos * inv[i]).astype(np.float32).astype(np.float64)
        C[d, :TL] = np.cos(ang)
        Sg[d, :TL] = np.sin(ang) * (-1.0 if blk % 2 == 0 else 1.0)
    tab = np.zeros((2, 128, T), np.float32)
    tab[0, :64] = C
    tab[0, 64:] = C
    tab[1, :64] = Sg
    tab[1, 64:] = Sg
    return tab


def _prep_shared(inp):
    f = lambda k: np.asarray(inp[k], np.float32)
    L = DEPTH
    sh = {}
    small = np.zeros((L, 128, NS), np.float32)
    for l in range(L):
        small[l, :, O_NMIX:O_NMIX + 16] = _fm(f("norm_mix")[l])
        small[l, :, O_NMLP:O_NMLP + 16] = _fm(f("norm_mlp")[l])
        small[l, :, O_QN:O_QN + 4] = _fm(f("mla_q_norm")[l])
        small[l, :, O_KVN:O_KVN + 4] = _fm(f("mla_kv_norm")[l])
        small[l, :, O_CB:O_CB + 4] = _fm(f("conv_b")[l])
        small[l, :, O_LNG:O_LNG + 4] = _fm(f("conv_ln_g")[l])
        small[l, :, O_LNB:O_LNB + 4] = _fm(f("conv_ln_b")[l])
        small[l, :, O_ON:O_ON + 16] = _fm(f("out_norm")[l])
        small[l, :, O_ADAB:O_ADAB + 96] = _fm(f("ada_b")[l])
        cw = f("conv_w")[l][:, 0, :]
        small[l, :, O_CW:O_CW + 124] = cw.T.reshape(4, 128, 31).transpose(1, 0, 2).reshape(128, 124)
        small[l, :, O_SINK:O_SINK + 8] = np.broadcast_to(f("swa_sink")[l][None, :], (128, 8))
    sh["small"] = small
    sh["fnorm"] = _fm(f("final_norm"))
    sh["_rope"] = _rope_tables()
    cbf = np.zeros((128, 1152), np.float32)
    cbf[:, 0:128] = np.eye(128, dtype=np.float32)
    j = np.arange(128)[:, None]
    qi = np.arange(128)[None, :]
    mp = np.where(j >= qi, 0.0, NEG).astype(np.float32)
    mn = np.where(j <= qi, 0.0, NEG).astype(np.float32)
    cbf[:, 128:640] = np.tile(mp, (1, 4))
    cbf[:, 640:1152] = np.tile(mn, (1, 4))
    sh["_cbf"] = cbf
    sh["ada_t"] = np.stack([_tile(f("ada_w")[l], 512) for l in range(L)])
    wins = []
    for l in range(L):
        W = f("w_in")[l]
        groups = []
        pad = lambda a: np.concatenate([a, np.zeros((2048, 128 - a.shape[1]), np.float32)], 1) if a.shape[1] < 128 else a
        for j_ in range(4):
            groups.append(W[:, j_ * 128:(j_ + 1) * 128])
        for j_ in range(4):
            groups.append(W[:, 512 + j_ * 128:512 + (j_ + 1) * 128])
        kr = W[:, 1024:1088]
        groups.append(pad(kr))
        groups.append(pad(kr[:, PERM64]))
        p128 = np.concatenate([PERM64, 64 + PERM64])
        for j_ in range(4):
            bq = W[:, 1088 + j_ * 128:1088 + (j_ + 1) * 128]
            groups.append(bq)
            groups.append(bq[:, p128])
        bk = W[:, 1600:1728]
        groups.append(bk)
        groups.append(bk[:, p128])
        groups.append(W[:, 1728:1856])
        for j_ in range(4):
            groups.append(W[:, 2368 + j_ * 128:2368 + (j_ + 1) * 128])
            groups.append(W[:, 1856 + j_ * 128:1856 + (j_ + 1) * 128])
        wins.append(np.stack([_tile(np.ascontiguousarray(g), 128)[0] for g in groups]))
    sh["win_t"] = np.stack(wins)
    t4 = lambda A: np.ascontiguousarray(A.reshape(4, 128, A.shape[1]).transpose(1, 0, 2))
    ukv = f("mla_w_ukv")
    uq = f("mla_w_uq")
    sh["wk_t"] = np.stack([np.stack([t4(ukv[l][:, h, 0:128]) for h in range(8)]) for l in range(L)])
    sh["wv_t"] = np.stack([np.stack([t4(ukv[l][:, 4 * a:4 * a + 4, 128:256].reshape(512, 512))
                                     for a in range(2)]) for l in range(L)])
    sh["wqn_t"] = np.stack([np.stack([t4(uq[l][:, h, 0:128]) for h in range(8)]) for l in range(L)])
    sh["wqp_t"] = np.stack([np.stack([t4(uq[l][:, h, 128:192]) for h in range(8)]) for l in range(L)])
    sh["wqr_t"] = np.stack([np.stack([t4(uq[l][:, h, 128:192][:, PERM64]) for h in range(8)]) for l in range(L)])
    sh["wout_t"] = np.stack([_tile(f("w_out")[l], 128) for l in range(L)])
    sh["w1_t"] = np.stack([_tile(f("mlp_w1")[l], 128) for l in range(L)])
    sh["w2_t"] = np.stack([np.ascontiguousarray(
        f("mlp_w2")[l].reshape(2, 32, 128, 16, 128).transpose(0, 3, 2, 1, 4)) for l in range(L)])
    return sh


def _prep_core(inp, sh, b, half):
    perm = np.concatenate([np.arange(half * TH, (half + 1) * TH), np.arange((1 - half) * TH, (2 - half) * TH)])
    x = np.asarray(inp["x"], np.float32)[b][perm]
    ctx = np.asarray(inp["ctx"], np.float32)[b]
    xt = np.concatenate([x, ctx], 0).T
    d = {"xT": np.ascontiguousarray(xt.reshape(16, 128, T))}
    cT = np.zeros((128, 17, 2), np.float32)
    cT[:, :16, 0] = _fm(np.asarray(inp["c"], np.float32)[b])
    cT[:, :16, 1] = _fm(np.asarray(inp["c_ctx"], np.float32))
    cT[:, 16, 0] = 1.0 if half == 1 else 0.0
    cT[:, 16, 1] = 1.0 if half == 0 else 0.0
    d["cT"] = cT.reshape(128, 34)
    rt = sh["_rope"]
    d["rope"] = np.ascontiguousarray(np.concatenate([rt[:, :, :TL][:, :, perm], rt[:, :, TL:]], 2))
    base = sh["_cbf"]
    triP = base[:, 128:640]
    triN = base[:, 640:1152]
    allm = np.full((128, 512), NEG, np.float32)
    ms = [allm, triN, triP, allm] if half == 0 else [triP, allm, allm, triN]
    d["cbf"] = np.concatenate([base] + ms, 1).astype(ml_dtypes.bfloat16)
    return d


_CACHE = {}


def kernel(**inputs):
    if "nc" not in _CACHE:
        _CACHE["nc"] = build_program()[0]
    nc = _CACHE["nc"]
    sh = _prep_shared(inputs)
    shared = {k: v for k, v in sh.items() if not k.startswith("_")}
    in_maps = []
    for cid in range(NCORES):
        m = dict(shared)
        m.update(_prep_core(inputs, sh, cid // 2, cid % 2))
        in_maps.append(m)
    res = run_bass_kernel_spmd(nc, in_maps, core_ids=list(range(NCORES)))
    out = np.zeros((4, TL, D), np.float32)
    for cid in range(NCORES):
        b, half = cid // 2, cid % 2
        o = np.asarray(res.results[cid]["outT"], np.float32).reshape(2048, TH)
        out[b, half * TH:(half + 1) * TH, :] = o.T
    return out
```

```python
import numpy as np
import ml_dtypes
from contextlib import ExitStack
import concourse.bass as bass
import concourse.mybir as mybir
from concourse.bass_utils import run_bass_kernel_spmd

F32 = mybir.dt.float32
BF16 = mybir.dt.bfloat16
AF = mybir.ActivationFunctionType
ALU = mybir.AluOpType

D = 2048
TL = 4096
CL = 256
T = TL + CL
DEPTH = 2
EPS = 1e-6
MLA_SCALE = 192 ** -0.5
SWA_SCALE = 64 ** -0.5
NEG = -30000.0
NCORES = 8
TH = 2048
GROUPS = [(i * 512, 512, 0) for i in range(8)] + [(TL, 256, 1)]

O_NMIX, O_NMLP, O_QN, O_KVN, O_CB, O_LNG, O_LNB, O_ON, O_ADAB, O_CW, O_SINK = (
    0, 16, 32, 36, 40, 44, 48, 52, 68, 164, 288)
NS = 296

COMPUTE = ("pe", "act", "dve")
DMAQ = {"sp": 8, "pool": 6}


class Op:
    __slots__ = ("q", "fn", "deps", "signal", "sem", "val", "slot", "key")

    def __init__(self, q, fn):
        self.q = q
        self.fn = fn
        self.deps = {}
        self.signal = False
        self.sem = None
        self.val = 0
        self.slot = None
        self.key = q


class Res:
    __slots__ = ("w", "r", "name")

    def __init__(self, sched, name=""):
        self.w = dict(sched.snapshot)
        self.r = {}
        self.name = name


class Sched:
    def __init__(self, nc):
        self.nc = nc
        self.streams = {q: [] for q in ("pe", "act", "dve", "pool", "sp")}
        self.latest = {}
        self.snapshot = {}
        self.slot_rr = {q: 0 for q in DMAQ}
        self.slot_last = {}

    def barrier(self):
        self.snapshot = dict(self.latest)

    def res(self, name=""):
        return Res(self, name)

    def _add(self, op, reads, writes):
        allk = (op.slot is not None) or (op.q != "pe")
        for r in reads:
            for k, d in r.w.items():
                if allk or k != op.key:
                    op.deps[id(d)] = d
        for r in writes:
            for k, d in r.w.items():
                if allk or k != op.key:
                    op.deps[id(d)] = d
            for k, d in r.r.items():
                if (allk and k != op.key) or (k != op.key):
                    op.deps[id(d)] = d
        for r in reads:
            r.r[op.key] = op
        for r in writes:
            r.w[op.key] = op
        self.latest[op.key] = op
        self.streams[op.q].append(op)
        return op

    def op(self, q, fn, reads=(), writes=()):
        return self._add(Op(q, fn), reads, writes)

    def dma(self, q, out, in_, reads=(), writes=()):
        op = Op(q, lambda e: e.dma_start(out=out, in_=in_))
        n = DMAQ[q]
        s = self.slot_rr[q]
        self.slot_rr[q] = (s + 1) % n
        op.slot = s
        op.key = (q, s)
        prev = self.slot_last.get((q, s))
        if prev is not None:
            op.deps[id(prev)] = prev
        self.slot_last[(q, s)] = op
        op.signal = True
        return self._add(op, reads, writes)

    def finalize(self, sems, block):
        for q, ops in self.streams.items():
            for op in ops:
                for d in op.deps.values():
                    d.signal = True
        cnt = {}
        for q, ops in self.streams.items():
            for op in ops:
                if not op.signal:
                    continue
                if op.slot is not None:
                    key = (q, op.slot)
                    cnt[key] = cnt.get(key, 0) + 16
                    op.sem = sems[key]
                    op.val = cnt[key]
                else:
                    cnt[q] = cnt.get(q, 0) + 1
                    op.sem = sems[q]
                    op.val = cnt[q]
        self.counts = cnt
        engs = {"pe": block.tensor, "act": block.scalar, "dve": block.vector,
                "pool": block.gpsimd, "sp": block.sync}
        for q, deco in engs.items():
            ops = self.streams[q]

            def body(e, ops=ops, q=q):
                waited = {}
                for op in ops:
                    need = {}
                    for d in op.deps.values():
                        sid = id(d.sem)
                        if waited.get(sid, 0) >= d.val:
                            continue
                        if sid not in need or need[sid][1] < d.val:
                            need[sid] = (d.sem, d.val)
                    for sid, (sem, val) in need.items():
                        e.wait_ge(sem, val)
                        waited[sid] = val
                    ins = op.fn(e)
                    if op.signal:
                        ins.then_inc(op.sem, 16 if op.slot is not None else 1)
                if q in DMAQ:
                    for s in range(DMAQ[q]):
                        last = self.slot_last.get((q, s))
                        if last is not None and waited.get(id(last.sem), 0) < last.val:
                            e.wait_ge(last.sem, last.val)

            deco(body)


class Arena:
    def __init__(self, ap2d, nwords):
        self.ap = ap2d
        self.n = nwords
        self.off = 0

    def reset(self):
        self.off = 0

    def f32(self, n):
        a = self.ap[:, self.off:self.off + n]
        self.off += n
        assert self.off <= self.n, ("arena overflow", self.off)
        return a

    def bf16(self, n):
        w = (n + 1) // 2
        a = self.ap[:, self.off:self.off + w].bitcast(BF16)
        self.off += w
        assert self.off <= self.n, ("arena overflow", self.off)
        return a


class Rot:
    def __init__(self, S, aps, name=""):
        self.items = [(a, S.res(name)) for a in aps]
        self.i = 0

    def next(self):
        it = self.items[self.i % len(self.items)]
        self.i += 1
        return it


def build_program(layers=(0, 1), first_in="xT", final=True, debug=()):
    nc = bass.Bass("TRN2", target_bir_lowering=False)
    es = ExitStack()
    L = DEPTH

    def din(name, shape, dt=F32):
        return nc.dram_tensor(name, list(shape), dt, kind="ExternalInput").ap()

    def dscr(name, shape, dt):
        kind = "ExternalOutput" if name in debug else None
        if kind:
            return nc.dram_tensor(name, list(shape), dt, kind=kind).ap()
        return nc.dram_tensor(name, list(shape), dt).ap()

    xT_in = din("xT", [16, 128, T])
    cT_d = din("cT", [128, 34])
    small_d = din("small", [L, 128, NS])
    fnorm_d = din("fnorm", [128, 16])
    rope_d = din("rope", [2, 128, T])
    cbf_d = din("cbf", [128, 3200], BF16)
    ada_d = din("ada_t", [L, 24, 128, 16, 512])
    win_d = din("win_t", [L, 29, 128, 16, 128])
    wk_d = din("wk_t", [L, 8, 128, 4, 128])
    wv_d = din("wv_t", [L, 2, 128, 4, 512])
    wqn_d = din("wqn_t", [L, 8, 128, 4, 128])
    wqp_d = din("wqp_t", [L, 8, 128, 4, 64])
    wqr_d = din("wqr_t", [L, 8, 128, 4, 64])
    wout_d = din("wout_t", [L, 16, 128, 16, 128])
    w1_d = din("w1_t", [L, 64, 128, 16, 128])
    w2_d = din("w2_t", [L, 2, 16, 128, 32, 128])
    outT = nc.dram_tensor("outT", [16, 128, TH], F32, kind="ExternalOutput").ap()

    cq_s = dscr("cq_s", [4, 128, T], BF16)
    ckv_s = dscr("ckv_s", [4, 128, T], BF16)
    kpe_s = dscr("kpe_s", [64, T], BF16)
    qb_s = dscr("qb_s", [4, 128, T], BF16)
    kb_s = dscr("kb_s", [128, T], BF16)
    vb_s = dscr("vb_s", [34, 128, 128], BF16)
    glu_s = dscr("glu_s", [4, 128, T], BF16)
    kn_s = dscr("kn_s", [8, 128, T], BF16)
    va_s = dscr("va_s", [34, 128, 1024], BF16)
    qn_s = dscr("qn_s", [8, 128, T], BF16)
    qpe_s = dscr("qpe_s", [8, 64, T], BF16)
    mix_s = dscr("mix_s", [16, 128, T], F32)
    xB = dscr("xB", [16, 128, T], F32)

    arena_t = es.enter_context(nc.sbuf_tensor("arena", [128, 44544], F32))
    ar = Arena(arena_t[:, :], 44544)
    ones_f = es.enter_context(nc.sbuf_tensor("ones_f", [128, 128], F32))
    ones_b = es.enter_context(nc.sbuf_tensor("ones_b", [128, 128], BF16))
    cbf = es.enter_context(nc.sbuf_tensor("cbf_sb", [128, 3200], BF16))
    small = es.enter_context(nc.sbuf_tensor("small_sb", [128, L, NS], F32))
    fnorm = es.enter_context(nc.sbuf_tensor("fnorm_sb", [128, 16], F32))
    cTs = es.enter_context(nc.sbuf_tensor("cT_sb", [128, 34], F32))
    scT = es.enter_context(nc.sbuf_tensor("scT_sb", [128, 32], F32))
    modT_all = es.enter_context(nc.sbuf_tensor("modT", [128, L * 96, 2], F32))
    dtab_all = es.enter_context(nc.sbuf_tensor("dtab", [128, L * 6, 16, 2], F32))
    sinkx_all = es.enter_context(nc.sbuf_tensor("sinkx", [128, L * 8], F32))
    sinkbc_all = es.enter_context(nc.sbuf_tensor("sinkbc", [64, L * 8, 128], F32))
    PS = [es.enter_context(nc.psum_tensor(f"ps{i}", [128, 512], F32)) for i in range(8)]

    S = Sched(nc)
    sems = {}
    for q in COMPUTE:
        sems[q] = es.enter_context(nc.semaphore(f"sem_{q}"))
    for q, n in DMAQ.items():
        for s in range(n):
            sems[(q, s)] = es.enter_context(nc.semaphore(f"sem_{q}{s}"))

    RPS = [S.res(f"ps{i}") for i in range(8)]
    r_const = S.res("const")
    r_small = S.res("small")
    r_mod_l = [S.res("mod") for _ in range(L)]
    r_dtab_l = [S.res("dtab") for _ in range(L)]
    r_sink_l = [S.res("sink") for _ in range(L)]
    DR = {n: S.res(n) for n in ("xT", "xB", "cq", "ckv", "kpe", "qb", "kb", "vb", "glu", "kn", "va",
                                "qn", "qpe", "mix", "out")}

    ident = cbf[:, 0:128]
    maskP = cbf[:, 128:640]
    maskN = cbf[:, 640:1152]
    maskM = [cbf[:, 1152 + i * 512:1152 + (i + 1) * 512] for i in range(4)]

    def MM(out, lhsT, rhs, start, stop, reads, writes):
        S.op("pe", lambda e: e.matmul(out, lhsT=lhsT, rhs=rhs, start=start, stop=stop), reads, writes)

    def ACT(out, in_, func, reads, writes, bias=None, scale=None, q="act"):
        kw = {}
        if bias is not None:
            kw["bias"] = bias
        if scale is not None:
            kw["scale"] = scale
        S.op(q, lambda e: e.activation(out=out, in_=in_, func=func, **kw), reads, writes)

    def TT(out, in0, in1, op, reads, writes):
        S.op("dve", lambda e: e.tensor_tensor(out=out, in0=in0, in1=in1, op=op), reads, writes)

    def STT(out, in0, scalar, in1, op0, op1, reads, writes):
        S.op("dve", lambda e: e.scalar_tensor_tensor(out=out, in0=in0, scalar=scalar, in1=in1,
                                                      op0=op0, op1=op1), reads, writes)

    def TS(out, in0, s1, s2, op0, op1, reads, writes):
        if op1 is None:
            S.op("dve", lambda e: e.tensor_scalar(out=out, in0=in0, scalar1=s1, scalar2=None, op0=op0),
                 reads, writes)
        else:
            S.op("dve", lambda e: e.tensor_scalar(out=out, in0=in0, scalar1=s1, scalar2=s2, op0=op0,
                                                  op1=op1), reads, writes)

    def RECIP(out, in_, reads, writes):
        S.op("dve", lambda e: e.reciprocal(out=out, in_=in_), reads, writes)

    def COPY(q, out, in_, reads, writes):
        if q == "act":
            S.op("act", lambda e: e.activation(out=out, in_=in_, func=AF.Copy), reads, writes)
        else:
            S.op("dve", lambda e: e.tensor_copy(out=out, in_=in_), reads, writes)

    evac_rr = [0]

    def EVAC(out, in_, reads, writes):
        q = "act" if evac_rr[0] % 2 == 0 else "dve"
        evac_rr[0] += 1
        COPY(q, out, in_, reads, writes)

    def wstream(bufs, dram_list, lookahead):
        st = {"n": 0}

        def get(i):
            while st["n"] < min(len(dram_list), i + lookahead + 1):
                b, rb = bufs[st["n"] % len(bufs)]
                S.dma("pool", b, dram_list[st["n"]], writes=[rb])
                st["n"] += 1
            return bufs[i % len(bufs)]

        return get

    def rstd_from_ps(ps_ap, rps, n, sd_ap, rsd, out_ap, rout, eps_ap):
        ACT(sd_ap, ps_ap, AF.Sqrt, [rps, r_const], [rsd], bias=eps_ap, scale=1.0 / n)
        RECIP(out_ap, sd_ap, [rsd], [rout])

    eps_t = es.enter_context(nc.sbuf_tensor("eps_t", [128, 1], F32))
    S.op("dve", lambda e: e.memset(ones_f[:, :], 1.0), (), [r_const])
    S.op("dve", lambda e: e.memset(ones_b[:, :], 1.0), (), [r_const])
    S.op("dve", lambda e: e.memset(eps_t[:, :], EPS), (), [r_const])
    S.dma("sp", cbf[:, :], cbf_d, writes=[r_const])
    S.dma("sp", small[:, :, :], small_d.rearrange("l p n -> p l n"), writes=[r_small])
    S.dma("sp", fnorm[:, :], fnorm_d, writes=[r_small])
    S.dma("sp", cTs[:, :], cT_d, writes=[r_small])
    ACT(scT[:, :], cTs[:, 0:32], AF.Silu, [r_small], [r_small])
    flagA = cTs[:, 32:33]
    flagB = cTs[:, 33:34]
    eps_ap = eps_t[:, 0:1]

    def mod_steps(l, bufs, CW, pbank, rpbank):
        modT = modT_all[:, l * 96:(l + 1) * 96, :]
        dt = dtab_all[:, l * 6:(l + 1) * 6]
        sx = sinkx_all[:, l * 8:(l + 1) * 8]
        sbc = sinkbc_all[:, l * 8:(l + 1) * 8, :]
        r_mod, r_dt, r_sk = r_mod_l[l], r_dtab_l[l], r_sink_l[l]
        sc3 = scT[:, :].rearrange("p (k r) -> p k r", r=2)
        nst = 12288 // CW
        per = 512 // CW
        nj = CW // 128
        steps = []

        def issue(i):
            wt, rw = bufs[i % 2]
            S.dma("sp", wt, ada_d[l, i // per][:, :, (i % per) * CW:(i % per + 1) * CW], writes=[rw])

        def mk(i):
            def step():
                if i == 0:
                    issue(0)
                if i + 1 < nst:
                    issue(i + 1)
                wt, rw = bufs[i % 2]
                for j in range(nj):
                    for k in range(16):
                        MM(pbank[:, 2 * j:2 * j + 2], wt[:, k, j * 128:(j + 1) * 128], sc3[:, k, :],
                           k == 0, k == 15, [rw, r_small], [rpbank])
                c0 = i * nj
                for r in range(2):
                    src_ = pbank[:, 0:2 * nj].rearrange("p (j r) -> p j r", r=2)[:, :, r]
                    TT(modT[:, c0:c0 + nj, r], src_, small[:, l, O_ADAB + c0:O_ADAB + c0 + nj], ALU.add,
                       [rpbank, r_small], [r_mod])
            return step

        for i in range(nst):
            steps.append(mk(i))

        def derived():
            for r in range(2):
                STT(dt[:, 0, :, r], modT[:, 16:32, r], 1.0, small[:, l, O_NMIX:O_NMIX + 16],
                    ALU.add, ALU.mult, [r_mod, r_small], [r_dt])
                S.op("dve", lambda e, r=r: e.tensor_copy(out=dt[:, 1, :, r], in_=modT[:, 0:16, r]),
                     [r_mod], [r_dt])
                S.op("dve", lambda e, r=r: e.tensor_copy(out=dt[:, 2, :, r], in_=modT[:, 32:48, r]),
                     [r_mod], [r_dt])
                STT(dt[:, 3, :, r], modT[:, 64:80, r], 1.0, small[:, l, O_NMLP:O_NMLP + 16],
                    ALU.add, ALU.mult, [r_mod, r_small], [r_dt])
                S.op("dve", lambda e, r=r: e.tensor_copy(out=dt[:, 4, :, r], in_=modT[:, 48:64, r]),
                     [r_mod], [r_dt])
                S.op("dve", lambda e, r=r: e.tensor_copy(out=dt[:, 5, :, r], in_=modT[:, 80:96, r]),
                     [r_mod], [r_dt])
            ACT(sx[:, :], small[:, l, O_SINK:O_SINK + 8], AF.Exp, [r_small], [r_sk])
            for h in range(8):
                TS(sbc[:, h, :], ones_f[0:64, :], sx[0:64, h:h + 1], None, ALU.mult, None,
                   [r_const, r_sk], [r_sk])

        steps.append(derived)
        return steps

    def stage_mod(l):
        S.barrier()
        ar.reset()
        bufs = [(ar.f32(8192).rearrange("p (k j) -> p k j", j=512), S.res("adaw")) for _ in range(2)]
        for st in mod_steps(l, bufs, 512, PS[0], RPS[0]):
            st()

    def modulate_group(xg, rx, G, r, which, hT, rh, sqrot, tmprot, pss, rpss, rstd, rrstd, sd, rsd):
        for k in range(16):
            sq, rsq = sqrot.next()
            ACT(sq[:, :G], xg[:, k, :], AF.Square, [rx], [rsq])
            MM(pss[:, :G], ones_f[:, :], sq[:, :G], k == 0, k == 15, [rsq, r_const], [rpss])
        rstd_from_ps(pss[:, :G], rpss, D, sd[:, :G], rsd, rstd[:, :G], rrstd, eps_ap)
        for k in range(16):
            tmp, rt = tmprot.next()
            STT(tmp[:, :G], xg[:, k, :], dtab[:, 3 * which, k, r:r + 1], rstd[:, :G], ALU.mult, ALU.mult,
                [rx, rrstd, r_dtab], [rt])
            ACT(hT[:, k, :], tmp[:, :G], AF.Identity, [rt, r_dtab], [rh],
                bias=dtab[:, 3 * which + 1, k, r:r + 1], scale=1.0)

    def stage_in(l, xsrc, rxsrc, groups, modes):
        S.barrier()
        ar.reset()
        bigf = [(ar.f32(8192), S.res("xg")) for _ in range(2)]
        bigh = [(ar.bf16(8192), S.res("hT")) for _ in range(2)]
        wb = [(ar.bf16(2048).rearrange("p (k j) -> p k j", j=128), S.res("w")) for _ in range(4)]
        aqf = ar.f32(2048)
        r_aqf = S.res()
        cqst = ar.bf16(2048)
        r_cqst = S.res()
        ropeo = Rot(S, [ar.bf16(512) for _ in range(2)])
        tmpf = Rot(S, [ar.f32(512) for _ in range(4)])
        sqrot = Rot(S, [ar.f32(512) for _ in range(2)])
        vbst = ar.bf16(512)
        r_vbst = S.res()
        glust = ar.bf16(2048)
        r_glust = S.res()
        rstd = ar.f32(512)
        r_rstd = S.res()
        sd = ar.f32(512)
        r_sd = S.res()
        rstd2 = ar.f32(512)
        r_rstd2 = S.res()
        tabs = [(ar.f32(512), ar.f32(512), S.res("tab")) for _ in range(2)]
        psrot = Rot(S, [None] * 4)
        psi = [0]

        def nextps():
            i = psi[0] % 6
            psi[0] += 1
            return PS[i], RPS[i]

        pss, rpss = PS[7], RPS[7]
        xv = xsrc.rearrange("k p t -> p k t")
        wl = []
        for gi, (t0, G, r) in enumerate(groups):
            cgs = {"full": list(range(29)), "kv": [4, 5, 6, 7, 8, 9, 18, 19, 20],
                   "kvglu": [4, 5, 6, 7, 8, 9, 18, 19, 20] + list(range(21, 29))}[modes[gi]]
            for c in cgs:
                wl.append((gi, c))
        getw = wstream(wb, [win_d[l, c] for (_, c) in wl], 3)
        wi = 0

        def load_x(gi):
            t0, G, r = groups[gi]
            xg2, rx = bigf[gi % 2]
            xg = xg2[:, :16 * G].rearrange("p (k t) -> p k t", t=G)
            S.dma("sp", xg, xv[:, :, t0:t0 + G], reads=[rxsrc], writes=[rx])
            ct, st, rt = tabs[gi % 2]
            S.dma("sp", ct[:, :G], rope_d[0, :, t0:t0 + G], writes=[rt])
            S.dma("sp", st[:, :G], rope_d[1, :, t0:t0 + G], writes=[rt])

        load_x(0)
        for gi, (t0, G, r) in enumerate(groups):
            kvonly = modes[gi] != "full"
            doglu = modes[gi] != "kv"
            if gi + 1 < len(groups):
                load_x(gi + 1)
            xg2, rx = bigf[gi % 2]
            xg = xg2[:, :16 * G].rearrange("p (k t) -> p k t", t=G)
            h2, rh = bigh[gi % 2]
            hT = h2[:, :16 * G].rearrange("p (k t) -> p k t", t=G)
            ct, st, rtab = tabs[gi % 2]
            if gi == 0:
                modulate_group(xg, rx, G, r, 0, hT, rh, sqrot, tmpf, pss, rpss, rstd, r_rstd, sd, r_sd)

            def proj(M):
                nonlocal wi
                wt, rw = getw(wi)
                wi += 1
                pb, rp = nextps()
                for k in range(16):
                    MM(pb[:M, :G], wt[:, k, :M], hT[:, k, :], k == 0, k == 15, [rw, rh], [rp])
                return pb, rp

            def rope(pa, rpa, pbb, rpb, M, out_ap, rout):
                t1, r1 = tmpf.next()
                t2, r2 = tmpf.next()
                TT(t1[:M, :G], pbb[:M, :G], st[:M, :G], ALU.mult, [rpb, rtab], [r1])
                TT(t2[:M, :G], pa[:M, :G], ct[:M, :G], ALU.mult, [rpa, rtab], [r2])
                TT(out_ap, t1[:M, :G], t2[:M, :G], ALU.add, [r1, r2], [rout])

            def latent_norm(norm_off, dst, rdst):
                aq3 = aqf[:, :4 * G].rearrange("p (j t) -> p j t", t=G)
                for j in range(4):
                    pb, rp = proj(128)
                    COPY("act", aq3[:, j, :], pb[:, :G], [rp], [r_aqf])
                for j in range(4):
                    sq, rsq = sqrot.next()
                    ACT(sq[:, :G], aq3[:, j, :], AF.Square, [r_aqf], [rsq])
                    MM(pss[:, :G], ones_f[:, :], sq[:, :G], j == 0, j == 3, [rsq, r_const], [rpss])
                rstd_from_ps(pss[:, :G], rpss, 512, sd[:, :G], r_sd, rstd2[:, :G], r_rstd2, eps_ap)
                cq3 = cqst[:, :4 * G].rearrange("p (j t) -> p j t", t=G)
                for j in range(4):
                    STT(cq3[:, j, :], aq3[:, j, :], small[:, l, norm_off + j:norm_off + j + 1],
                        rstd2[:, :G], ALU.mult, ALU.mult, [r_aqf, r_rstd2, r_small], [r_cqst])
                S.dma("sp", dst.rearrange("j p t -> p j t")[:, :, t0:t0 + G], cq3, reads=[r_cqst],
                      writes=[rdst])

            if not kvonly:
                latent_norm(O_QN, cq_s, DR["cq"])
            latent_norm(O_KVN, ckv_s, DR["ckv"])
            pa, rpa = proj(64)
            pbb, rpb = proj(64)
            o, ro = ropeo.next()
            rope(pa, rpa, pbb, rpb, 64, o[:64, :G], ro)
            S.dma("sp", kpe_s[:, t0:t0 + G], o[:64, :G], reads=[ro], writes=[DR["kpe"]])
            if gi + 1 < len(groups):
                t0n, Gn, rn = groups[gi + 1]
                xn2, rxn = bigf[(gi + 1) % 2]
                hn2, rhn = bigh[(gi + 1) % 2]
                modulate_group(xn2[:, :16 * Gn].rearrange("p (k t) -> p k t", t=Gn), rxn, Gn, rn, 0,
                               hn2[:, :16 * Gn].rearrange("p (k t) -> p k t", t=Gn), rhn, sqrot, tmpf,
                               pss, rpss, rstd, r_rstd, sd, r_sd)
            if not kvonly:
                for j in range(4):
                    pa, rpa = proj(128)
                    pbb, rpb = proj(128)
                    o, ro = ropeo.next()
                    rope(pa, rpa, pbb, rpb, 128, o[:, :G], ro)
                    S.dma("sp", qb_s[j, :, t0:t0 + G], o[:, :G], reads=[ro], writes=[DR["qb"]])
            pa, rpa = proj(128)
            pbb, rpb = proj(128)
            o, ro = ropeo.next()
            rope(pa, rpa, pbb, rpb, 128, o[:, :G], ro)
            S.dma("sp", kb_s[:, t0:t0 + G], o[:, :G], reads=[ro], writes=[DR["kb"]])
            wt, rw = getw(wi)
            wi += 1
            pb, rp = nextps()
            ntt = G // 128
            for tt in range(ntt):
                for k in range(16):
                    MM(pb[:, tt * 128:(tt + 1) * 128], hT[:, k, tt * 128:(tt + 1) * 128], wt[:, k, :],
                       k == 0, k == 15, [rw, rh], [rp])
            COPY("act", vbst[:, :G], pb[:, :G], [rp], [r_vbst])
            S.dma("sp", vb_s.rearrange("c p d -> p c d")[:, t0 // 128:t0 // 128 + ntt, :],
                  vbst[:, :G].rearrange("p (c d) -> p c d", d=128), reads=[r_vbst], writes=[DR["vb"]])
            if doglu:
                gl3 = glust[:, :4 * G].rearrange("p (j t) -> p j t", t=G)
                for j in range(4):
                    pg, rpg = proj(128)
                    sg, rsg = tmpf.next()
                    ACT(sg[:, :G], pg[:, :G], AF.Sigmoid, [rpg], [rsg])
                    pa, rpa = proj(128)
                    TT(gl3[:, j, :], pa[:, :G], sg[:, :G], ALU.mult, [rpa, rsg], [r_glust])
                S.dma("sp", glu_s.rearrange("j p t -> p j t")[:, :, t0:t0 + G], gl3, reads=[r_glust],
                      writes=[DR["glu"]])

    def stage_kvup(l):
        S.barrier()
        ar.reset()
        wk = ar.bf16(8 * 512).rearrange("p (h r d) -> p h r d", h=8, r=4)
        wv = ar.bf16(2 * 2048).rearrange("p (a r d) -> p a r d", a=2, r=4)
        r_w = S.res()
        S.dma("pool", wk, wk_d[l].rearrange("h p r d -> p h r d"), writes=[r_w])
        S.dma("pool", wv, wv_d[l].rearrange("a p r d -> p a r d"), writes=[r_w])
        ck = [(ar.bf16(2048), S.res()) for _ in range(2)]
        kst = [(ar.bf16(8 * 512), S.res()) for _ in range(2)]
        vst = [(ar.bf16(4 * 1024), S.res()) for _ in range(2)]
        psi = [0]

        def nextps():
            i = psi[0] % 8
            psi[0] += 1
            return PS[i], RPS[i]

        def load(gi):
            t0, G, r = GROUPS[gi]
            c2, rc = ck[gi % 2]
            S.dma("sp", c2[:, :4 * G].rearrange("p (j t) -> p j t", t=G),
                  ckv_s.rearrange("j p t -> p j t")[:, :, t0:t0 + G], reads=[DR["ckv"]], writes=[rc])

        load(0)
        for gi, (t0, G, r) in enumerate(GROUPS):
            if gi + 1 < len(GROUPS):
                load(gi + 1)
            c2, rc = ck[gi % 2]
            c3 = c2[:, :4 * G].rearrange("p (j t) -> p j t", t=G)
            k2, rk = kst[gi % 2]
            k3 = k2[:, :8 * G].rearrange("p (h t) -> p h t", t=G)
            for h in range(8):
                pb, rp = nextps()
                for j in range(4):
                    MM(pb[:, :G], wk[:, h, j, :], c3[:, j, :], j == 0, j == 3, [r_w, rc], [rp])
                EVAC(k3[:, h, :], pb[:, :G], [rp], [rk])
            S.dma("sp", kn_s.rearrange("h p t -> p h t")[:, :, t0:t0 + G], k3, reads=[rk], writes=[DR["kn"]])
            ntt = G // 128
            v2, rv = vst[gi % 2]
            v3 = v2[:, :ntt * 1024].rearrange("p (c d) -> p c d", d=1024)
            for tt in range(ntt):
                for a in range(2):
                    pb, rp = nextps()
                    for j in range(4):
                        MM(pb[:, :], c3[:, j, tt * 128:(tt + 1) * 128], wv[:, a, j, :], j == 0, j == 3,
                           [r_w, rc], [rp])
                    EVAC(v3[:, tt, a * 512:(a + 1) * 512], pb[:, :], [rp], [rv])
            S.dma("sp", va_s.rearrange("c p d -> p c d")[:, t0 // 128:t0 // 128 + ntt, :], v3, reads=[rv],
                  writes=[DR["va"]])

    def stage_qup(l, groups):
        S.barrier()
        ar.reset()
        wqn = ar.bf16(8 * 512).rearrange("p (h r d) -> p h r d", h=8, r=4)
        wqp = ar.bf16(8 * 256).rearrange("p (h r d) -> p h r d", h=8, r=4)
        wqr = ar.bf16(8 * 256).rearrange("p (h r d) -> p h r d", h=8, r=4)
        r_w = S.res()
        S.dma("pool", wqn, wqn_d[l].rearrange("h p r d -> p h r d"), writes=[r_w])
        S.dma("pool", wqp, wqp_d[l].rearrange("h p r d -> p h r d"), writes=[r_w])
        S.dma("pool", wqr, wqr_d[l].rearrange("h p r d -> p h r d"), writes=[r_w])
        ck = [(ar.bf16(2048), S.res()) for _ in range(2)]
        qst = [(ar.bf16(8 * 512), S.res()) for _ in range(2)]
        pst = [(ar.bf16(8 * 512), S.res()) for _ in range(2)]
        tabs = [(ar.f32(512), ar.f32(512), S.res("tab")) for _ in range(2)]
        tmpf = Rot(S, [ar.f32(512) for _ in range(4)])
        psi = [0]

        def nextps():
            i = psi[0] % 8
            psi[0] += 1
            return PS[i], RPS[i]

        def load(gi):
            t0, G, r = groups[gi]
            c2, rc = ck[gi % 2]
            S.dma("sp", c2[:, :4 * G].rearrange("p (j t) -> p j t", t=G),
                  cq_s.rearrange("j p t -> p j t")[:, :, t0:t0 + G], reads=[DR["cq"]], writes=[rc])
            ct, st, rt = tabs[gi % 2]
            S.dma("sp", ct[:64, :G], rope_d[0, 0:64, t0:t0 + G], writes=[rt])
            S.dma("sp", st[:64, :G], rope_d[1, 0:64, t0:t0 + G], writes=[rt])

        load(0)
        for gi, (t0, G, r) in enumerate(groups):
            if gi + 1 < len(groups):
                load(gi + 1)
            c2, rc = ck[gi % 2]
            c3 = c2[:, :4 * G].rearrange("p (j t) -> p j t", t=G)
            ct, st, rtab = tabs[gi % 2]
            q2, rq = qst[gi % 2]
            q3 = q2[:, :8 * G].rearrange("p (h t) -> p h t", t=G)
            p2, rpp = pst[gi % 2]
            p3 = p2[:, :8 * G].rearrange("p (h t) -> p h t", t=G)
            for h in range(8):
                pb, rp = nextps()
                for j in range(4):
                    MM(pb[:, :G], wqn[:, h, j, :], c3[:, j, :], j == 0, j == 3, [r_w, rc], [rp])
                EVAC(q3[:, h, :], pb[:, :G], [rp], [rq])
                pa, rpa = nextps()
                for j in range(4):
                    MM(pa[:64, :G], wqp[:, h, j, :], c3[:, j, :], j == 0, j == 3, [r_w, rc], [rpa])
                pbb, rpb = nextps()
                for j in range(4):
                    MM(pbb[:64, :G], wqr[:, h, j, :], c3[:, j, :], j == 0, j == 3, [r_w, rc], [rpb])
                t1, r1 = tmpf.next()
                t2, r2 = tmpf.next()
                TT(t1[:64, :G], pbb[:64, :G], st[:64, :G], ALU.mult, [rpb, rtab], [r1])
                TT(t2[:64, :G], pa[:64, :G], ct[:64, :G], ALU.mult, [rpa, rtab], [r2])
                TT(p3[:64, h, :], t1[:64, :G], t2[:64, :G], ALU.add, [r1, r2], [rpp])
            S.dma("sp", qn_s.rearrange("h p t -> p h t")[:, :, t0:t0 + G], q3, reads=[rq], writes=[DR["qn"]])
            S.dma("sp", qpe_s.rearrange("h p t -> p h t")[:, :, t0:t0 + G], p3[:64], reads=[rpp],
                  writes=[DR["qpe"]])

    def stage_atta(l, groups, extra_mod=None):
        S.barrier()
        ar.reset()
        extra = []
        if extra_mod is not None:
            ebufs = [(ar.f32(4096).rearrange("p (k j) -> p k j", j=256), S.res("adaw")) for _ in range(2)]
            extra = mod_steps(extra_mod, ebufs, 256, PS[3], RPS[3])
        kpe = ar.bf16(T)
        r_kpe = S.res()
        S.dma("sp", kpe[:64, :], kpe_s, reads=[DR["kpe"]], writes=[r_kpe])
        kn = [(ar.bf16(T), S.res()) for _ in range(2)]
        vh = [(ar.bf16(34 * 128).rearrange("p (c d) -> p c d", d=128), S.res()) for _ in range(2)]
        qn = [(ar.bf16(512), ar.bf16(512), S.res()) for _ in range(3)]
        pt = Rot(S, [ar.bf16(512) for _ in range(4)])
        ost = Rot(S, [ar.f32(512) for _ in range(2)])
        rden = Rot(S, [ar.f32(512) for _ in range(2)])
        work = [(h, gi) for h in range(8) for gi in range(len(groups))]

        def load_head(h):
            k2, rk = kn[h % 2]
            S.dma("sp", k2[:, :], kn_s[h], reads=[DR["kn"]], writes=[rk])
            v3, rv = vh[h % 2]
            S.dma("sp", v3, va_s.rearrange("c p d -> p c d")[:, :, h * 128:(h + 1) * 128],
                  reads=[DR["va"]], writes=[rv])

        def load_q(wi_):
            h, gi = work[wi_]
            t0, G, r = groups[gi]
            qa, qp, rq = qn[wi_ % 3]
            S.dma("sp", qa[:, :G], qn_s[h, :, t0:t0 + G], reads=[DR["qn"]], writes=[rq])
            S.dma("sp", qp[:64, :G], qpe_s[h, :, t0:t0 + G], reads=[DR["qpe"]], writes=[rq])

        load_head(0)
        load_q(0)
        load_q(1)
        sps = [0]
        pend = []

        def flush(keep):
            while len(pend) > keep:
                pend.pop(0)()

        for wi_, (h, gi) in enumerate(work):
            t0, G, r = groups[gi]
            if gi == 1 and h + 1 < 8:
                load_head(h + 1)
            if wi_ + 2 < len(work):
                load_q(wi_ + 2)
            k2, rk = kn[h % 2]
            v3, rv = vh[h % 2]
            qa, qp, rq = qn[wi_ % 3]
            chunks = list(range(34)) if r == 0 else [32, 33]
            po, rpo = PS[4 + (wi_ % 2) * 2], RPS[4 + (wi_ % 2) * 2]
            pd, rpd = PS[5 + (wi_ % 2) * 2], RPS[5 + (wi_ % 2) * 2]
            for ci, c in enumerate(chunks):
                nsb = 3 if extra_mod is not None else 4
                psb, rps_ = PS[sps[0] % nsb], RPS[sps[0] % nsb]
                sps[0] += 1
                MM(psb[:, :G], k2[:, c * 128:(c + 1) * 128], qa[:, :G], True, False, [rk, rq], [rps_])
                MM(psb[:, :G], kpe[:64, c * 128:(c + 1) * 128], qp[:64, :G], False, True, [r_kpe, rq], [rps_])
                p, rp = pt.next()
                ACT(p[:, :G], psb[:, :G], AF.Exp, [rps_], [rp], scale=MLA_SCALE)

                def back(ci=ci, c=c, p=p, rp=rp, po=po, rpo=rpo, pd=pd, rpd=rpd, v3=v3, rv=rv, G=G,
                         n=len(chunks), h=h, t0=t0):
                    MM(po[:, :G], v3[:, c, :], p[:, :G], ci == 0, ci == n - 1, [rv, rp], [rpo])
                    MM(pd[:, :G], ones_b[:, :], p[:, :G], ci == 0, ci == n - 1, [r_const, rp], [rpd])
                    if ci == n - 1:
                        rd, rrd = rden.next()
                        RECIP(rd[:, :G], pd[:, :G], [rpd], [rrd])
                        o, ro = ost.next()
                        TT(o[:, :G], po[:, :G], rd[:, :G], ALU.mult, [rpo, rrd], [ro])
                        S.dma("sp", mix_s[h, :, t0:t0 + G], o[:, :G], reads=[ro], writes=[DR["mix"]])

                pend.append(back)
                flush(2)
            if extra and wi_ >= 1:
                extra.pop(0)()
        flush(0)
        while extra:
            extra.pop(0)()

    def stage_attb(l, groups):
        S.barrier()
        ar.reset()
        kb = ar.bf16(2 * T).rearrange("p (h t) -> p h t", h=2)
        vb = ar.bf16(34 * 128).rearrange("p (c d) -> p c d", d=128)
        r_kv = S.res()
        S.dma("sp", kb[:64], kb_s.rearrange("(h p) t -> p h t", p=64), reads=[DR["kb"]], writes=[r_kv])
        S.dma("sp", vb, vb_s.rearrange("c p d -> p c d"), reads=[DR["vb"]], writes=[r_kv])
        qg = [(ar.bf16(8 * 512).rearrange("p (h t) -> p h t", h=8), S.res()) for _ in range(2)]
        pt = Rot(S, [ar.bf16(512) for _ in range(4)])
        ost = Rot(S, [ar.f32(512) for _ in range(2)])
        den = Rot(S, [ar.f32(512) for _ in range(2)])
        qbv = qb_s.rearrange("j (a p) t -> p (j a) t", p=64)
        mixb = mix_s.rearrange("k (a p) t -> p (k a) t", p=64)

        def load(gi):
            t0, G, r = groups[gi]
            q3, rq = qg[gi % 2]
            S.dma("sp", q3[:64, :, :G], qbv[:, :, t0:t0 + G], reads=[DR["qb"]], writes=[rq])

        load(0)
        it = 0
        sps = [0]
        pend = []
        for gi, (t0, G, r) in enumerate(groups):
            if gi + 1 < len(groups):
                load(gi + 1)
            q3, rq = qg[gi % 2]
            for nn in range(G // 128):
                n = t0 // 128 + nn
                if r == 0:
                    prv = (31, maskM[0]) if n == 0 else ((15, maskM[2]) if n == 16 else (n - 1, maskP))
                    nxt = (16, maskM[1]) if n == 15 else ((0, maskM[3]) if n == 31 else (n + 1, maskN))
                    chunks = [prv, (n, None), nxt, (32, None), (33, None)]
                else:
                    chunks = [(32, None), (33, None)]
                for kh in range(2):
                    po, rpo = PS[4 + (it % 2) * 2], RPS[4 + (it % 2) * 2]
                    pd, rpd = PS[5 + (it % 2) * 2], RPS[5 + (it % 2) * 2]
                    it += 1
                    rhs = q3[:64, 4 * kh:4 * kh + 4, nn * 128:(nn + 1) * 128]
                    for ci, (c, mask) in enumerate(chunks):
                        psb, rps_ = PS[sps[0] % 4], RPS[sps[0] % 4]
                        sps[0] += 1
                        ps3 = psb[:, :].rearrange("p (g q) -> p g q", g=4)
                        MM(ps3, kb[:64, kh, c * 128:(c + 1) * 128], rhs, True, mask is None, [r_kv, rq], [rps_])
                        if mask is not None:
                            MM(ps3, ident, mask.rearrange("p (g q) -> p g q", g=4), False, True,
                               [r_const], [rps_])
                        p, rp = pt.next()
                        ACT(p[:, :], psb[:, :], AF.Exp, [rps_], [rp], scale=SWA_SCALE)

                        def back(ci=ci, c=c, p=p, rp=rp, po=po, rpo=rpo, pd=pd, rpd=rpd, kh=kh, n=n,
                                 nch=len(chunks)):
                            last = ci == nch - 1
                            MM(po[:64, :], vb[:, c, kh * 64:(kh + 1) * 64], p[:, :], ci == 0, last,
                               [r_kv, rp], [rpo])
                            MM(pd[:64, :], ones_b[:, 0:64], p[:, :], ci == 0, last, [r_const, rp], [rpd])
                            if last:
                                dn, rdn = den.next()
                                TT(dn[:64, :], pd[:64, :],
                                   sinkbc[:, 4 * kh:4 * kh + 4, :].rearrange("p g q -> p (g q)"), ALU.add,
                                   [rpd, r_sink], [rdn])
                                RECIP(dn[:64, :], dn[:64, :], [rdn], [rdn])
                                o, ro = ost.next()
                                TT(o[:64, :], po[:64, :], dn[:64, :], ALU.mult, [rpo, rdn], [ro])
                                S.dma("sp", mixb[:, 16 + 4 * kh:16 + 4 * kh + 4, n * 128:(n + 1) * 128],
                                      o[:64, :].rearrange("p (g q) -> p g q", g=4), reads=[ro],
                                      writes=[DR["mix"]])

                        pend.append(back)
                        while len(pend) > 2:
                            pend.pop(0)()
        while pend:
            pend.pop(0)()

    def stage_conv(l, groups):
        S.barrier()
        ar.reset()
        gin = [(ar.bf16(4 * 544).rearrange("p (c t) -> p c t", c=4), S.res()) for _ in range(2)]
        dg = ar.bf16(124 * 128).rearrange("p (i j) -> p i j", j=128)
        r_dg = S.res()
        for i in range(124):
            if i % 2 == 0:
                TS(dg[:, i, :], ident, small[:, l, O_CW + i:O_CW + i + 1], None, ALU.mult, None,
                   [r_const, r_small], [r_dg])
            else:
                ACT(dg[:, i, :], ident, AF.Copy, [r_const, r_small], [r_dg],
                    scale=small[:, l, O_CW + i:O_CW + i + 1])
        hb = [(ar.f32(512), S.res()) for _ in range(4)]
        sqrot = Rot(S, [ar.f32(512) for _ in range(2)])
        mean = ar.f32(512)
        r_mean = S.res()
        msq = ar.f32(512)
        r_msq = S.res()
        var = ar.f32(512)
        r_var = S.res()
        rstd = ar.f32(512)
        r_rstd = S.res()
        tmpf = Rot(S, [ar.f32(512) for _ in range(2)])
        ost = [(ar.f32(2048).rearrange("p (c t) -> p c t", c=4), S.res()) for _ in range(2)]
        gv = glu_s.rearrange("j p t -> p j t")
        cps = [0]

        def load(gi):
            t0, G, r = groups[gi]
            g3, rg = gin[gi % 2]
            if r == 1:
                S.op("dve", lambda e, g3=g3: e.memset(g3[:, :, 0:15], 0.0), (), [rg])
                S.op("dve", lambda e, g3=g3, G=G: e.memset(g3[:, :, G + 15:G + 30], 0.0), (), [rg])
                S.dma("sp", g3[:, :, 15:15 + G], gv[:, :, t0:t0 + G], reads=[DR["glu"]], writes=[rg])
                return
            gidx = t0 // 512
            a, b = t0 - 15, t0 + G + 15
            if gidx == 0:
                S.dma("sp", g3[:, :, 0:15], gv[:, :, TL - 15:TL], reads=[DR["glu"]], writes=[rg])
                a = 0
            if gidx == 7:
                S.dma("sp", g3[:, :, G + 15:G + 30], gv[:, :, 0:15], reads=[DR["glu"]], writes=[rg])
                b = TL
            S.dma("sp", g3[:, :, a - (t0 - 15):b - (t0 - 15)], gv[:, :, a:b], reads=[DR["glu"]], writes=[rg])
            if gidx in (0, 4):
                fl = flagA if gidx == 0 else flagB
                TS(g3[:, :, 0:15], g3[:, :, 0:15], fl, None, ALU.mult, None, [rg, r_small], [rg])
            if gidx in (3, 7):
                fl = flagA if gidx == 7 else flagB
                TS(g3[:, :, G + 15:G + 30], g3[:, :, G + 15:G + 30], fl, None, ALU.mult, None,
                   [rg, r_small], [rg])

        load(0)
        cw = O_CW
        for gi, (t0, G, r) in enumerate(groups):
            if gi + 1 < len(groups):
                load(gi + 1)
            g3, rg = gin[gi % 2]
            pm, rpm = PS[(gi % 2) * 2], RPS[(gi % 2) * 2]
            pv, rpv = PS[(gi % 2) * 2 + 1], RPS[(gi % 2) * 2 + 1]
            for c in range(4):
                pc, rpc = PS[4 + cps[0] % 4], RPS[4 + cps[0] % 4]
                cps[0] += 1
                for k in range(31):
                    MM(pc[:, :G], dg[:, c * 31 + k, :], g3[:, c, k:k + G], k == 0, k == 30, [r_dg, rg], [rpc])
                h_, rh_ = hb[c]
                ACT(h_[:, :G], pc[:, :G], AF.Identity, [rpc, r_small], [rh_],
                    bias=small[:, l, O_CB + c:O_CB + c + 1], scale=1.0)
                MM(pm[:, :G], ones_f[:, :], h_[:, :G], c == 0, c == 3, [rh_, r_const], [rpm])
                sq, rsq = sqrot.next()
                ACT(sq[:, :G], h_[:, :G], AF.Square, [rh_], [rsq])
                MM(pv[:, :G], ones_f[:, :], sq[:, :G], c == 0, c == 3, [rsq, r_const], [rpv])
            S.op("act", lambda e, pm=pm, G=G: e.activation(out=mean[:, :G], in_=pm[:, :G], func=AF.Copy,
                                                          scale=1.0 / 512), [rpm], [r_mean])
            TT(msq[:, :G], mean[:, :G], mean[:, :G], ALU.mult, [r_mean], [r_msq])
            STT(var[:, :G], pv[:, :G], 1.0 / 512, msq[:, :G], ALU.mult, ALU.subtract, [rpv, r_msq], [r_var])
            ACT(var[:, :G], var[:, :G], AF.Sqrt, [r_var, r_const], [r_var], bias=eps_ap, scale=1.0)
            RECIP(rstd[:, :G], var[:, :G], [r_var], [r_rstd])
            o3, ro = ost[gi % 2]
            for c in range(4):
                h_, rh_ = hb[c]
                t1, r1 = tmpf.next()
                TT(t1[:, :G], h_[:, :G], mean[:, :G], ALU.subtract, [rh_, r_mean], [r1])
                STT(t1[:, :G], t1[:, :G], small[:, l, O_LNG + c:O_LNG + c + 1], rstd[:, :G], ALU.mult, ALU.mult,
                    [r1, r_rstd, r_small], [r1])
                ACT(o3[:, c, :G], t1[:, :G], AF.Silu, [r1, r_small], [ro],
                    bias=small[:, l, O_LNB + c:O_LNB + c + 1], scale=1.0)
            S.dma("sp", mix_s.rearrange("k p t -> p k t")[:, 12:16, t0:t0 + G], o3[:, :, :G], reads=[ro],
                  writes=[DR["mix"]])

    def stage_out_mlp(l, groups, xsrc, rxsrc, xdst, rxdst, final_norm):
        S.barrier()
        ar.reset()
        bigf = [(ar.f32(8192), S.res("bigf")) for _ in range(2)]
        bigh = (ar.bf16(8192), S.res("bigh"))
        ut = (ar.bf16(32 * 512), S.res("ut"))
        wb = [(ar.bf16(2048).rearrange("p (k j) -> p k j", j=128), S.res("w")) for _ in range(4)]
        w2b = [(ar.bf16(4096).rearrange("p (k j) -> p k j", j=128), S.res("w2")) for _ in range(3)]
        sqrot = Rot(S, [ar.f32(512) for _ in range(2)])
        tmpf = Rot(S, [ar.f32(512) for _ in range(2)])
        rs = [(ar.f32(512), S.res()) for _ in range(3)]
        sd = ar.f32(512)
        r_sd = S.res()
        rstd = ar.f32(512)
        r_rstd = S.res()
        psi = [0]

        def nextps():
            i = psi[0] % 4
            psi[0] += 1
            return PS[i], RPS[i]

        xv = xsrc.rearrange("k p t -> p k t")
        mv = mix_s.rearrange("k p t -> p k t")
        wl = []
        w2l = []
        for gi in range(len(groups)):
            wl += [wout_d[l, d] for d in range(16)]
            for hf in range(2):
                wl += [w1_d[l, hf * 32 + f] for f in range(32)]
                w2l += [w2_d[l, hf, d] for d in range(16)]
        getw = wstream(wb, wl, 3)
        getw2 = wstream(w2b, w2l, 2)
        wi = 0
        w2i = 0

        def load_mix(gi):
            t0, G, r = groups[gi]
            m2, rm = bigf[0]
            S.dma("sp", m2[:, :16 * G].rearrange("p (k t) -> p k t", t=G), mv[:, :, t0:t0 + G],
                  reads=[DR["mix"]], writes=[rm])

        def load_x(gi):
            t0, G, r = groups[gi]
            x2, rx = bigf[1]
            S.dma("sp", x2[:, :16 * G].rearrange("p (k t) -> p k t", t=G), xv[:, :, t0:t0 + G],
                  reads=[rxsrc], writes=[rx])

        load_mix(0)
        for gi, (t0, G, r) in enumerate(groups):
            load_x(gi)
            m2, rm = bigf[0]
            mix = m2[:, :16 * G].rearrange("p (k t) -> p k t", t=G)
            x2, rx = bigf[1]
            xg = x2[:, :16 * G].rearrange("p (k t) -> p k t", t=G)
            h2, rh = bigh
            hT = h2[:, :16 * G].rearrange("p (k t) -> p k t", t=G)
            segs = [(0, 8, 1024), (8, 12, 512), (12, 16, 512)]
            for si, (k0, k1, n) in enumerate(segs):
                pb, rp = PS[4 + si], RPS[4 + si]
                for k in range(k0, k1):
                    sq, rsq = sqrot.next()
                    ACT(sq[:, :G], mix[:, k, :], AF.Square, [rm], [rsq])
                    MM(pb[:, :G], ones_f[:, :], sq[:, :G], k == k0, k == k1 - 1, [rsq, r_const], [rp])
                rstd_from_ps(pb[:, :G], rp, n, sd[:, :G], r_sd, rs[si][0][:, :G], rs[si][1], eps_ap)
            for si, (k0, k1, n) in enumerate(segs):
                for k in range(k0, k1):
                    STT(hT[:, k, :], mix[:, k, :], small[:, l, O_ON + k:O_ON + k + 1], rs[si][0][:, :G],
                        ALU.mult, ALU.mult, [rm, rs[si][1], r_small], [rh])
            if gi + 1 < len(groups):
                load_mix(gi + 1)
            for d in range(16):
                wt, rw = getw(wi)
                wi += 1
                pb, rp = nextps()
                for k in range(16):
                    MM(pb[:, :G], wt[:, k, :], hT[:, k, :], k == 0, k == 15, [rw, rh], [rp])
                STT(xg[:, d, :], pb[:, :G], dtab[:, 2, d, r:r + 1], xg[:, d, :], ALU.mult, ALU.add,
                    [rp, rx, r_dtab], [rx])
            modulate_group(xg, rx, G, r, 1, hT, rh, sqrot, tmpf, PS[7], RPS[7], rstd, r_rstd, sd, r_sd)
            u2, ru = ut
            u3 = u2[:, :32 * G].rearrange("p (f t) -> p f t", t=G)
            for hf in range(2):
                for f in range(32):
                    wt, rw = getw(wi)
                    wi += 1
                    pb, rp = nextps()
                    for k in range(16):
                        MM(pb[:, :G], wt[:, k, :], hT[:, k, :], k == 0, k == 15, [rw, rh], [rp])
                    sq, rsq = sqrot.next()
                    ACT(sq[:, :G], pb[:, :G], AF.Square, [rp], [rsq])
                    STT(u3[:, f, :], pb[:, :G], 0.0, sq[:, :G], ALU.is_gt, ALU.mult, [rp, rsq], [ru])
                for d in range(16):
                    w2t, rw2 = getw2(w2i)
                    w2i += 1
                    pb, rp = nextps()
                    for f in range(32):
                        MM(pb[:, :G], w2t[:, f, :], u3[:, f, :], f == 0, f == 31, [rw2, ru], [rp])
                    STT(xg[:, d, :], pb[:, :G], dtab[:, 5, d, r:r + 1], xg[:, d, :], ALU.mult, ALU.add,
                        [rp, rx, r_dtab], [rx])
            if final_norm:
                for k in range(16):
                    sq, rsq = sqrot.next()
                    ACT(sq[:, :G], xg[:, k, :], AF.Square, [rx], [rsq])
                    MM(PS[7][:, :G], ones_f[:, :], sq[:, :G], k == 0, k == 15, [rsq, r_const], [RPS[7]])
                rstd_from_ps(PS[7][:, :G], RPS[7], D, sd[:, :G], r_sd, rstd[:, :G], r_rstd, eps_ap)
                for k in range(16):
                    STT(xg[:, k, :], xg[:, k, :], fnorm[:, k:k + 1], rstd[:, :G], ALU.mult, ALU.mult,
                        [rx, r_rstd, r_small], [rx])
            S.dma("sp", xdst.rearrange("k p t -> p k t")[:, :, t0:t0 + G], xg, reads=[rx], writes=[rxdst])

    for l in layers:
        lastl = (l == DEPTH - 1)
        xsrc, rxsrc = (xT_in, DR["xT"]) if l == layers[0] else (xB, DR["xB"])
        xdst, rxdst = (outT, DR["out"]) if (lastl and final) else (xB, DR["xB"])
        qgroups = GROUPS[:4] if lastl else GROUPS
        modes = (["full"] * 4 + ["kvglu", "kv", "kv", "kvglu", "kv"]) if lastl else ["full"] * 9
        dtab = dtab_all[:, l * 6:(l + 1) * 6]
        sinkbc = sinkbc_all[:, l * 8:(l + 1) * 8, :]
        r_dtab = r_dtab_l[l]
        r_sink = r_sink_l[l]
        if l == layers[0]:
            stage_mod(l)
        stage_in(l, xsrc, rxsrc, GROUPS, modes)
        stage_kvup(l)
        stage_qup(l, qgroups)
        stage_atta(l, qgroups, extra_mod=(l + 1) if (l + 1) in layers else None)
        stage_attb(l, qgroups)
        stage_conv(l, qgroups)
        stage_out_mlp(l, qgroups, xsrc, rxsrc, xdst, rxdst, lastl and final)

    blk = es.enter_context(nc.Block())
    S.finalize(sems, blk)
    es.close()
    return nc, S


def _fm(v):
    return np.ascontiguousarray(np.asarray(v, np.float32).reshape(-1, 128).T)


def _tile(W, nc_):
    K, N = W.shape
    return np.ascontiguousarray(W.reshape(K // 128, 128, N // nc_, nc_).transpose(2, 1, 0, 3))


PERM64 = np.concatenate([np.arange(16, 32), np.arange(0, 16), np.arange(48, 64), np.arange(32, 48)])


def _rope_tables():
    rows = TL // 64
    row = np.repeat(np.arange(rows, dtype=np.float32), 64)
    col = np.tile(np.arange(64, dtype=np.float32), rows)
    inv = (np.float32(10000.0) ** (-(np.arange(16, dtype=np.float32) / np.float32(16)))).astype(np.float32)
    C = np.ones((64, T), np.float32)
    Sg = np.zeros((64, T), np.float32)
    for d in range(64):
        blk, i = d // 16, d % 16
        pos = row if blk < 2 else col
        ang = (pos * inv[i]).astype(np.float32).astype(np.float64)
        C[d, :TL] = np.cos(ang)
        Sg[d, :TL] = np.sin(ang) * (-1.0 if blk % 2 == 0 else 1.0)
    tab = np.zeros((2, 128, T), np.float32)
    tab[0, :64] = C
    tab[0, 64:] = C
    tab[1, :64] = Sg
    tab[1, 64:] = Sg
    return tab


def _prep_shared(inp):
    f = lambda k: np.asarray(inp[k], np.float32)
    L = DEPTH
    sh = {}
    small = np.zeros((L, 128, NS), np.float32)
    for l in range(L):
        small[l, :, O_NMIX:O_NMIX + 16] = _fm(f("norm_mix")[l])
        small[l, :, O_NMLP:O_NMLP + 16] = _fm(f("norm_mlp")[l])
        small[l, :, O_QN:O_QN + 4] = _fm(f("mla_q_norm")[l])
        small[l, :, O_KVN:O_KVN + 4] = _fm(f("mla_kv_norm")[l])
        small[l, :, O_CB:O_CB + 4] = _fm(f("conv_b")[l])
        small[l, :, O_LNG:O_LNG + 4] = _fm(f("conv_ln_g")[l])
        small[l, :, O_LNB:O_LNB + 4] = _fm(f("conv_ln_b")[l])
        small[l, :, O_ON:O_ON + 16] = _fm(f("out_norm")[l])
        small[l, :, O_ADAB:O_ADAB + 96] = _fm(f("ada_b")[l])
        cw = f("conv_w")[l][:, 0, :]
        small[l, :, O_CW:O_CW + 124] = cw.T.reshape(4, 128, 31).transpose(1, 0, 2).reshape(128, 124)
        small[l, :, O_SINK:O_SINK + 8] = np.broadcast_to(f("swa_sink")[l][None, :], (128, 8))
    sh["small"] = small
    sh["fnorm"] = _fm(f("final_norm"))
    sh["_rope"] = _rope_tables()
    cbf = np.zeros((128, 1152), np.float32)
    cbf[:, 0:128] = np.eye(128, dtype=np.float32)
    j = np.arange(128)[:, None]
    qi = np.arange(128)[None, :]
    mp = np.where(j >= qi, 0.0, NEG).astype(np.float32)
    mn = np.where(j <= qi, 0.0, NEG).astype(np.float32)
    cbf[:, 128:640] = np.tile(mp, (1, 4))
    cbf[:, 640:1152] = np.tile(mn, (1, 4))
    sh["_cbf"] = cbf
    sh["ada_t"] = np.stack([_tile(f("ada_w")[l], 512) for l in range(L)])
    wins = []
    for l in range(L):
        W = f("w_in")[l]
        groups = []
        pad = lambda a: np.concatenate([a, np.zeros((2048, 128 - a.shape[1]), np.float32)], 1) if a.shape[1] < 128 else a
        for j_ in range(4):
            groups.append(W[:, j_ * 128:(j_ + 1) * 128])
        for j_ in range(4):
            groups.append(W[:, 512 + j_ * 128:512 + (j_ + 1) * 128])
        kr = W[:, 1024:1088]
        groups.append(pad(kr))
        groups.append(pad(kr[:, PERM64]))
        p128 = np.concatenate([PERM64, 64 + PERM64])
        for j_ in range(4):
            bq = W[:, 1088 + j_ * 128:1088 + (j_ + 1) * 128]
            groups.append(bq)
            groups.append(bq[:, p128])
        bk = W[:, 1600:1728]
        groups.append(bk)
        groups.append(bk[:, p128])
        groups.append(W[:, 1728:1856])
        for j_ in range(4):
            groups.append(W[:, 2368 + j_ * 128:2368 + (j_ + 1) * 128])
            groups.append(W[:, 1856 + j_ * 128:1856 + (j_ + 1) * 128])
        wins.append(np.stack([_tile(np.ascontiguousarray(g), 128)[0] for g in groups]))
    sh["win_t"] = np.stack(wins)
    t4 = lambda A: np.ascontiguousarray(A.reshape(4, 128, A.shape[1]).transpose(1, 0, 2))
    ukv = f("mla_w_ukv")
    uq = f("mla_w_uq")
    sh["wk_t"] = np.stack([np.stack([t4(ukv[l][:, h, 0:128]) for h in range(8)]) for l in range(L)])
    sh["wv_t"] = np.stack([np.stack([t4(ukv[l][:, 4 * a:4 * a + 4, 128:256].reshape(512, 512))
                                     for a in range(2)]) for l in range(L)])
    sh["wqn_t"] = np.stack([np.stack([t4(uq[l][:, h, 0:128]) for h in range(8)]) for l in range(L)])
    sh["wqp_t"] = np.stack([np.stack([t4(uq[l][:, h, 128:192]) for h in range(8)]) for l in range(L)])
    sh["wqr_t"] = np.stack([np.stack([t4(uq[l][:, h, 128:192][:, PERM64]) for h in range(8)]) for l in range(L)])
    sh["wout_t"] = np.stack([_tile(f("w_out")[l], 128) for l in range(L)])
    sh["w1_t"] = np.stack([_tile(f("mlp_w1")[l], 128) for l in range(L)])
    sh["w2_t"] = np.stack([np.ascontiguousarray(
        f("mlp_w2")[l].reshape(2, 32, 128, 16, 128).transpose(0, 3, 2, 1, 4)) for l in range(L)])
    return sh


def _prep_core(inp, sh, b, half):
    perm = np.concatenate([np.arange(half * TH, (half + 1) * TH), np.arange((1 - half) * TH, (2 - half) * TH)])
    x = np.asarray(inp["x"], np.float32)[b][perm]
    ctx = np.asarray(inp["ctx"], np.float32)[b]
    xt = np.concatenate([x, ctx], 0).T
    d = {"xT": np.ascontiguousarray(xt.reshape(16, 128, T))}
    cT = np.zeros((128, 17, 2), np.float32)
    cT[:, :16, 0] = _fm(np.asarray(inp["c"], np.float32)[b])
    cT[:, :16, 1] = _fm(np.asarray(inp["c_ctx"], np.float32))
    cT[:, 16, 0] = 1.0 if half == 1 else 0.0
    cT[:, 16, 1] = 1.0 if half == 0 else 0.0
    d["cT"] = cT.reshape(128, 34)
    rt = sh["_rope"]
    d["rope"] = np.ascontiguousarray(np.concatenate([rt[:, :, :TL][:, :, perm], rt[:, :, TL:]], 2))
    base = sh["_cbf"]
    triP = base[:, 128:640]
    triN = base[:, 640:1152]
    allm = np.full((128, 512), NEG, np.float32)
    ms = [allm, triN, triP, allm] if half == 0 else [triP, allm, allm, triN]
    d["cbf"] = np.concatenate([base] + ms, 1).astype(ml_dtypes.bfloat16)
    return d


_CACHE = {}


def kernel(**inputs):
    if "nc" not in _CACHE:
        _CACHE["nc"] = build_program()[0]
    nc = _CACHE["nc"]
    sh = _prep_shared(inputs)
    shared = {k: v for k, v in sh.items() if not k.startswith("_")}
    in_maps = []
    for cid in range(NCORES):
        m = dict(shared)
        m.update(_prep_core(inputs, sh, cid // 2, cid % 2))
        in_maps.append(m)
    res = run_bass_kernel_spmd(nc, in_maps, core_ids=list(range(NCORES)))
    out = np.zeros((4, TL, D), np.float32)
    for cid in range(NCORES):
        b, half = cid // 2, cid % 2
        o = np.asarray(res.results[cid]["outT"], np.float32).reshape(2048, TH)
        out[b, half * TH:(half + 1) * TH, :] = o.T
    return out
```
